# Optimizing a Trainium2 kernel written in Bass

```python
import jax, jax.numpy as jnp
from jax import lax
import numpy as np

D_MODEL = 1024
BATCH = 8
SEQ = 2048
DEPTH = 2
DEC_BATCH = 32
DEC_SEQ = 32
PAST_LEN = 2048

CHUNK = 64
GLA_HEADS = 4
GLA_DK = 64
GLA_DV = 128
GLA_QK = GLA_HEADS * GLA_DK
GLA_WIDTH = GLA_HEADS * GLA_DV
GLA_RANK = 16
GLA_TAU = 16.0
RG_WIDTH = D_MODEL - GLA_WIDTH
RG_BLOCKS = 8
RG_BLOCK = RG_WIDTH // RG_BLOCKS
RG_C = 8.0
CONV_W = 4
D_FF = 4 * D_MODEL
EPS = 1e-6
OFF_Q = 0
OFF_K = OFF_Q + GLA_QK
OFF_V = OFF_K + GLA_QK
OFF_G = OFF_V + GLA_WIDTH
OFF_LR = OFF_G + GLA_WIDTH
OFF_XR = OFF_LR + GLA_RANK
OFF_GR = OFF_XR + RG_WIDTH
D_IN = OFF_GR + RG_WIDTH

kernel_name = 'hymba_gla_rglru_streaming_step'


def rmsnorm(x, g):
    xf = x.astype(jnp.float32)
    y = xf * lax.rsqrt(jnp.mean(xf * xf, axis=-1, keepdims=True) + EPS) * g.astype(jnp.float32)
    return y.astype(x.dtype)


def gla_mix(q, k, v, log_a, S0):
    B, T = q.shape[0], q.shape[1]
    C = min(CHUNK, T)
    n = T // C

    def to_chunks(t):
        return t.reshape(B, n, C, GLA_HEADS, t.shape[-1]).transpose(1, 0, 3, 2, 4)

    causal = jnp.tril(jnp.ones((C, C), dtype=bool))[None, None, :, :, None]

    def step(S, inp):
        qc, kc, vc, lc = inp
        b = jnp.cumsum(lc, axis=2)
        o_inter = jnp.einsum('bhcd,bhde->bhce', qc * jnp.exp(b), S)
        diff = jnp.where(causal, b[:, :, :, None, :] - b[:, :, None, :, :], -jnp.inf)
        att = jnp.einsum('bhid,bhjd,bhijd->bhij', qc, kc, jnp.exp(diff))
        o = o_inter + jnp.einsum('bhij,bhje->bhie', att, vc)
        b_last = b[:, :, -1:, :]
        S_new = jnp.exp(b_last[:, :, 0, :])[..., None] * S + jnp.einsum(
            'bhcd,bhce->bhde', kc * jnp.exp(b_last - b), vc)
        return S_new, o

    S, o = lax.scan(step, S0, (to_chunks(q), to_chunks(k), to_chunks(v), to_chunks(log_a)))
    o = o.transpose(1, 0, 3, 2, 4).reshape(B, T, GLA_HEADS, GLA_DV)
    return o, S


def causal_conv(x, buf, w, b):
    T = x.shape[1]
    xp = jnp.concatenate([buf, x], axis=1)
    y = b + xp[:, 0:T] * w[0]
    for j in range(1, CONV_W):
        y = y + xp[:, j:j + T] * w[j]
    return y, xp[:, -(CONV_W - 1):]


def rg_lru(x, h0, wa, ba, wx, bx, lam):
    B, T = x.shape[0], x.shape[1]
    xb = x.reshape(B, T, RG_BLOCKS, RG_BLOCK)
    r = jax.nn.sigmoid(jnp.einsum('btgi,gij->btgj', xb, wa).reshape(B, T, RG_WIDTH) + ba)
    i = jax.nn.sigmoid(jnp.einsum('btgi,gij->btgj', xb, wx).reshape(B, T, RG_WIDTH) + bx)
    log_a = -RG_C * r * jax.nn.softplus(-lam)
    a = jnp.exp(log_a)
    u = jnp.sqrt(-jnp.expm1(2.0 * log_a)) * (i * x)
    u = u.at[:, 0].add(a[:, 0] * h0)

    def combine(left, right):
        a1, b1 = left
        a2, b2 = right
        return a1 * a2, a2 * b1 + b2

    _, h = lax.associative_scan(combine, (a, u), axis=1)
    return h, h[:, -1]


def layer(x, S0, h0, buf, g_pre_mix, w_in, w_lr2, b_lr, gla_norm, conv_w, conv_b,
          rg_wa, rg_ba, rg_wx, rg_bx, rg_lambda, w_out, g_post_mix, g_pre_ff, w_ff1, w_ff2,
          g_post_ff):
    B, T = x.shape[0], x.shape[1]
    f32 = jnp.float32
    z = rmsnorm(x, g_pre_mix) @ w_in
    q = z[..., OFF_Q:OFF_K].astype(f32).reshape(B, T, GLA_HEADS, GLA_DK) * (GLA_DK ** -0.5)
    k = z[..., OFF_K:OFF_V].astype(f32).reshape(B, T, GLA_HEADS, GLA_DK)
    v = z[..., OFF_V:OFF_G].astype(f32).reshape(B, T, GLA_HEADS, GLA_DV)
    g = z[..., OFF_G:OFF_LR].astype(f32)
    lr = z[..., OFF_LR:OFF_XR].astype(f32)
    xr = z[..., OFF_XR:OFF_GR].astype(f32)
    gr = z[..., OFF_GR:D_IN].astype(f32)

    log_a = (jax.nn.log_sigmoid(lr @ w_lr2.astype(f32) + b_lr.astype(f32)) / GLA_TAU).reshape(
        B, T, GLA_HEADS, GLA_DK)
    o, S = gla_mix(q, k, v, log_a, S0.astype(f32))
    o = o * lax.rsqrt(jnp.mean(o * o, axis=-1, keepdims=True) + EPS) * gla_norm.astype(f32)
    o = o.reshape(B, T, GLA_WIDTH) * jax.nn.silu(g)

    xc, buf_new = causal_conv(xr, buf.astype(f32), conv_w.astype(f32), conv_b.astype(f32))
    hr, h_last = rg_lru(xc, h0.astype(f32), rg_wa.astype(f32), rg_ba.astype(f32),
                        rg_wx.astype(f32), rg_bx.astype(f32), rg_lambda.astype(f32))
    yr = hr * jax.nn.gelu(gr)

    mix = jnp.concatenate([o, yr], axis=-1).astype(x.dtype) @ w_out
    x = x + rmsnorm(mix, g_post_mix)
    f = jnp.square(jax.nn.relu(rmsnorm(x, g_pre_ff) @ w_ff1)) @ w_ff2
    x = x + rmsnorm(f, g_post_ff)
    return x, S, h_last, buf_new


def setup_inputs(seed: int = 0) -> dict:
    key = jax.random.key(seed)
    ks = jax.random.split(key, 32)
    nrm = lambda i, shape, s: jax.random.normal(ks[i], shape, jnp.float32) * s
    a0 = jax.random.uniform(ks[20], (DEPTH, RG_WIDTH), jnp.float32, 0.9, 0.999)
    s0 = a0 ** (1.0 / RG_C)
    rg_lambda = jnp.log(s0) - jnp.log1p(-s0)
    return {
        'x_prompt': nrm(0, (BATCH, SEQ, D_MODEL), 1.0),
        'x_sample': nrm(1, (DEC_BATCH, DEC_SEQ, D_MODEL), 1.0),
        'state_gla': nrm(2, (DEPTH, DEC_BATCH, GLA_HEADS, GLA_DK, GLA_DV), 0.5),
        'state_rglru': nrm(3, (DEPTH, DEC_BATCH, RG_WIDTH), 0.5),
        'state_conv': nrm(4, (DEPTH, DEC_BATCH, CONV_W - 1, RG_WIDTH), 1.0),
        'g_pre_mix': 1.0 + nrm(5, (DEPTH, D_MODEL), 0.02),
        'w_in': nrm(6, (DEPTH, D_MODEL, D_IN), D_MODEL ** -0.5),
        'w_lr2': nrm(7, (DEPTH, GLA_RANK, GLA_QK), GLA_RANK ** -0.5),
        'b_lr': nrm(8, (DEPTH, GLA_QK), 0.1),
        'gla_norm': 1.0 + nrm(9, (DEPTH, GLA_DV), 0.02),
        'conv_w': nrm(10, (DEPTH, CONV_W, RG_WIDTH), CONV_W ** -0.5),
        'conv_b': nrm(11, (DEPTH, RG_WIDTH), 0.02),
        'rg_wa': nrm(12, (DEPTH, RG_BLOCKS, RG_BLOCK, RG_BLOCK), RG_BLOCK ** -0.5),
        'rg_ba': nrm(13, (DEPTH, RG_WIDTH), 0.1),
        'rg_wx': nrm(14, (DEPTH, RG_BLOCKS, RG_BLOCK, RG_BLOCK), RG_BLOCK ** -0.5),
        'rg_bx': nrm(15, (DEPTH, RG_WIDTH), 0.1),
        'rg_lambda': rg_lambda,
        'w_out': nrm(16, (DEPTH, D_MODEL, D_MODEL), D_MODEL ** -0.5),
        'g_post_mix': 1.0 + nrm(17, (DEPTH, D_MODEL), 0.02),
        'g_pre_ff': 1.0 + nrm(18, (DEPTH, D_MODEL), 0.02),
        'w_ff1': nrm(19, (DEPTH, D_MODEL, D_FF), D_MODEL ** -0.5),
        'w_ff2': nrm(21, (DEPTH, D_FF, D_MODEL), D_FF ** -0.5),
        'g_post_ff': 1.0 + nrm(22, (DEPTH, D_MODEL), 0.02),
    }


def reference(x_prompt, x_sample, state_gla, state_rglru, state_conv, g_pre_mix, w_in, w_lr2,
              b_lr, gla_norm, conv_w, conv_b, rg_wa, rg_ba, rg_wx, rg_bx, rg_lambda, w_out,
              g_post_mix, g_pre_ff, w_ff1, w_ff2, g_post_ff):
    f32 = jnp.float32
    xp = x_prompt
    xs = x_sample
    gla_p, rg_p, cv_p, gla_s, rg_s, cv_s = [], [], [], [], [], []
    for l in range(DEPTH):
        params = (g_pre_mix[l], w_in[l], w_lr2[l], b_lr[l], gla_norm[l], conv_w[l], conv_b[l],
                  rg_wa[l], rg_ba[l], rg_wx[l], rg_bx[l], rg_lambda[l], w_out[l], g_post_mix[l],
                  g_pre_ff[l], w_ff1[l], w_ff2[l], g_post_ff[l])
        xp, S, h, buf = layer(xp,
                              jnp.zeros((BATCH, GLA_HEADS, GLA_DK, GLA_DV), f32),
                              jnp.zeros((BATCH, RG_WIDTH), f32),
                              jnp.zeros((BATCH, CONV_W - 1, RG_WIDTH), f32),
                              *params)
        gla_p.append(S)
        rg_p.append(h)
        cv_p.append(buf)
        xs, S2, h2, buf2 = layer(xs, state_gla[l], state_rglru[l], state_conv[l], *params)
        gla_s.append(S2)
        rg_s.append(h2)
        cv_s.append(buf2)
    gla_prompt = jnp.stack(gla_p)
    rglru_prompt = jnp.stack(rg_p)
    conv_prompt = jnp.stack(cv_p)
    gla_sample = jnp.stack(gla_s)
    rglru_sample = jnp.stack(rg_s)
    conv_sample = jnp.stack(cv_s)
    return (xp, xs, gla_prompt, rglru_prompt, conv_prompt, gla_sample, rglru_sample, conv_sample)
```

```python
import contextlib
import math
import numpy as np
import concourse.bass as bass
import concourse.mybir as mybir
from concourse.bass_utils import run_bass_kernel_spmd

F32 = mybir.dt.float32
BF16 = mybir.dt.bfloat16
AF = mybir.ActivationFunctionType
ALU = mybir.AluOpType

ENGS = ("pe", "act", "dve", "pool", "sp")
SEM_L = 4000


class _Op:
    __slots__ = ("eng", "emit", "deps", "idx", "sig", "signo", "dma_key", "dma_seq",
                 "clock", "waits", "is_dma", "name", "gseq", "dur", "nbytes", "aset", "cls", "prio")


class _Rec:
    def __init__(self):
        self.call = None

    def __getattr__(self, name):
        def f(*a, **k):
            self.call = (name, a, k)
            return None
        return f


class Prog:
    def __init__(self, nc):
        self.nc = nc
        self.streams = {e: [] for e in ENGS}
        self.ops = []
        self.last_write = {}
        self.readers = {}
        self.dma_count = {}
        self.dma_last = {}
        self.dma_ops = {}
        self.stack = contextlib.ExitStack()

    def sbuf(self, name, shape, dtype):
        return self.stack.enter_context(self.nc.sbuf_tensor("sb_" + name, list(shape), dtype))

    def psum(self, name, shape, dtype):
        return self.stack.enter_context(self.nc.psum_tensor("ps_" + name, list(shape), dtype))

    def _mk(self, eng, emit, reads, writes, name=None):
        o = _Op()
        o.eng = eng
        if emit is not None:
            rec = _Rec()
            emit(rec)
            mname, a, k = rec.call
            emit = (lambda engh, mname=mname, a=a, k=k: getattr(engh, mname)(*a, **k))
            o.dur, o.nbytes = self._est(eng, mname, a, k)
            o.aset = None
            if mname == "activation":
                fn = k.get("func")
                if fn == AF.Tanh:
                    o.aset = "E"
                elif fn == AF.Ln:
                    o.aset = "L"
                elif fn == AF.Exp:
                    o.aset = "X"
        else:
            o.dur, o.nbytes = 0.0, 0
            o.aset = None
        o.emit = emit
        o.name = name
        o.is_dma = False
        o.dma_key = None
        o.dma_seq = 0
        o.sig = False
        o.signo = 0
        o.prio = None
        deps = []
        for r in reads:
            w = self.last_write.get(r)
            if w is not None:
                deps.append(w)
        for w_ in writes:
            w = self.last_write.get(w_)
            if w is not None:
                deps.append(w)
            deps.extend(self.readers.get(w_, ()))
        for r in reads:
            self.readers.setdefault(r, []).append(o)
        for w_ in writes:
            self.last_write[w_] = o
            self.readers[w_] = []
        o.deps = deps
        o.idx = len(self.streams[eng])
        o.gseq = len(self.ops)
        self.streams[eng].append(o)
        self.ops.append(o)
        return o

    @staticmethod
    def _est(eng, mname, a, k):
        out = k.get("out", a[0] if a else None)
        n = 1
        try:
            for d in out.shape[1:]:
                n *= d
        except Exception:
            n = 512
        if mname == "dma_start":
            return 0.0, n * 4 * 128
        if eng == "pe":
            lh = k.get("lhsT", k.get("in_"))
            f = 4.0 if (mname == "matmul" and lh is not None and lh.dtype == F32) else 1.0
            return max(max(64.0, n) / 2.4 * f + 10.0, 105.0), 0
        wide = max(0, n - 512)
        if eng == "act":
            return 1.2 * (120.0 + 0.8 * min(n, 512) + 1.5 * wide), 0
        if eng == "dve":
            f = 2.0 if mname in ("tensor_tensor_scan",) else 1.0
            return 1.24 * (110.0 + f * (0.85 * min(n, 512) + 1.5 * wide)), 0
        if eng == "pool":
            return 1.23 * (150.0 + 1.6 * n), 0
        return 60.0, 0

    def schedule(self):
        ops = self.ops
        n = len(ops)
        succ = [[] for _ in range(n)]
        indeg = [0] * n
        for o in ops:
            ds = set(p.gseq for p in o.deps if p is not o)
            indeg[o.gseq] = len(ds)
            for d in ds:
                succ[d].append(o.gseq)
        for o in ops:
            o.cls = o.prio if o.prio is not None else (0 if any((p.eng == "pe" and o.eng != "pe") for p in o.deps) else 1)
        cur_set = [None]
        ready_time = [0.0] * n
        eng_free = {e: 0.0 for e in ENGS}
        ready = {e: [] for e in ENGS}
        for o in ops:
            if indeg[o.gseq] == 0:
                ready[o.eng].append(o.gseq)
        order = []
        SYNC = 150.0
        while len(order) < n:
            best = None
            for e in ENGS:
                rl = ready[e]
                if not rl:
                    continue
                tfree = eng_free[e]
                cg = None
                ckey = None
                for g_ in rl:
                    rt = ready_time[g_]
                    og = ops[g_]
                    sw = 0
                    if e == "act" and og.aset in ("E", "L") and cur_set[0] is not None and og.aset != cur_set[0]:
                        sw = 1
                    key = (0.0, sw, og.cls, g_) if rt <= tfree else (rt - tfree, sw, og.cls, g_)
                    if ckey is None or key < ckey:
                        ckey, cg = key, g_
                start = max(tfree, ready_time[cg])
                if best is None or (start, cg) < (best[0], best[2]):
                    best = (start, e, cg)
            start, e, g_ = best
            o = ops[g_]
            ready[e].remove(g_)
            if o.is_dma:
                busy = 1400.0 if e == "pool" else 120.0
                fin = start + busy + 2200.0 + o.nbytes / 180.0
            else:
                busy = o.dur
                if e == "act" and o.aset in ("E", "L"):
                    if cur_set[0] is not None and cur_set[0] != o.aset:
                        busy += 1283.0
                    cur_set[0] = o.aset
                fin = start + busy
            eng_free[e] = start + busy
            order.append(o)
            for s_ in succ[g_]:
                rt = fin if (e == "pe" and ops[s_].eng == "pe") else fin + SYNC
                if ready_time[s_] < rt:
                    ready_time[s_] = rt
                indeg[s_] -= 1
                if indeg[s_] == 0:
                    ready[ops[s_].eng].append(s_)
        self.ops = order
        self.streams = {e: [] for e in ENGS}
        for i, o in enumerate(order):
            o.gseq = i
            o.idx = len(self.streams[o.eng])
            self.streams[o.eng].append(o)
        self.est_span = max(eng_free.values())

    def op(self, eng, emit, reads=(), writes=(), name=None):
        return self._mk(eng, emit, tuple(reads), tuple(writes), name)

    def dma(self, eng, emit, key, reads=(), writes=(), name=None):
        o = self._mk(eng, emit, tuple(reads), tuple(writes), name)
        o.is_dma = True
        o.dma_key = key
        prev = self.dma_last.get(key)
        if prev is not None:
            o.deps.append(prev)
        self.dma_count[key] = self.dma_count.get(key, 0) + 1
        o.dma_seq = self.dma_count[key]
        self.dma_last[key] = o
        self.dma_ops[(key, o.dma_seq)] = o
        return o

    def finish(self):
        o = self._mk("sp", None, (), (), "finish")
        o.deps = list(self.dma_last.values())
        return o

    def emit_all(self, sched=True):
        nc = self.nc
        if sched:
            self.schedule()
        last_clock = {e: {} for e in ENGS}
        for o in self.ops:
            clock = dict(last_clock[o.eng])
            waits = {}
            for p in o.deps:
                if p is o:
                    continue
                if p.is_dma:
                    chan = ("dma", p.dma_key)
                    need = p.dma_seq
                else:
                    if p.eng == "pe" and o.eng == "pe" and not o.is_dma:
                        continue
                    chan = p.eng
                    need = p.idx
                if clock.get(chan, -1) >= need:
                    continue
                if waits.get(chan, -1) < need:
                    waits[chan] = need
            for chan, need in waits.items():
                if isinstance(chan, tuple):
                    p = self.dma_ops[(chan[1], need)]
                else:
                    p = self.streams[chan][need]
                    p.sig = True
                if clock.get(chan, -1) < need:
                    clock[chan] = need
                for c, v in p.clock.items():
                    if clock.get(c, -1) < v:
                        clock[c] = v
            o.clock = clock
            o.waits = waits
            last_clock[o.eng] = clock

        eng_sems = {}
        for e in ENGS:
            n = 0
            for o in self.streams[e]:
                if o.sig:
                    n += 1
                    o.signo = n
            nsem = (n + SEM_L - 1) // SEM_L
            eng_sems[e] = [self.stack.enter_context(nc.semaphore(f"s_{e}_{i}")) for i in range(nsem)]
        dma_sems = {}
        for i, k in enumerate(self.dma_count):
            dma_sems[k] = self.stack.enter_context(nc.semaphore(f"d_{i}"))
        self.n_sems = sum(len(v) for v in eng_sems.values()) + len(dma_sems)
        streams = self.streams

        def run(engh, e):
            for o in streams[e]:
                for chan, need in o.waits.items():
                    if isinstance(chan, tuple):
                        engh.wait_ge(dma_sems[chan[1]], 16 * need)
                    else:
                        p = streams[chan][need]
                        s = p.signo - 1
                        engh.wait_ge(eng_sems[chan][s // SEM_L], s % SEM_L + 1)
                if o.emit is None:
                    continue
                ins = o.emit(engh)
                if o.is_dma:
                    ins.then_inc(dma_sems[o.dma_key], 16)
                elif o.sig:
                    s = o.signo - 1
                    ins.then_inc(eng_sems[e][s // SEM_L], 1)

        with nc.allow_non_contiguous_dma(reason="small strided state/param transfers"), nc.Block() as block:
            @block.tensor
            def _(eng):
                run(eng, "pe")

            @block.scalar
            def _(eng):
                run(eng, "act")

            @block.vector
            def _(eng):
                run(eng, "dve")

            @block.gpsimd
            def _(eng):
                run(eng, "pool")

            @block.sync
            def _(eng):
                run(eng, "sp")
        self.stack.close()


D = 1024
DIN = 2576
DFF = 4096
NL = 2
SEQ = 2048
NTOK = 2176
GROUPS = [(0, 768, 0), (768, 768, 0), (1536, 512, 128)]
GMAX = 768
EPS = 1e-6
OFF_Q, OFF_K, OFF_V, OFF_G, OFF_LR, OFF_XR, OFF_GR = 0, 256, 512, 1024, 1536, 1552, 2064
PC = 65
C_GPRE, C_GPOSTM, C_GPREF, C_GPOSTF, C_GN, C_CW, C_CB, C_BA, C_BX, C_LAM = 0, 8, 16, 24, 32, 33, 49, 53, 57, 61
K_ID, K_TRI2, K_TRI4, K_M2, K_M4, K_SEQ = 0, 128, 256, 384, 512, 640
CW = 644
NSCR = 6
NSCB = 4
WSLOTS = 3


def build_program(layers=NL, ngroups=len(GROUPS), stop=10 ** 9):
    nc = bass.Bass("TRN2", target_bir_lowering=False)
    P = Prog(nc)

    def din(name, shape):
        return nc.dram_tensor(name, list(shape), F32, kind="ExternalInput").ap()

    def dout(name, shape):
        return nc.dram_tensor(name, list(shape), F32, kind="ExternalOutput").ap()

    x_d = din("x", [NTOK, D])
    sgla_d = din("sgla", [NL, 4, 4, 64, 128])
    srg_d = din("srg", [NL, 4, 512])
    scv_d = din("scv", [NL, 4, 3, 512])
    w_in_d = din("w_in", [NL, D, DIN])
    w_out_d = din("w_out", [NL, D, D])
    w_ff1_d = din("w_ff1", [NL, D, DFF])
    w_ff2_d = din("w_ff2", [NL, DFF, D])
    pvec_d = din("pvec", [128, NL, PC])
    wlr_d = din("wlr", [17, NL, 256])
    wbd_d = din("wbd", [128, NL * 2 * 4, 128])
    consts_d = din("consts", [128, CW])

    y_d = dout("y", [NTOK, D])
    glap_d = dout("glap", [NL, 4, 64, 128])
    rgp_d = dout("rgp", [NL, 512])
    cvp_d = dout("cvp", [NL, 3, 512])
    glas_d = dout("glas", [NL, 4, 4, 64, 128])
    rgs_d = dout("rgs", [NL, 4, 512])
    cvs_d = dout("cvs", [NL, 4, 3, 512])

    xT = P.sbuf("xT", [128, 8, GMAX], F32)
    hT = P.sbuf("hT", [128, 8, GMAX], BF16)
    om = P.sbuf("om", [128, 8, GMAX], BF16)
    resbuf = P.sbuf("resbuf", [128, 8, GMAX], F32)
    wring = [P.sbuf(f"wring{i}", [128, 4096], BF16) for i in range(WSLOTS)]
    arena = P.sbuf("arena", [128, 24576], BF16)
    hid = arena[:, :].rearrange("p (k g) -> p k g", k=32)
    _ao = [0]

    def carve(nbf16):
        a = _ao[0]
        _ao[0] += nbf16
        assert _ao[0] <= 24576
        return arena[:, a:a + nbf16]

    eb = carve(2 * GMAX * 2).bitcast(F32).rearrange("p (a g) -> p a g", a=2)
    enb = carve(2 * GMAX * 2).bitcast(F32).rearrange("p (a g) -> p a g", a=2)
    qk = carve(4 * GMAX).rearrange("p (a g) -> p a g", a=4)
    v_tm = carve(6 * 512).rearrange("p (t e) -> p t e", t=6)
    sg = carve(4 * GMAX).rearrange("p (a g) -> p a g", a=4)
    ggr = carve(4 * GMAX).rearrange("p (a g) -> p a g", a=4)
    k_tm = carve(6 * 256).rearrange("p (t e) -> p t e", t=6)
    k_tm_s = carve(4 * 256).rearrange("p (t e) -> p t e", t=4)
    NSB = 7
    Sz = carve(NSB * 512).rearrange("p (t h e) -> p t h e", t=NSB, h=4)

    att = P.sbuf("att", [128, 3, 512], BF16)
    xr = P.sbuf("xr", [128, 4, GMAX + 3], F32)
    xr_s = P.sbuf("xr_s", [128, 4, 4, 35], F32)
    NDS = 3
    dSp = P.sbuf("dSp", [128, NDS, 2, 128], F32)
    scr = [P.sbuf(f"scr{i}", [128, 512], F32) for i in range(NSCR)]
    scb = [P.sbuf(f"scb{i}", [128, 512], BF16) for i in range(NSCB)]
    xcbw = [P.sbuf(f"xcbw{i}", [128, GMAX], BF16) for i in range(2)]
    lrT = P.sbuf("lrT", [32, GMAX], F32)
    consts = P.sbuf("consts", [128, CW], F32)
    ident_bf = P.sbuf("ident_bf", [128, 128], BF16)
    ones_bf = P.sbuf("ones_bf", [128, 128], BF16)
    pvec = P.sbuf("pvec", [128, NL, PC], F32)
    dv = P.sbuf("dv", [128, NL, 24], F32)
    kcol = P.sbuf("kcol", [128, 4], F32)
    wlr = P.sbuf("wlr", [17, NL, 256], F32)
    wbd = P.sbuf("wbd", [128, NL * 2 * 4, 128], BF16)
    eb_last = P.sbuf("eb_last", [128, 2, 16], F32)
    Sf = P.sbuf("Sf", [128, 2, 2, 128], F32)
    S_state = P.sbuf("S_state", [128, NL, 2, 128], F32)
    S0 = P.sbuf("S0", [128, 4, 2, 128], F32)
    hst = P.sbuf("hst", [128, NL, 4], F32)
    h0s = P.sbuf("h0s", [128, NL, 4, 4], F32)
    hso = P.sbuf("hso", [128, 4, 4], F32)
    convst = P.sbuf("convst", [128, NL, 4, 3], F32)

    mmb = [P.psum(f"mmb{i}", [128, 512], F32) for i in range(3)]
    ssb2 = P.psum("ssb2", [128, 1024], F32)
    ssb = [ssb2[:, 0:512], ssb2[:, 512:1024]]
    smb = [P.psum(f"smb{i}", [128, 512], F32) for i in range(3)]
    cnt = {"mm": 0, "sm": 0, "scr": 0, "scb": 0, "w": 0}

    def MM():
        i = cnt["mm"] % 3
        cnt["mm"] += 1
        return mmb[i], ("mmb", i)

    def SM():
        i = cnt["sm"] % 3
        cnt["sm"] += 1
        return smb[i], ("smb", i)

    def SC():
        i = cnt["scr"] % NSCR
        cnt["scr"] += 1
        return scr[i], ("scr", i)

    def SB():
        i = cnt["scb"] % NSCB
        cnt["scb"] += 1
        return scb[i], ("scb", i)

    ARENA = "arena"
    nph = [0]
    bgref = [None]
    bgn = [4]
    pending = []
    pre_ss = {}

    def barrier():
        P.op("sp", lambda e: e.nop(), reads=(), writes=[ARENA])

    ident = consts[:, K_ID:K_ID + 128]
    tri2 = consts[:, K_TRI2:K_TRI2 + 128]
    tri4 = consts[:, K_TRI4:K_TRI4 + 128]
    mask2 = consts[:, K_M2:K_M2 + 128]
    mask4 = consts[:, K_M4:K_M4 + 128]
    seqmask = consts[:, K_SEQ:K_SEQ + 4]
    epsc = kcol[:, 0:1]
    onec = kcol[:, 1:2]
    lnq = kcol[:, 2:3]

    X0KEYS = [("om", 0, t_) for t_ in range(6)] + [("om", 4 + q_, ("b", b_)) for q_ in range(4) for b_ in range(2)]

    P.dma("sp", lambda e: e.dma_start(out=consts[:, :], in_=consts_d), "c0", writes=["consts"])
    P.dma("sp", lambda e: e.dma_start(out=pvec[:, :, :], in_=pvec_d), "c1", writes=["pvec"])
    P.dma("sp", lambda e: e.dma_start(out=wlr[:, :, :], in_=wlr_d), "c2", writes=["wlr"])
    wbd_dma = P.dma("pool", lambda e: e.dma_start(out=wbd[:, :, :], in_=wbd_d), "c3", writes=["wbd"])
    with nc.allow_non_contiguous_dma(reason="tiny state loads"):
        for l_ in range(NL):
            for s_ in range(4):
                P.dma("sp", lambda e, l_=l_, s_=s_: e.dma_start(out=h0s[:, l_, s_, :],
                                                                in_=srg_d[l_, s_].rearrange("(m p) -> p m", p=128)),
                      ("c4", s_), writes=["h0s"])
    P.op("dve", lambda e: e.memset(kcol[:, 0:1], EPS), writes=["kcol"])
    P.op("dve", lambda e: e.memset(kcol[:, 1:2], 1.0), writes=["kcol"])
    P.op("dve", lambda e: e.memset(kcol[:, 2:3], math.log(0.125)), writes=["kcol"])
    P.op("dve", lambda e: e.memset(kcol[:, 3:4], 0.0), writes=["kcol"])
    P.op("dve", lambda e: e.memset(lrT[:, :], 1.0), writes=["lrT"])
    P.op("dve", lambda e: e.memset(hst[:, :, :], 0.0), writes=[("hst", l_, m_) for l_ in range(NL) for m_ in range(4)])
    P.op("dve", lambda e: e.memset(S_state[:, :, :, :], 0.0), writes=[("S_state", l_) for l_ in range(NL)])
    P.op("dve", lambda e: e.memset(convst[:, :, :, :], 0.0), writes=[("convst", l_) for l_ in range(NL)])
    P.op("dve", lambda e: e.tensor_copy(out=ident_bf[:, :], in_=ident), reads=["consts"], writes=["ident_bf"])
    P.op("dve", lambda e: e.memset(ones_bf[:, :], 1.0), writes=["ones_bf"])
    t0, t0k = SC()
    P.op("dve", lambda e: e.tensor_scalar(out=dv[:, :, 0:8], in0=pvec[:, :, C_BA:C_BA + 8], scalar1=0.5, scalar2=None,
                                          op0=ALU.mult), reads=["pvec"], writes=["dv0"])
    P.op("act", lambda e: e.activation(out=t0[:, 0:NL * 4].rearrange("p (l m) -> p l m", l=NL),
                                       in_=pvec[:, :, C_LAM:C_LAM + 4], func=AF.Exp, scale=-1.0),
         reads=["pvec"], writes=[t0k])
    P.op("act", lambda e: e.activation(out=t0[:, 16:16 + NL * 4], in_=t0[:, 0:NL * 4], func=AF.Ln, bias=onec),
         reads=[t0k, "kcol"], writes=[t0k])
    P.op("dve", lambda e: e.tensor_scalar(out=dv[:, :, 8:12], in0=t0[:, 16:16 + NL * 4].rearrange("p (l m) -> p l m", l=NL),
                                          scalar1=-4.0, scalar2=None, op0=ALU.mult), reads=[t0k], writes=["dv1"])
    P.op("dve", lambda e: e.tensor_scalar(out=dv[:, :, 12:16], in0=t0[:, 16:16 + NL * 4].rearrange("p (l m) -> p l m", l=NL),
                                          scalar1=-8.0, scalar2=None, op0=ALU.mult), reads=[t0k], writes=["dv2"])
    P.op("dve", lambda e: e.tensor_scalar(out=dv[:, :, 16:17], in0=pvec[:, :, C_GN:C_GN + 1], scalar1=0.5, scalar2=None,
                                          op0=ALU.mult), reads=["pvec"], writes=["dv3"])
    DVK = ["dv0", "dv1", "dv2", "dv3", "pvec", "kcol"]

    first_w = [2]

    def wload(src_ap, shape3):
        i = cnt["w"] % WSLOTS
        cnt["w"] += 1
        a, b = shape3
        view = wring[i][:, 0:a * b].rearrange("p (a b) -> p a b", a=a)
        extra = []
        if first_w[0] > 0 and a * b >= 4096:
            first_w[0] -= 1
            extra = list(X0KEYS)
        P.dma("pool", lambda e: e.dma_start(out=view, in_=src_ap), ("w", i), reads=extra, writes=[("w", i)])
        return view, ("w", i)

    def w_cols(wd, l, c0, nc_):
        return wd[l, :, c0:c0 + nc_].rearrange("(k p) n -> p k n", p=128)

    def geom(g):
        p0, npr, ns = GROUPS[g]
        blocks = []
        c = 0
        while c < npr:
            n = min(512, npr - c)
            blocks.append((c, n, "P"))
            c += n
        if ns:
            blocks.append((npr, ns, "S"))
        ntile = (npr + ns) // 128
        return p0, npr, ns, blocks, ntile

    def tiles_of(c0, n):
        return list(range(c0 // 128, (c0 + n) // 128))

    rflat = resbuf[:, :, :].rearrange("p a g -> p (a g)")
    oflat = om[:, :, :].rearrange("p a g -> p (a g)").bitcast(F32)
    hflat = hT[:, :, :].rearrange("p a g -> p (a g)").bitcast(F32)
    OMKEYS = [("om", 0, t_) for t_ in range(6)] + [("om", 4 + q_, ("b", b_)) for q_ in range(4) for b_ in range(2)]
    HKEYS = [("h", b_, k_) for b_ in range(2) for k_ in range(8)]

    def stg_store(si):
        lo, hi = si * 1024, si * 1024 + 1023
        keys = [("res", m_, b_) for m_ in range(lo // GMAX, hi // GMAX + 1) for b_ in range(2)]
        return rflat[:, lo:lo + 1024], keys

    def stg_load(t):
        if t < 3:
            return oflat[:, t * 1024:(t + 1) * 1024], OMKEYS, ("stgO", t)
        return hflat[:, (t - 3) * 1024:(t - 2) * 1024], HKEYS, ("stgH", t - 3)

    def row0(g, t):
        p0, npr, ns, blocks, ntile = geom(g)
        return p0 + t * 128 if t * 128 < npr else SEQ + (t * 128 - npr)

    def prefetch_x(g, tiles):
        ntile = geom(g)[4]
        for t in tiles:
            if t < ntile:
                buf, keys, dk = stg_load(t)
                r0 = row0(g, t)
                o_ = P.dma("sp", lambda e, buf=buf, r0=r0: e.dma_start(out=buf, in_=x_d[r0:r0 + 128, :]), dk, writes=keys)
                if g == 0:
                    o_.prio = -1
                    if t < 3 and wbd_dma not in getattr(prefetch_x, "_done", []):
                        wbd_dma.deps.append(o_)

    def load_tile(g, t):
        buf, keys, dk = stg_load(t)
        for half in range(2):
            ps, pk = SM()
            for j in range(4):
                f = half * 4 + j
                P.op("pe", lambda e, ps=ps, j=j, f=f: e.transpose(out=ps[:, j * 128:(j + 1) * 128],
                                                              in_=buf[:, f * 128:(f + 1) * 128], identity=ident),
                     reads=keys + ["consts"], writes=[pk])
            P.op("act", lambda e, ps=ps, half=half: e.activation(
                out=xT[:, half * 4:half * 4 + 4, t * 128:(t + 1) * 128],
                in_=ps[:, :].rearrange("p (a b) -> p a b", a=4), func=AF.Copy),
                reads=[pk], writes=[("x", t, half * 4 + j) for j in range(4)])

    def store_tile(g, t, si):
        buf, keys = stg_store(si)
        r0 = row0(g, t)
        for half in range(2):
            ps, pk = SM()
            for j in range(4):
                f = half * 4 + j
                P.op("pe", lambda e, ps=ps, j=j, f=f: e.transpose(out=ps[:, j * 128:(j + 1) * 128],
                                                              in_=xT[:, f, t * 128:(t + 1) * 128], identity=ident),
                     reads=[("x", t, f), "consts"], writes=[pk])
            P.op("act", lambda e, ps=ps, half=half: e.activation(out=buf[:, half * 512:(half + 1) * 512],
                                                                in_=ps[:, :], func=AF.Copy),
                 reads=[pk], writes=keys)
        P.dma("sp", lambda e: e.dma_start(out=y_d[r0:r0 + 128, :], in_=buf), ("stgR", si), reads=keys)

    def load_store(gs, gl):
        nts = geom(gs)[4] if gs is not None else 0
        ntl = geom(gl)[4] if gl is not None else 0
        if gs is None:
            prefetch_x(gl, range(6))
        for t in range(max(nts, ntl)):
            if t < nts:
                store_tile(gs, t, t)
            if t < ntl:
                load_tile(gl, t)

    def rstd_inplace(ss, sk, n, inv_n):
        P.op("act", lambda e: e.activation(out=ss[:, 0:n], in_=ss[:, 0:n], func=AF.Ln, bias=epsc, scale=inv_n),
             reads=[sk, "kcol"], writes=[sk])
        P.op("act", lambda e: e.activation(out=ss[:, 0:n], in_=ss[:, 0:n], func=AF.Exp, scale=-0.5),
             reads=[sk], writes=[sk])

    def norm_to_h(g, l, gcol, only=None):
        p0, npr, ns, blocks, ntile = geom(g)
        for bi, (c0, n, kind) in enumerate(blocks):
            if only is not None and bi != only:
                continue
            xk = lambda k: [("x", t, k) for t in tiles_of(c0, n)]
            if bi in pre_ss:
                ss, sk = pre_ss.pop(bi)
            else:
                ss, sk = ssb[bi % 2], ("ssb", bi % 2)
                for k in range(8):
                    sq, sqk = SB()
                    P.op("act", lambda e, sq=sq, k=k: e.activation(out=sq[:, 0:n], in_=xT[:, k, c0:c0 + n], func=AF.Square),
                         reads=xk(k), writes=[sqk])
                    P.op("pe", lambda e, sq=sq, k=k, ss=ss: e.matmul(ss[:, 0:n], lhsT=ones_bf[:, :], rhs=sq[:, 0:n],
                                                                   start=(k == 0), stop=(k == 7)),
                         reads=[sqk, "ones_bf"], writes=[sk])
            rstd_inplace(ss, sk, n, 1.0 / D)
            for k in range(8):
                P.op("dve", lambda e, k=k, ss=ss: e.scalar_tensor_tensor(
                    out=hT[:, k, c0:c0 + n], in0=xT[:, k, c0:c0 + n], scalar=pvec[:, l, gcol + k:gcol + k + 1],
                    in1=ss[:, 0:n], op0=ALU.mult, op1=ALU.mult),
                    reads=xk(k) + [sk, "pvec"], writes=[("h", bi, k)])

    def gemm_fm(wv, wk, mcol, kparts, rhs_fn, rhs_keys, block, evac):
        ps, pk = MM()
        c0, n, kind = block
        for k in range(kparts):
            P.op("pe", lambda e, k=k, ps=ps: e.matmul(ps[:, 0:n], lhsT=wv[:, k, mcol:mcol + 128], rhs=rhs_fn(k),
                                                    start=(k == 0), stop=(k == kparts - 1)),
                 reads=[wk] + rhs_keys(k), writes=[pk])
        flush_pending()
        evac(ps, pk)
        bgstep(bgref[0], bgn[0])

    def phase_a(g, l, bg=None):
        p0, npr, ns, blocks, ntile = geom(g)
        nchp = npr // 64

        def hk(bi):
            return [("h", bi), ARENA]

        wv, wk = wload(w_cols(w_in_d, l, OFF_LR, 16), (8, 16))
        for bi, (c0, n, kind) in enumerate(blocks):
            ps, pk = MM()
            for k in range(8):
                P.op("pe", lambda e, k=k, ps=ps: e.matmul(ps[0:16, 0:n], lhsT=wv[:, k, 0:16], rhs=hT[:, k, c0:c0 + n],
                                                        start=(k == 0), stop=(k == 7)),
                     reads=[wk, ("h", bi, k)], writes=[pk])
            P.op("act", lambda e, ps=ps: e.activation(out=lrT[0:16, c0:c0 + n], in_=ps[0:16, 0:n], func=AF.Copy),
                 reads=[pk], writes=[("lrT", bi)])
        def la_gen():
          for bi, (c0, n, kind) in enumerate(blocks):
            tl = tiles_of(c0, n)
            pb = [SM(), SM()]
            tri = tri2 if kind == "P" else tri4
            for ti, t in enumerate(tl):
                ps, pk = MM()
                P.op("pe", lambda e, ps=ps, t=t: e.matmul(ps[:, 0:256], lhsT=lrT[0:17, t * 128:(t + 1) * 128],
                                                        rhs=wlr[0:17, l, :], start=True, stop=True),
                     reads=[("lrT", bi), "lrT", "wlr"], writes=[pk])
                e1, e1k = SC()
                P.op("act", lambda e, ps=ps, e1=e1: e.activation(out=e1[:, 0:256], in_=ps[:, 0:256], func=AF.Exp, scale=-1.0),
                     reads=[pk], writes=[e1k])
                e2, e2k = SC()
                P.op("act", lambda e, e1=e1, e2=e2: e.activation(out=e2[:, 0:256], in_=e1[:, 0:256], func=AF.Ln, bias=onec),
                     reads=[e1k, "kcol"], writes=[e2k])
                for pair in range(2):
                    P.op("pe", lambda e, e2=e2, pair=pair, ti=ti, tri=tri: e.matmul(
                        pb[pair][0][:, ti * 128:(ti + 1) * 128], lhsT=e2[:, pair * 128:(pair + 1) * 128], rhs=tri,
                        start=True, stop=True),
                        reads=[e2k, "consts"], writes=[pb[pair][1]])
                yield
            cs = 64 if kind == "P" else 32
            ch0 = c0 // 64 if kind == "P" else nchp
            nch = n // cs
            for pair in range(2):
                pbt, pbk = pb[pair]
                P.op("act", lambda e, pbt=pbt, pair=pair: e.activation(out=eb[:, pair, c0:c0 + n], in_=pbt[:, 0:n],
                                                                      func=AF.Exp, bias=lnq),
                     reads=[pbk, "kcol", ARENA], writes=[("eb", bi)])
                P.op("act", lambda e, pbt=pbt, pair=pair: e.activation(out=enb[:, pair, c0:c0 + n], in_=pbt[:, 0:n],
                                                                      func=AF.Exp, scale=-1.0),
                     reads=[pbk, ARENA], writes=[("enb", bi)])
                P.op("act", lambda e, pbt=pbt, pair=pair: e.activation(out=eb_last[:, pair, ch0:ch0 + nch],
                                                                      in_=pbt[:, cs - 1:n:cs], func=AF.Exp),
                     reads=[pbk], writes=[("ebl", bi)])
            yield

        lag = la_gen()
        bgref[0] = lag
        bgn[0] = 1
        wv, wk = wload(w_cols(w_in_d, l, OFF_XR, 512), (8, 512))
        for m in range(4):
            for bi, blk in enumerate(blocks):
                c0, n, kind = blk

                def ev(ps, pk, m=m, c0=c0, n=n, bi=bi, kind=kind):
                    if kind == "P":
                        P.op("dve", lambda e: e.tensor_copy(out=xr[:, m, 3 + c0:3 + c0 + n], in_=ps[:, 0:n]),
                             reads=[pk], writes=[("xr", m, bi)])
                    else:
                        P.op("act", lambda e: e.activation(out=xr_s[:, m, :, 3:35],
                                                           in_=ps[:, 0:128].rearrange("p (s t) -> p s t", s=4),
                                                           func=AF.Copy),
                             reads=[pk], writes=[("xr", m, bi)])
                gemm_fm(wv, wk, m * 128, 8, lambda k, c0=c0, n=n: hT[:, k, c0:c0 + n], lambda k, bi=bi: [("h", bi, k)], blk, ev)
        wv, wk = wload(w_cols(w_in_d, l, OFF_GR, 512), (8, 512))
        for m in range(4):
            for bi, blk in enumerate(blocks):
                c0, n, kind = blk

                def ev(ps, pk, m=m, c0=c0, n=n, bi=bi):
                    a1, a1k = SC()
                    P.op("act", lambda e: e.activation(out=a1[:, 0:n], in_=ps[:, 0:n], func=AF.Square),
                         reads=[pk], writes=[a1k])
                    a2, a2k = SC()
                    P.op("dve", lambda e: e.scalar_tensor_tensor(out=a2[:, 0:n], in0=a1[:, 0:n], scalar=1.0 / 0.044715,
                                                                 in1=ps[:, 0:n], op0=ALU.add, op1=ALU.mult),
                         reads=[pk, a1k], writes=[a2k])
                    P.op("act", lambda e: e.activation(out=a1[:, 0:n], in_=a2[:, 0:n], func=AF.Tanh,
                                                       scale=0.7978845608028654 * 0.044715),
                         reads=[a2k], writes=[a1k])
                    P.op("dve", lambda e: e.scalar_tensor_tensor(out=ggr[:, m, c0:c0 + n], in0=a1[:, 0:n], scalar=1.0,
                                                                 in1=ps[:, 0:n], op0=ALU.add, op1=ALU.mult),
                         reads=[pk, a1k, ARENA], writes=[("ggr", m, bi)])
                gemm_fm(wv, wk, m * 128, 8, lambda k, c0=c0, n=n: hT[:, k, c0:c0 + n], lambda k, bi=bi: [("h", bi, k)], blk, ev)
        for _ in lag:
            pass
        bgref[0] = bg
        bgn[0] = 4
        wv, wk = wload(w_cols(w_in_d, l, OFF_V, 512), (8, 512))
        for t in range(ntile):
            bi = [i for i, b in enumerate(blocks) if t in tiles_of(b[0], b[1])][0]
            ps, pk = MM()
            for k in range(8):
                P.op("pe", lambda e, k=k, ps=ps, t=t: e.matmul(ps[:, 0:512], lhsT=hT[:, k, t * 128:(t + 1) * 128],
                                                             rhs=wv[:, k, :], start=(k == 0), stop=(k == 7)),
                     reads=[wk, ("h", bi, k)], writes=[pk])
            P.op("act", lambda e, ps=ps, t=t: e.activation(out=v_tm[:, t, :], in_=ps[:, 0:512], func=AF.Copy),
                 reads=[pk, ARENA], writes=[("v", t)])
            bgstep(bgref[0], bgn[0])
        wv, wk = wload(w_cols(w_in_d, l, OFF_G, 512), (8, 512))
        for m in range(4):
            for bi, blk in enumerate(blocks):
                c0, n, kind = blk

                def ev(ps, pk, m=m, c0=c0, n=n, bi=bi):
                    th, thk = SC()
                    P.op("act", lambda e: e.activation(out=th[:, 0:n], in_=ps[:, 0:n], func=AF.Tanh, scale=0.5),
                         reads=[pk], writes=[thk])
                    P.op("dve", lambda e: e.scalar_tensor_tensor(out=sg[:, m, c0:c0 + n], in0=th[:, 0:n], scalar=1.0,
                                                                 in1=ps[:, 0:n], op0=ALU.add, op1=ALU.mult),
                         reads=[pk, thk, ARENA], writes=[("sg", m, bi)])
                gemm_fm(wv, wk, m * 128, 8, lambda k, c0=c0, n=n: hT[:, k, c0:c0 + n], lambda k, bi=bi: [("h", bi, k)], blk, ev)
        wv, wk = wload(w_cols(w_in_d, l, OFF_Q, 512), (8, 512))
        for m in range(4):
            for bi, blk in enumerate(blocks):
                c0, n, kind = blk

                def ev(ps, pk, m=m, c0=c0, n=n, bi=bi):
                    src = eb if m < 2 else enb
                    sk_ = ("eb", bi) if m < 2 else ("enb", bi)
                    P.op("dve", lambda e: e.tensor_tensor(out=qk[:, m, c0:c0 + n], in0=ps[:, 0:n],
                                                          in1=src[:, m % 2, c0:c0 + n], op=ALU.mult),
                         reads=[pk, sk_, ARENA], writes=[("qk", m, bi)])
                gemm_fm(wv, wk, m * 128, 8, lambda k, c0=c0, n=n: hT[:, k, c0:c0 + n], lambda k, bi=bi: [("h", bi, k)], blk, ev)

        bgref[0] = None

    def gla(g, l, bg=None):
        p0, npr, ns, blocks, ntile = geom(g)
        nchp = npr // 64
        last_prompt_group = (p0 + npr == SEQ)

        def bi_of(t):
            return [i for i, b in enumerate(blocks) if t in tiles_of(b[0], b[1])][0]

        qkk = lambda m, t: ("qk", m, bi_of(t))
        sslot = {}
        nsl = [0]

        def new_sslot(ci):
            s = nsl[0] % NSB
            nsl[0] += 1
            sslot[ci] = s
            return s

        def sz_write(s, src, srck):
            for h2 in range(2):
                P.op("pool", lambda e, s=s, h2=h2: e.tensor_copy(out=Sz[h2 * 64:(h2 + 1) * 64, s, h2::2, :],
                                                                in_=src[h2 * 64:(h2 + 1) * 64, :, :]),
                     reads=[srck, ARENA], writes=[("Sz", s, h2)])

        P.op("pool", lambda e: e.memset(Sz[:, :, :, :].rearrange("p t h e -> p (t h e)"), 0.0),
             reads=[ARENA], writes=[("Sz", s_, h2_) for s_ in range(NSB) for h2_ in range(2)])
        if ns:
            with nc.allow_non_contiguous_dma(reason="state load"):
                P.dma("sp", lambda e: e.dma_start(
                    out=S0[:, :, :, :].rearrange("p s a e -> p (s a) e"),
                    in_=sgla_d[l].rearrange("s (a h) d e -> (h d) (s a) e", h=2)), "S0", writes=["S0"])
        tinfo = {}

        tinfo_a = {}

        def pass1a(t):
            sample = (t * 128 >= npr)
            cs = 32 if sample else 64
            ncin = 128 // cs
            ci0 = nchp + 0 if sample else t * 2
            tc0 = t * 128
            bi = bi_of(t)
            ps, pk = SM()
            psb = ps[:, :].bitcast(BF16)
            for pair in range(2):
                P.op("pe", lambda e, psb=psb, pair=pair, tc0=tc0: e.transpose(
                    out=psb[:, pair * 128:(pair + 1) * 128], in_=qk[:, 2 + pair, tc0:tc0 + 128], identity=ident_bf[:, :]),
                    reads=[qkk(2 + pair, t), "ident_bf", ARENA], writes=[pk])
            if not sample:
                P.op("act", lambda e, psb=psb, t=t: e.activation(out=k_tm[:, t, :], in_=psb[:, 0:256], func=AF.Copy),
                     reads=[pk, ARENA], writes=[("k_tm", t)])
            else:
                for s in range(4):
                    P.op("dve", lambda e, psb=psb, s=s: e.tensor_scalar(out=k_tm_s[:, s, :], in0=psb[:, 0:256],
                                                                      scalar1=seqmask[:, s:s + 1], scalar2=None,
                                                                      op0=ALU.mult),
                         reads=[pk, "consts", ARENA], writes=[("k_tm_s", s)])
            tinfo_a[t] = True

        def pass1b(t):
            sample = (t * 128 >= npr)
            cs = 32 if sample else 64
            ncin = 128 // cs
            ci0 = nchp + 0 if sample else t * 2
            tc0 = t * 128
            bi = bi_of(t)
            asl = t % 3
            msk = mask4 if sample else mask2
            for h2 in range(2):
                ps, pk = SM()
                for pair in range(2):
                    P.op("pe", lambda e, ps=ps, pair=pair, h2=h2, tc0=tc0: e.matmul(
                        ps[:, pair * 128:(pair + 1) * 128], lhsT=qk[h2 * 64:(h2 + 1) * 64, 2 + pair, tc0:tc0 + 128],
                        rhs=qk[h2 * 64:(h2 + 1) * 64, pair, tc0:tc0 + 128], start=True, stop=True),
                        reads=[qkk(2 + pair, t), qkk(pair, t), ARENA], writes=[pk])
                P.op("dve", lambda e, ps=ps, asl=asl, msk=msk, h2=h2: e.tensor_tensor(
                    out=att[:, asl, :].rearrange("p (h i) -> p h i", h=4)[:, h2::2, :],
                    in0=ps[:, 0:256].rearrange("p (h i) -> p h i", h=2),
                    in1=msk.unsqueeze(1).to_broadcast([128, 2, 128]), op=ALU.mult),
                    reads=[pk, "consts", ARENA], writes=[("att", asl, h2)])
            for cc in range(ncin):
                ci = ci0 + cc
                ps, pk = SM()
                for pair in range(2):
                    if not sample:
                        r0 = cc * 64
                        P.op("pe", lambda e, ps=ps, pair=pair, r0=r0, t=t: e.matmul(
                            ps[:, pair * 256:(pair + 1) * 256], lhsT=k_tm[r0:r0 + 64, t, pair * 128:(pair + 1) * 128],
                            rhs=v_tm[r0:r0 + 64, t, pair * 256:(pair + 1) * 256], start=True, stop=True),
                            reads=[("k_tm", t), ("v", t), ARENA], writes=[pk])
                    else:
                        P.op("pe", lambda e, ps=ps, pair=pair, cc=cc, t=t: e.matmul(
                            ps[:, pair * 256:(pair + 1) * 256], lhsT=k_tm_s[:, cc, pair * 128:(pair + 1) * 128],
                            rhs=v_tm[:, t, pair * 256:(pair + 1) * 256], start=True, stop=True),
                            reads=[("k_tm_s", cc), ("v", t), ARENA], writes=[pk])
                dsl = ci % NDS
                for h2 in range(2):
                    P.op("dve", lambda e, ps=ps, h2=h2, dsl=dsl, ci=ci: e.tensor_tensor(
                        out=dSp[h2 * 64:(h2 + 1) * 64, dsl, :, :],
                        in0=ps[:, :].rearrange("p (a b e) -> p a b e", a=2, b=2)[h2 * 64:(h2 + 1) * 64, :, h2, :],
                        in1=eb_last[h2 * 64:(h2 + 1) * 64, :, ci:ci + 1].to_broadcast([64, 2, 128]), op=ALU.mult),
                        reads=[pk, ("ebl", bi)], writes=[("dSp", dsl, h2)])
                if not sample:
                    if ci == 0:
                        s = new_sslot(0)
                        sz_write(s, S_state[:, l, :, :], ("S_state", l))
                    src = S_state[:, l, :, :] if ci == 0 else Sf[:, (ci - 1) % 2, :, :]
                    srck = ("S_state", l) if ci == 0 else ("Sf", (ci - 1) % 2)
                    lastc = (ci == nchp - 1)
                    dst = S_state[:, l, :, :] if lastc else Sf[:, ci % 2, :, :]
                    dstk = ("S_state", l) if lastc else ("Sf", ci % 2)
                    if lastc and ci == 0:
                        raise AssertionError
                    for pair in range(2):
                        P.op("dve", lambda e, src=src, dst=dst, pair=pair, ci=ci, dsl=dsl: e.scalar_tensor_tensor(
                            out=dst[:, pair, :], in0=src[:, pair, :], scalar=eb_last[:, pair, ci:ci + 1],
                            in1=dSp[:, dsl, pair, :], op0=ALU.mult, op1=ALU.add),
                            reads=[srck, ("ebl", bi), ("dSp", dsl, 0), ("dSp", dsl, 1)], writes=[dstk])
                    if not lastc:
                        s = new_sslot(ci + 1)
                        sz_write(s, dst, dstk)
                    elif last_prompt_group:
                        with nc.allow_non_contiguous_dma(reason="state store"):
                            P.dma("sp", lambda e: e.dma_start(
                                out=glap_d[l].rearrange("(a h) d e -> (h d) a e", h=2), in_=S_state[:, l, :, :]),
                                ("glap", l), reads=[("S_state", l)])
                else:
                    s = new_sslot(ci)
                    sz_write(s, S0[:, cc, :, :], "S0")
                    for pair in range(2):
                        P.op("dve", lambda e, pair=pair, ci=ci, cc=cc, dsl=dsl: e.scalar_tensor_tensor(
                            out=Sf[:, cc % 2, pair, :], in0=S0[:, cc, pair, :], scalar=eb_last[:, pair, ci:ci + 1],
                            in1=dSp[:, dsl, pair, :], op0=ALU.mult, op1=ALU.add),
                            reads=["S0", ("ebl", bi), ("dSp", dsl, 0), ("dSp", dsl, 1)], writes=[("Sf", cc % 2)])
                    with nc.allow_non_contiguous_dma(reason="state store"):
                        P.dma("sp", lambda e, cc=cc: e.dma_start(
                            out=glas_d[l, cc].rearrange("(a h) d e -> (h d) a e", h=2), in_=Sf[:, cc % 2, :, :]),
                            ("glas", cc % 2), reads=[("Sf", cc % 2)])
            tinfo[t] = (sample, cs, ncin, ci0, tc0, bi, asl)

        def pass3(t):
            sample, cs, ncin, ci0, tc0, bi, asl = tinfo[t]
            po, pok = MM()
            for h in range(4):
                pair, h2 = h // 2, h % 2
                P.op("pe", lambda e, po=po, h=h, t=t, asl=asl: e.matmul(
                    po[:, h * 128:(h + 1) * 128], lhsT=v_tm[:, t, h * 128:(h + 1) * 128],
                    rhs=att[:, asl, h * 128:(h + 1) * 128], start=True, stop=False),
                    reads=[("v", t), ("att", asl, h2), ARENA], writes=[pok])
                for cc in range(ncin):
                    ci = ci0 + cc
                    s = sslot[ci]
                    P.op("pe", lambda e, po=po, h=h, pair=pair, h2=h2, cc=cc, s=s, tc0=tc0, cs=cs: e.matmul(
                        po[:, h * 128 + cc * cs:h * 128 + (cc + 1) * cs],
                        lhsT=Sz[:, s, h, :],
                        rhs=qk[:, pair, tc0 + cc * cs:tc0 + (cc + 1) * cs],
                        start=False, stop=(cc == ncin - 1)),
                        reads=[("Sz", s, 0), ("Sz", s, 1), qkk(pair, t), ARENA], writes=[pok])
            sq, sqk = SB()
            P.op("act", lambda e, po=po, sq=sq: e.activation(out=sq[:, :], in_=po[:, :], func=AF.Square),
                 reads=[pok], writes=[sqk])
            ss, sk = MM()
            P.op("pe", lambda e, ss=ss, sq=sq: e.matmul(ss[:, :], lhsT=ones_bf[:, :], rhs=sq[:, :], start=True, stop=True),
                 reads=[sqk, "ones_bf"], writes=[sk])
            rstd_inplace(ss, sk, 512, 1.0 / 128)
            t1, t1k = SC()
            P.op("dve", lambda e, po=po, t1=t1, tc0=tc0: e.tensor_tensor(
                out=t1[:, :].rearrange("p (h i) -> p h i", h=4), in0=po[:, :].rearrange("p (h i) -> p h i", h=4),
                in1=sg[:, :, tc0:tc0 + 128], op=ALU.mult),
                reads=[pok, sqk, ARENA] + [("sg", m, bi) for m in range(4)], writes=[t1k])
            P.op("dve", lambda e, t1=t1, ss=ss, tc0=tc0: e.scalar_tensor_tensor(
                out=om[:, 0:4, tc0:tc0 + 128], in0=t1[:, :].rearrange("p (h i) -> p h i", h=4),
                scalar=dv[:, l, 16:17], in1=ss[:, :].rearrange("p (h i) -> p h i", h=4), op0=ALU.mult, op1=ALU.mult),
                reads=[t1k, sk, "dv3"], writes=[("om", 0, t)])

        LAG = 2
        done3 = -1
        done1b = -1

        def flush3(upto):
            nonlocal done3
            while done3 < upto:
                done3 += 1
                pass3(done3)
                bgstep(bg, 6)

        for step in range(ntile + 1):
            if step < ntile:
                pass1a(step)
            t = step - 1
            if t >= 0:
                if t * 128 >= npr:
                    flush3(t - 1)
                pass1b(t)
                bgstep(bg, 6)
                flush3(t - LAG)
        flush3(ntile - 1)

    def bgstep(bg, n):
        if bg is None:
            return
        for _ in range(n):
            try:
                next(bg)
            except StopIteration:
                return

    def rglru(g, l):
        p0, npr, ns, blocks, ntile = geom(g)
        last_prompt_group = (p0 + npr == SEQ)
        P.op("dve", lambda e: e.tensor_copy(out=xr[:, :, 0:3], in_=convst[:, l, :, :]),
             reads=[("convst", l)], writes=["xrh"])
        if ns:
            with nc.allow_non_contiguous_dma(reason="conv state load"):
                for s in range(4):
                    for m_ in range(4):
                        P.dma("sp", lambda e, s=s, m_=m_: e.dma_start(
                            out=xr_s[:, m_, s, 0:3], in_=scv_d[l, s][:, m_ * 128:(m_ + 1) * 128].rearrange("j p -> p j")),
                            ("scv", m_), writes=[("xrsh", s)])
        def rg_iter(m, seg, slot):
            c0, n, kind, bis = seg
            cw = lambda j: pvec[:, l, C_CW + m * 4 + j:C_CW + m * 4 + j + 1]
            xk = [("xr", m, b_) for b_ in bis] + (["xrh"] if kind == "P" else [("xrsh", s_) for s_ in range(4)])
            ggk = [("ggr", m, b_) for b_ in bis]
            omk = [("om", 4 + m, ("b", b_)) for b_ in bis]

            def X(j):
                if kind == "P":
                    return xr[:, m, c0 + j:c0 + j + n]
                return xr_s[:, m, :, j:j + 32]

            def V(buf):
                if kind == "P":
                    return buf[:, 0:n]
                return buf[:, 0:128].rearrange("p (s t) -> p s t", s=4)
            bufs = [(resbuf[:, 4 * slot + q, :], [("res", 4 * slot + q, 0), ("res", 4 * slot + q, 1)]) for q in range(4)]
            (A, Ak), (B, Bk), (C, Ck), (Dd, Dk) = bufs
            xcb, xcbk = xcbw[slot], [("rg", "xcb", slot)]
            pg, pgk = ssb2, [("ssb", 0), ("ssb", 1)]
            pieces = [(0, min(n, 512))] + ([(512, n - 512)] if n > 512 else [])
            P.op("act", lambda e: e.activation(out=V(A), in_=X(3), func=AF.Identity,
                                               bias=pvec[:, l, C_CB + m:C_CB + m + 1], scale=cw(3)),
                 reads=xk + ["pvec"], writes=Ak)
            yield
            P.op("dve", lambda e: e.scalar_tensor_tensor(out=V(B), in0=X(2), scalar=cw(2), in1=V(A),
                                                         op0=ALU.mult, op1=ALU.add),
                 reads=xk + Ak + ["pvec"], writes=Bk)
            yield
            P.op("dve", lambda e: e.scalar_tensor_tensor(out=V(A), in0=X(1), scalar=cw(1), in1=V(B),
                                                         op0=ALU.mult, op1=ALU.add),
                 reads=xk + Bk + ["pvec"], writes=Ak)
            yield
            P.op("dve", lambda e: e.scalar_tensor_tensor(out=V(B), in0=X(0), scalar=cw(0), in1=V(A),
                                                         op0=ALU.mult, op1=ALU.add),
                 reads=xk + Ak + ["pvec"], writes=Bk)
            yield
            P.op("pool", lambda e: e.tensor_copy(out=xcb[:, 0:n], in_=B[:, 0:n]),
                 reads=Bk, writes=xcbk)
            yield
            for gi, (dst, dstk, bcol) in enumerate(((A, Ak, 0), (C, Ck, 4))):
                for (pc, pw) in pieces:
                    P.op("pe", lambda e, pc=pc, pw=pw, gi=gi: e.matmul(
                        pg[:, pc:pc + pw], lhsT=wbd[:, (l * 2 + gi) * 4 + m, :], rhs=xcb[:, pc:pc + pw],
                        start=True, stop=True), reads=xcbk + ["wbd"], writes=pgk)
                P.op("act", lambda e, dst=dst, bcol=bcol: e.activation(out=dst[:, 0:n], in_=pg[:, 0:n], func=AF.Tanh,
                                                                      scale=0.5, bias=dv[:, l, bcol + m:bcol + m + 1]),
                     reads=pgk + ["dv0"], writes=dstk)
                yield
            P.op("act", lambda e: e.activation(out=Dd[:, 0:n], in_=A[:, 0:n], func=AF.Exp,
                                               scale=dv[:, l, 8 + m:9 + m], bias=dv[:, l, 8 + m:9 + m]),
                 reads=Ak + ["dv1"], writes=Dk)
            yield
            P.op("act", lambda e: e.activation(out=A[:, 0:n], in_=A[:, 0:n], func=AF.Exp,
                                               scale=dv[:, l, 12 + m:13 + m], bias=dv[:, l, 12 + m:13 + m]),
                 reads=Ak + ["dv2"], writes=Ak)
            yield
            P.op("dve", lambda e: e.tensor_scalar(out=A[:, 0:n], in0=A[:, 0:n], scalar1=0.9999999, scalar2=None,
                                                  op0=ALU.min), reads=Ak, writes=Ak)
            yield
            P.op("act", lambda e: e.activation(out=A[:, 0:n], in_=A[:, 0:n], func=AF.Ln, scale=-1.0, bias=onec),
                 reads=Ak + ["kcol"], writes=Ak)
            yield
            P.op("act", lambda e: e.activation(out=A[:, 0:n], in_=A[:, 0:n], func=AF.Exp, scale=0.5),
                 reads=Ak, writes=Ak)
            yield
            P.op("dve", lambda e: e.scalar_tensor_tensor(out=C[:, 0:n], in0=C[:, 0:n], scalar=1.0,
                                                         in1=B[:, 0:n], op0=ALU.add, op1=ALU.mult),
                 reads=Ck + Bk, writes=Ck)
            yield
            P.op("dve", lambda e: e.scalar_tensor_tensor(out=C[:, 0:n], in0=C[:, 0:n], scalar=0.5,
                                                         in1=A[:, 0:n], op0=ALU.mult, op1=ALU.mult),
                 reads=Ck + Ak, writes=Ck)
            yield
            if kind == "P":
                P.op("dve", lambda e: e.tensor_tensor_scan(
                    out=B[:, 0:n], data0=Dd[:, 0:n], data1=C[:, 0:n], initial=hst[:, l, m:m + 1],
                    op0=ALU.mult, op1=ALU.add),
                    reads=Dk + Ck + [("hst", l, m)], writes=Bk)
                yield
                P.op("dve", lambda e: e.tensor_copy(out=hst[:, l, m:m + 1], in_=B[:, n - 1:n]),
                     reads=Bk, writes=[("hst", l, m)])
                yield
            else:
                for s in range(4):
                    P.op("dve", lambda e, s=s: e.tensor_tensor_scan(
                        out=B[:, s * 32:(s + 1) * 32], data0=Dd[:, s * 32:(s + 1) * 32],
                        data1=C[:, s * 32:(s + 1) * 32], initial=h0s[:, l, s, m:m + 1],
                        op0=ALU.mult, op1=ALU.add),
                        reads=Dk + Ck + ["h0s"], writes=Bk)
                    yield
                P.op("act", lambda e: e.activation(out=hso[:, :, m:m + 1],
                                                   in_=B[:, 31:128:32].unsqueeze(2), func=AF.Copy),
                     reads=Bk, writes=[("hso", m)])
                yield
            P.op("dve", lambda e: e.scalar_tensor_tensor(out=om[:, 4 + m, c0:c0 + n], in0=B[:, 0:n], scalar=0.5,
                                                         in1=ggr[:, m, c0:c0 + n], op0=ALU.mult, op1=ALU.mult),
                 reads=Bk + ggk + [ARENA], writes=omk)
            yield

        segs = [(0, npr, "P", [b_ for b_, bl in enumerate(blocks) if bl[2] == "P"])]
        if ns:
            segs.append((npr, ns, "S", [len(blocks) - 1]))
        for seg in segs:
            for m0 in (0, 2):
                gens = [rg_iter(m0, seg, 0), rg_iter(m0 + 1, seg, 1)]
                live = [True, True]
                while any(live):
                    for qi in range(2):
                        if live[qi]:
                            try:
                                next(gens[qi])
                            except StopIteration:
                                live[qi] = False
                    yield
        P.op("dve", lambda e: e.tensor_copy(out=convst[:, l, :, :], in_=xr[:, :, npr:npr + 3]),
             reads=[("xr", m, bi) for m in range(4) for bi in range(len(blocks))] + ["xrh"], writes=[("convst", l)])
        with nc.allow_non_contiguous_dma(reason="small state stores"):
            if last_prompt_group:
                for m_ in range(4):
                    P.dma("sp", lambda e, m_=m_: e.dma_start(
                        out=cvp_d[l][:, m_ * 128:(m_ + 1) * 128].rearrange("j p -> p j"), in_=convst[:, l, m_, :]),
                        ("cvp", m_), reads=[("convst", l)])
                P.dma("sp", lambda e: e.dma_start(out=rgp_d[l].rearrange("(m p) -> p m", p=128), in_=hst[:, l, :]),
                      ("rgp", l), reads=[("hst", l, m) for m in range(4)])
            if ns:
                bi_s = len(blocks) - 1
                for s in range(4):
                    for m_ in range(4):
                        P.dma("sp", lambda e, s=s, m_=m_: e.dma_start(
                            out=cvs_d[l, s][:, m_ * 128:(m_ + 1) * 128].rearrange("j p -> p j"),
                            in_=xr_s[:, m_, s, 32:35]), ("cvs", m_), reads=[("xr", m_, bi_s)])
                    P.dma("sp", lambda e, s=s: e.dma_start(out=rgs_d[l, s].rearrange("(m p) -> p m", p=128),
                                                           in_=hso[:, s, :]),
                          ("rgs", s), reads=[("hso", m) for m in range(4)])

    def res_evac(ps, pk, m, bi, c0, n):
        sq, sqk = SB()
        P.op("act", lambda e: e.activation(out=sq[:, 0:n], in_=ps[:, 0:n], func=AF.Square), reads=[pk], writes=[sqk])
        P.op("dve", lambda e: e.tensor_copy(out=resbuf[:, m, c0:c0 + n], in_=ps[:, 0:n]),
             reads=[pk, sqk], writes=[("res", m, bi)])
        ss, sk = ssb[bi % 2], ("ssb", bi % 2)
        pending.append(lambda: P.op(
            "pe", lambda e: e.matmul(ss[:, 0:n], lhsT=ones_bf[:, :], rhs=sq[:, 0:n], start=(m == 0), stop=(m == 7)),
            reads=[sqk, "ones_bf"], writes=[sk]))

    def flush_pending():
        while pending:
            pending.pop(0)()

    def res_apply(g, l, gcol, want_next=True, only=None):
        flush_pending()
        p0, npr, ns, blocks, ntile = geom(g)
        for bi, (c0, n, kind) in enumerate(blocks):
            if only is not None and bi != only:
                continue
            ss, sk = ssb[bi % 2], ("ssb", bi % 2)
            xk = lambda m: [("x", t, m) for t in tiles_of(c0, n)]
            rstd_inplace(ss, sk, n, 1.0 / D)
            if want_next:
                ns_, nsk = SM()
                pre_ss[bi] = (ns_, nsk)
            for m in range(8):
                tt, ttk = SC()
                P.op("dve", lambda e, m=m, tt=tt, ss=ss: e.scalar_tensor_tensor(
                    out=tt[:, 0:n], in0=resbuf[:, m, c0:c0 + n], scalar=pvec[:, l, gcol + m:gcol + m + 1],
                    in1=ss[:, 0:n], op0=ALU.mult, op1=ALU.mult),
                    reads=[("res", m, bi), sk, "pvec"], writes=[ttk])
                P.op("pool", lambda e, m=m, tt=tt: e.tensor_tensor(out=xT[:, m, c0:c0 + n], in0=xT[:, m, c0:c0 + n],
                                                                  in1=tt[:, 0:n], op=ALU.add),
                     reads=[ttk] + xk(m), writes=xk(m))
                if want_next:
                    sq, sqk = SB()
                    P.op("act", lambda e, m=m, sq=sq: e.activation(out=sq[:, 0:n], in_=xT[:, m, c0:c0 + n], func=AF.Square),
                         reads=xk(m), writes=[sqk])
                    P.op("pe", lambda e, m=m, sq=sq, ns_=ns_: e.matmul(ns_[:, 0:n], lhsT=ones_bf[:, :], rhs=sq[:, 0:n],
                                                                    start=(m == 0), stop=(m == 7)),
                         reads=[sqk, "ones_bf"], writes=[nsk])

    def phase_wout(g, l):
        p0, npr, ns, blocks, ntile = geom(g)
        wvs = [wload(w_cols(w_out_d, l, ch * 512, 512), (8, 512)) for ch in range(2)]
        for bi, blk in enumerate(blocks):
            c0, n, kind = blk
            omk = [("om", 0, t) for t in tiles_of(c0, n)] + [("om", 4 + q, ("b", bi)) for q in range(4)]
            for m in range(8):
                wv, wk = wvs[m // 4]
                gemm_fm(wv, wk, (m % 4) * 128, 8, lambda k, c0=c0, n=n: om[:, k, c0:c0 + n], lambda k, omk=omk: omk, blk,
                        lambda ps, pk, m=m, bi=bi, c0=c0, n=n: res_evac(ps, pk, m, bi, c0, n))
        if l == layers - 1 and g + 1 < ngroups and nph[0] < stop:
            prefetch_x(g + 1, range(0, 3))
        flush_pending()

    def ff1_group(g, l, wv, wk, mm, m, bi, blk):
        c0, n, kind = blk

        def ev(ps, pk):
            rb, rbk = SB()
            P.op("act", lambda e: e.activation(out=rb[:, 0:n], in_=ps[:, 0:n], func=AF.Relu),
                 reads=[pk], writes=[rbk])
            P.op("dve", lambda e: e.tensor_tensor(out=hid[:, m, c0:c0 + n], in0=ps[:, 0:n], in1=rb[:, 0:n],
                                                  op=ALU.mult),
                 reads=[pk, rbk, ARENA], writes=[("hid", m, bi)])
        gemm_fm(wv, wk, mm * 128, 8, lambda k: hT[:, k, c0:c0 + n], lambda k: [("h", bi, k)], blk, ev)

    def phase_ffn(g, l):
        p0, npr, ns, blocks, ntile = geom(g)
        barrier()
        wv, wk = wload(w_cols(w_ff1_d, l, 0, 512), (8, 512))
        for bi, blk in enumerate(blocks):
            res_apply(g, l, C_GPOSTM, only=bi)
            norm_to_h(g, l, C_GPREF, only=bi)
            for mm in range(4):
                ff1_group(g, l, wv, wk, mm, mm, bi, blk)
        for ch in range(1, 8):
            wv, wk = wload(w_cols(w_ff1_d, l, ch * 512, 512), (8, 512))
            for mm in range(4):
                for bi, blk in enumerate(blocks):
                    ff1_group(g, l, wv, wk, mm, ch * 4 + mm, bi, blk)
        if l == layers - 1 and g + 1 < ngroups and nph[0] < stop:
            prefetch_x(g + 1, range(3, 6))
        def ff2_w(m):
            return wload(w_ff2_d[l, :, m * 128:(m + 1) * 128].rearrange("(k p) n -> p k n", p=128), (32, 128))

        def ff2_group(wv, wk, m, bi, blk):
            c0, n, kind = blk
            gemm_fm(wv, wk, 0, 32, lambda k: hid[:, k, c0:c0 + n], lambda k: [("hid", k, bi), ARENA], blk,
                    lambda ps, pk: res_evac(ps, pk, m, bi, c0, n))

        TAIL = WSLOTS if l == layers - 1 else 0
        for m in range(8 - TAIL):
            wv, wk = ff2_w(m)
            for bi, blk in enumerate(blocks):
                ff2_group(wv, wk, m, bi, blk)
        if TAIL:
            tail_w = [ff2_w(m) for m in range(8 - TAIL, 8)]
            for bi, blk in enumerate(blocks):
                for i_, m in enumerate(range(8 - TAIL, 8)):
                    ff2_group(tail_w[i_][0], tail_w[i_][1], m, bi, blk)
        for bi in range(len(blocks)):
            res_apply(g, l, C_GPOSTF, want_next=(l + 1 < layers), only=bi)
            if l + 1 < layers:
                norm_to_h(g, l + 1, C_GPRE, only=bi)
        barrier()

    def mixer(g, l):
        bg = rglru(g, l)
        phase_a(g, l, bg)
        gla(g, l, bg)
        for _ in bg:
            pass

    def go(fn, *a):
        if nph[0] < stop:
            fn(*a)
        nph[0] += 1

    go(load_store, None, 0)
    for g in range(ngroups):
        for l in range(layers):
            if l == 0:
                go(norm_to_h, g, l, C_GPRE)
            go(mixer, g, l)
            go(phase_wout, g, l)
            go(phase_ffn, g, l)
        go(load_store, g, g + 1 if g + 1 < ngroups else None)
    P.finish()
    P.emit_all()
    return nc, P


def _consts():
    c = np.zeros((128, CW), np.float32)
    j = np.arange(128)[:, None]
    i = np.arange(128)[None, :]
    c[:, K_ID:K_ID + 128] = (j == i)
    m2 = ((j // 64) == (i // 64)) & (j <= i)
    m4 = ((j // 32) == (i // 32)) & (j <= i)
    c[:, K_TRI2:K_TRI2 + 128] = m2 * (-1.0 / 16.0)
    c[:, K_TRI4:K_TRI4 + 128] = m4 * (-1.0 / 16.0)
    c[:, K_M2:K_M2 + 128] = m2
    c[:, K_M4:K_M4 + 128] = m4
    for s in range(4):
        c[s * 32:(s + 1) * 32, K_SEQ + s] = 1.0
    return c


_CACHE = {}


def kernel(x_prompt, x_sample, state_gla, state_rglru, state_conv, g_pre_mix, w_in, w_lr2, b_lr, gla_norm,
           conv_w, conv_b, rg_wa, rg_ba, rg_wx, rg_bx, rg_lambda, w_out, g_post_mix, g_pre_ff, w_ff1, w_ff2,
           g_post_ff):
    f = lambda a: np.ascontiguousarray(np.asarray(a, dtype=np.float32))
    x_prompt, x_sample = f(x_prompt), f(x_sample)
    state_gla, state_rglru, state_conv = f(state_gla), f(state_rglru), f(state_conv)
    pvec = np.zeros((128, NL, PC), np.float32)
    for l in range(NL):
        for col, v in ((C_GPRE, g_pre_mix), (C_GPOSTM, g_post_mix), (C_GPREF, g_pre_ff), (C_GPOSTF, g_post_ff)):
            pvec[:, l, col:col + 8] = f(v)[l].reshape(8, 128).T
        pvec[:, l, C_GN] = f(gla_norm)[l]
        pvec[:, l, C_CW:C_CW + 16] = f(conv_w)[l].reshape(4, 4, 128).transpose(2, 1, 0).reshape(128, 16)
        pvec[:, l, C_CB:C_CB + 4] = f(conv_b)[l].reshape(4, 128).T
        pvec[:, l, C_BA:C_BA + 4] = f(rg_ba)[l].reshape(4, 128).T
        pvec[:, l, C_BX:C_BX + 4] = f(rg_bx)[l].reshape(4, 128).T
        pvec[:, l, C_LAM:C_LAM + 4] = f(rg_lambda)[l].reshape(4, 128).T
    wlr = np.zeros((17, NL, 256), np.float32)
    wlr[:16] = f(w_lr2).transpose(1, 0, 2)
    wlr[16] = f(b_lr)
    wbd = np.zeros((128, NL * 2 * 4, 128), np.float32)
    for l in range(NL):
        for gi, w in enumerate((f(rg_wa), f(rg_wx))):
            for m in range(4):
                for hb in range(2):
                    blk = w[l, m * 2 + hb]
                    wbd[hb * 64:(hb + 1) * 64, (l * 2 + gi) * 4 + m, hb * 64:(hb + 1) * 64] = blk
    consts = _consts()
    w_in, w_out, w_ff1, w_ff2 = f(w_in), f(w_out), f(w_ff1), f(w_ff2)

    if "nc" not in _CACHE:
        _CACHE["nc"] = build_program()[0]
    nc = _CACHE["nc"]
    in_maps = []
    for c in range(8):
        xs = x_sample[4 * c:4 * c + 4].reshape(128, D)
        in_maps.append({
            "x": np.ascontiguousarray(np.concatenate([x_prompt[c], xs], axis=0)),
            "sgla": np.ascontiguousarray(state_gla[:, 4 * c:4 * c + 4]),
            "srg": np.ascontiguousarray(state_rglru[:, 4 * c:4 * c + 4]),
            "scv": np.ascontiguousarray(state_conv[:, 4 * c:4 * c + 4]),
            "w_in": w_in, "w_out": w_out, "w_ff1": w_ff1, "w_ff2": w_ff2,
            "pvec": pvec, "wlr": wlr, "wbd": wbd, "consts": consts,
        })
    res = run_bass_kernel_spmd(nc, in_maps, core_ids=list(range(8)))
    R = res.results
    y_prompt = np.stack([R[c]["y"][:SEQ] for c in range(8)]).astype(np.float32)
    y_sample = np.concatenate([R[c]["y"][SEQ:].reshape(4, 32, D) for c in range(8)], axis=0).astype(np.float32)
    gla_prompt = np.stack([R[c]["glap"] for c in range(8)], axis=1).astype(np.float32)
    rglru_prompt = np.stack([R[c]["rgp"] for c in range(8)], axis=1).astype(np.float32)
    conv_prompt = np.stack([R[c]["cvp"] for c in range(8)], axis=1).astype(np.float32)
    gla_sample = np.concatenate([R[c]["glas"] for c in range(8)], axis=1).astype(np.float32)
    rglru_sample = np.concatenate([R[c]["rgs"] for c in range(8)], axis=1).astype(np.float32)
    conv_sample = np.concatenate([R[c]["cvs"] for c in range(8)], axis=1).astype(np.float32)
    return (y_prompt, y_sample, gla_prompt, rglru_prompt, conv_prompt, gla_sample, rglru_sample, conv_sample)
```

```python
import contextlib
import math
import numpy as np
import concourse.bass as bass
import concourse.mybir as mybir
from concourse.bass_utils import run_bass_kernel_spmd

F32 = mybir.dt.float32
BF16 = mybir.dt.bfloat16
AF = mybir.ActivationFunctionType
ALU = mybir.AluOpType

ENGS = ("pe", "act", "dve", "pool", "sp")
SEM_L = 4000


class _Op:
    __slots__ = ("eng", "emit", "deps", "idx", "sig", "signo", "dma_key", "dma_seq",
                 "clock", "waits", "is_dma", "name", "gseq", "dur", "nbytes", "aset", "cls", "prio")


class _Rec:
    def __init__(self):
        self.call = None

    def __getattr__(self, name):
        def f(*a, **k):
            self.call = (name, a, k)
            return None
        return f


class Prog:
    def __init__(self, nc):
        self.nc = nc
        self.streams = {e: [] for e in ENGS}
        self.ops = []
        self.last_write = {}
        self.readers = {}
        self.dma_count = {}
        self.dma_last = {}
        self.dma_ops = {}
        self.stack = contextlib.ExitStack()

    def sbuf(self, name, shape, dtype):
        return self.stack.enter_context(self.nc.sbuf_tensor("sb_" + name, list(shape), dtype))

    def psum(self, name, shape, dtype):
        return self.stack.enter_context(self.nc.psum_tensor("ps_" + name, list(shape), dtype))

    def _mk(self, eng, emit, reads, writes, name=None):
        o = _Op()
        o.eng = eng
        if emit is not None:
            rec = _Rec()
            emit(rec)
            mname, a, k = rec.call
            emit = (lambda engh, mname=mname, a=a, k=k: getattr(engh, mname)(*a, **k))
            o.dur, o.nbytes = self._est(eng, mname, a, k)
            o.aset = None
            if mname == "activation":
                fn = k.get("func")
                if fn == AF.Tanh:
                    o.aset = "E"
                elif fn == AF.Ln:
                    o.aset = "L"
                elif fn == AF.Exp:
                    o.aset = "X"
        else:
            o.dur, o.nbytes = 0.0, 0
            o.aset = None
        o.emit = emit
        o.name = name
        o.is_dma = False
        o.dma_key = None
        o.dma_seq = 0
        o.sig = False
        o.signo = 0
        o.prio = None
        deps = []
        for r in reads:
            w = self.last_write.get(r)
            if w is not None:
                deps.append(w)
        for w_ in writes:
            w = self.last_write.get(w_)
            if w is not None:
                deps.append(w)
            deps.extend(self.readers.get(w_, ()))
        for r in reads:
            self.readers.setdefault(r, []).append(o)
        for w_ in writes:
            self.last_write[w_] = o
            self.readers[w_] = []
        o.deps = deps
        o.idx = len(self.streams[eng])
        o.gseq = len(self.ops)
        self.streams[eng].append(o)
        self.ops.append(o)
        return o

    @staticmethod
    def _est(eng, mname, a, k):
        out = k.get("out", a[0] if a else None)
        n = 1
        try:
            for d in out.shape[1:]:
                n *= d
        except Exception:
            n = 512
        if mname == "dma_start":
            return 0.0, n * 4 * 128
        if eng == "pe":
            lh = k.get("lhsT", k.get("in_"))
            f = 4.0 if (mname == "matmul" and lh is not None and lh.dtype == F32) else 1.0
            return max(max(64.0, n) / 2.4 * f + 10.0, 105.0), 0
        wide = max(0, n - 512)
        if eng == "act":
            return 1.2 * (120.0 + 0.8 * min(n, 512) + 1.5 * wide), 0
        if eng == "dve":
            f = 2.0 if mname in ("tensor_tensor_scan",) else 1.0
            return 1.24 * (110.0 + f * (0.85 * min(n, 512) + 1.5 * wide)), 0
        if eng == "pool":
            return 1.23 * (150.0 + 1.6 * n), 0
        return 60.0, 0

    def schedule(self):
        ops = self.ops
        n = len(ops)
        succ = [[] for _ in range(n)]
        indeg = [0] * n
        for o in ops:
            ds = set(p.gseq for p in o.deps if p is not o)
            indeg[o.gseq] = len(ds)
            for d in ds:
                succ[d].append(o.gseq)
        for o in ops:
            o.cls = o.prio if o.prio is not None else (0 if any((p.eng == "pe" and o.eng != "pe") for p in o.deps) else 1)
        cur_set = [None]
        ready_time = [0.0] * n
        eng_free = {e: 0.0 for e in ENGS}
        ready = {e: [] for e in ENGS}
        for o in ops:
            if indeg[o.gseq] == 0:
                ready[o.eng].append(o.gseq)
        order = []
        SYNC = 150.0
        while len(order) < n:
            best = None
            for e in ENGS:
                rl = ready[e]
                if not rl:
                    continue
                tfree = eng_free[e]
                cg = None
                ckey = None
                for g_ in rl:
                    rt = ready_time[g_]
                    og = ops[g_]
                    sw = 0
                    if e == "act" and og.aset in ("E", "L") and cur_set[0] is not None and og.aset != cur_set[0]:
                        sw = 1
                    key = (0.0, sw, og.cls, g_) if rt <= tfree else (rt - tfree, sw, og.cls, g_)
                    if ckey is None or key < ckey:
                        ckey, cg = key, g_
                start = max(tfree, ready_time[cg])
                if best is None or (start, cg) < (best[0], best[2]):
                    best = (start, e, cg)
            start, e, g_ = best
            o = ops[g_]
            ready[e].remove(g_)
            if o.is_dma:
                busy = 1400.0 if e == "pool" else 120.0
                fin = start + busy + 2200.0 + o.nbytes / 180.0
            else:
                busy = o.dur
                if e == "act" and o.aset in ("E", "L"):
                    if cur_set[0] is not None and cur_set[0] != o.aset:
                        busy += 1283.0
                    cur_set[0] = o.aset
                fin = start + busy
            eng_free[e] = start + busy
            order.append(o)
            for s_ in succ[g_]:
                rt = fin if (e == "pe" and ops[s_].eng == "pe") else fin + SYNC
                if ready_time[s_] < rt:
                    ready_time[s_] = rt
                indeg[s_] -= 1
                if indeg[s_] == 0:
                    ready[ops[s_].eng].append(s_)
        self.ops = order
        self.streams = {e: [] for e in ENGS}
        for i, o in enumerate(order):
            o.gseq = i
            o.idx = len(self.streams[o.eng])
            self.streams[o.eng].append(o)
        self.est_span = max(eng_free.values())

    def op(self, eng, emit, reads=(), writes=(), name=None):
        return self._mk(eng, emit, tuple(reads), tuple(writes), name)

    def dma(self, eng, emit, key, reads=(), writes=(), name=None):
        o = self._mk(eng, emit, tuple(reads), tuple(writes), name)
        o.is_dma = True
        o.dma_key = key
        prev = self.dma_last.get(key)
        if prev is not None:
            o.deps.append(prev)
        self.dma_count[key] = self.dma_count.get(key, 0) + 1
        o.dma_seq = self.dma_count[key]
        self.dma_last[key] = o
        self.dma_ops[(key, o.dma_seq)] = o
        return o

    def finish(self):
        o = self._mk("sp", None, (), (), "finish")
        o.deps = list(self.dma_last.values())
        return o

    def emit_all(self, sched=True):
        nc = self.nc
        if sched:
            self.schedule()
        last_clock = {e: {} for e in ENGS}
        for o in self.ops:
            clock = dict(last_clock[o.eng])
            waits = {}
            for p in o.deps:
                if p is o:
                    continue
                if p.is_dma:
                    chan = ("dma", p.dma_key)
                    need = p.dma_seq
                else:
                    if p.eng == "pe" and o.eng == "pe" and not o.is_dma:
                        continue
                    chan = p.eng
                    need = p.idx
                if clock.get(chan, -1) >= need:
                    continue
                if waits.get(chan, -1) < need:
                    waits[chan] = need
            for chan, need in waits.items():
                if isinstance(chan, tuple):
                    p = self.dma_ops[(chan[1], need)]
                else:
                    p = self.streams[chan][need]
                    p.sig = True
                if clock.get(chan, -1) < need:
                    clock[chan] = need
                for c, v in p.clock.items():
                    if clock.get(c, -1) < v:
                        clock[c] = v
            o.clock = clock
            o.waits = waits
            last_clock[o.eng] = clock

        eng_sems = {}
        for e in ENGS:
            n = 0
            for o in self.streams[e]:
                if o.sig:
                    n += 1
                    o.signo = n
            nsem = (n + SEM_L - 1) // SEM_L
            eng_sems[e] = [self.stack.enter_context(nc.semaphore(f"s_{e}_{i}")) for i in range(nsem)]
        dma_sems = {}
        for i, k in enumerate(self.dma_count):
            dma_sems[k] = self.stack.enter_context(nc.semaphore(f"d_{i}"))
        self.n_sems = sum(len(v) for v in eng_sems.values()) + len(dma_sems)
        streams = self.streams

        def run(engh, e):
            for o in streams[e]:
                for chan, need in o.waits.items():
                    if isinstance(chan, tuple):
                        engh.wait_ge(dma_sems[chan[1]], 16 * need)
                    else:
                        p = streams[chan][need]
                        s = p.signo - 1
                        engh.wait_ge(eng_sems[chan][s // SEM_L], s % SEM_L + 1)
                if o.emit is None:
                    continue
                ins = o.emit(engh)
                if o.is_dma:
                    ins.then_inc(dma_sems[o.dma_key], 16)
                elif o.sig:
                    s = o.signo - 1
                    ins.then_inc(eng_sems[e][s // SEM_L], 1)

        with nc.allow_non_contiguous_dma(reason="small strided state/param transfers"), nc.Block() as block:
            @block.tensor
            def _(eng):
                run(eng, "pe")

            @block.scalar
            def _(eng):
                run(eng, "act")

            @block.vector
            def _(eng):
                run(eng, "dve")

            @block.gpsimd
            def _(eng):
                run(eng, "pool")

            @block.sync
            def _(eng):
                run(eng, "sp")
        self.stack.close()


D = 1024
DIN = 2576
DFF = 4096
NL = 2
SEQ = 2048
NTOK = 2176
GROUPS = [(0, 768, 0), (768, 768, 0), (1536, 512, 128)]
GMAX = 768
EPS = 1e-6
OFF_Q, OFF_K, OFF_V, OFF_G, OFF_LR, OFF_XR, OFF_GR = 0, 256, 512, 1024, 1536, 1552, 2064
PC = 65
C_GPRE, C_GPOSTM, C_GPREF, C_GPOSTF, C_GN, C_CW, C_CB, C_BA, C_BX, C_LAM = 0, 8, 16, 24, 32, 33, 49, 53, 57, 61
K_ID, K_TRI2, K_TRI4, K_M2, K_M4, K_SEQ = 0, 128, 256, 384, 512, 640
CW = 644
NSCR = 6
NSCB = 4
WSLOTS = 3


def build_program(layers=NL, ngroups=len(GROUPS), stop=10 ** 9):
    nc = bass.Bass("TRN2", target_bir_lowering=False)
    P = Prog(nc)

    def din(name, shape):
        return nc.dram_tensor(name, list(shape), F32, kind="ExternalInput").ap()

    def dout(name, shape):
        return nc.dram_tensor(name, list(shape), F32, kind="ExternalOutput").ap()

    x_d = din("x", [NTOK, D])
    sgla_d = din("sgla", [NL, 4, 4, 64, 128])
    srg_d = din("srg", [NL, 4, 512])
    scv_d = din("scv", [NL, 4, 3, 512])
    w_in_d = din("w_in", [NL, D, DIN])
    w_out_d = din("w_out", [NL, D, D])
    w_ff1_d = din("w_ff1", [NL, D, DFF])
    w_ff2_d = din("w_ff2", [NL, DFF, D])
    pvec_d = din("pvec", [128, NL, PC])
    wlr_d = din("wlr", [17, NL, 256])
    wbd_d = din("wbd", [128, NL * 2 * 4, 128])
    consts_d = din("consts", [128, CW])

    y_d = dout("y", [NTOK, D])
    glap_d = dout("glap", [NL, 4, 64, 128])
    rgp_d = dout("rgp", [NL, 512])
    cvp_d = dout("cvp", [NL, 3, 512])
    glas_d = dout("glas", [NL, 4, 4, 64, 128])
    rgs_d = dout("rgs", [NL, 4, 512])
    cvs_d = dout("cvs", [NL, 4, 3, 512])

    xT = P.sbuf("xT", [128, 8, GMAX], F32)
    hT = P.sbuf("hT", [128, 8, GMAX], BF16)
    om = P.sbuf("om", [128, 8, GMAX], BF16)
    resbuf = P.sbuf("resbuf", [128, 8, GMAX], F32)
    wring = [P.sbuf(f"wring{i}", [128, 4096], BF16) for i in range(WSLOTS)]
    arena = P.sbuf("arena", [128, 24576], BF16)
    hid = arena[:, :].rearrange("p (k g) -> p k g", k=32)
    _ao = [0]

    def carve(nbf16):
        a = _ao[0]
        _ao[0] += nbf16
        assert _ao[0] <= 24576
        return arena[:, a:a + nbf16]

    eb = carve(2 * GMAX * 2).bitcast(F32).rearrange("p (a g) -> p a g", a=2)
    enb = carve(2 * GMAX * 2).bitcast(F32).rearrange("p (a g) -> p a g", a=2)
    qk = carve(4 * GMAX).rearrange("p (a g) -> p a g", a=4)
    v_tm = carve(6 * 512).rearrange("p (t e) -> p t e", t=6)
    sg = carve(4 * GMAX).rearrange("p (a g) -> p a g", a=4)
    ggr = carve(4 * GMAX).rearrange("p (a g) -> p a g", a=4)
    k_tm = carve(6 * 256).rearrange("p (t e) -> p t e", t=6)
    k_tm_s = carve(4 * 256).rearrange("p (t e) -> p t e", t=4)
    NSB = 7
    Sz = carve(NSB * 512).rearrange("p (t h e) -> p t h e", t=NSB, h=4)

    att = P.sbuf("att", [128, 3, 512], BF16)
    xr = P.sbuf("xr", [128, 4, GMAX + 3], F32)
    xr_s = P.sbuf("xr_s", [128, 4, 4, 35], F32)
    NDS = 3
    dSp = P.sbuf("dSp", [128, NDS, 2, 128], F32)
    scr = [P.sbuf(f"scr{i}", [128, 512], F32) for i in range(NSCR)]
    scb = [P.sbuf(f"scb{i}", [128, 512], BF16) for i in range(NSCB)]
    xcbw = [P.sbuf(f"xcbw{i}", [128, GMAX], BF16) for i in range(2)]
    lrT = P.sbuf("lrT", [32, GMAX], F32)
    consts = P.sbuf("consts", [128, CW], F32)
    ident_bf = P.sbuf("ident_bf", [128, 128], BF16)
    ones_bf = P.sbuf("ones_bf", [128, 128], BF16)
    pvec = P.sbuf("pvec", [128, NL, PC], F32)
    dv = P.sbuf("dv", [128, NL, 24], F32)
    kcol = P.sbuf("kcol", [128, 4], F32)
    wlr = P.sbuf("wlr", [17, NL, 256], F32)
    wbd = P.sbuf("wbd", [128, NL * 2 * 4, 128], BF16)
    eb_last = P.sbuf("eb_last", [128, 2, 16], F32)
    Sf = P.sbuf("Sf", [128, 2, 2, 128], F32)
    S_state = P.sbuf("S_state", [128, NL, 2, 128], F32)
    S0 = P.sbuf("S0", [128, 4, 2, 128], F32)
    hst = P.sbuf("hst", [128, NL, 4], F32)
    h0s = P.sbuf("h0s", [128, NL, 4, 4], F32)
    hso = P.sbuf("hso", [128, 4, 4], F32)
    convst = P.sbuf("convst", [128, NL, 4, 3], F32)

    mmb = [P.psum(f"mmb{i}", [128, 512], F32) for i in range(3)]
    ssb2 = P.psum("ssb2", [128, 1024], F32)
    ssb = [ssb2[:, 0:512], ssb2[:, 512:1024]]
    smb = [P.psum(f"smb{i}", [128, 512], F32) for i in range(3)]
    cnt = {"mm": 0, "sm": 0, "scr": 0, "scb": 0, "w": 0}

    def MM():
        i = cnt["mm"] % 3
        cnt["mm"] += 1
        return mmb[i], ("mmb", i)

    def SM():
        i = cnt["sm"] % 3
        cnt["sm"] += 1
        return smb[i], ("smb", i)

    def SC():
        i = cnt["scr"] % NSCR
        cnt["scr"] += 1
        return scr[i], ("scr", i)

    def SB():
        i = cnt["scb"] % NSCB
        cnt["scb"] += 1
        return scb[i], ("scb", i)

    ARENA = "arena"
    nph = [0]
    bgref = [None]
    bgn = [4]
    pending = []
    pre_ss = {}

    def barrier():
        P.op("sp", lambda e: e.nop(), reads=(), writes=[ARENA])

    ident = consts[:, K_ID:K_ID + 128]
    tri2 = consts[:, K_TRI2:K_TRI2 + 128]
    tri4 = consts[:, K_TRI4:K_TRI4 + 128]
    mask2 = consts[:, K_M2:K_M2 + 128]
    mask4 = consts[:, K_M4:K_M4 + 128]
    seqmask = consts[:, K_SEQ:K_SEQ + 4]
    epsc = kcol[:, 0:1]
    onec = kcol[:, 1:2]
    lnq = kcol[:, 2:3]

    X0KEYS = [("stg", "O", i_) for i_ in range(3)]

    P.dma("sp", lambda e: e.dma_start(out=consts[:, :], in_=consts_d), "c0", writes=["consts"])
    P.dma("sp", lambda e: e.dma_start(out=pvec[:, :, :], in_=pvec_d), "c1", writes=["pvec"])
    P.dma("sp", lambda e: e.dma_start(out=wlr[:, :, :], in_=wlr_d), "c2", writes=["wlr"])
    wbd_dma = P.dma("pool", lambda e: e.dma_start(out=wbd[:, :, :], in_=wbd_d), "c3", writes=["wbd"])
    with nc.allow_non_contiguous_dma(reason="tiny state loads"):
        for l_ in range(NL):
            for s_ in range(4):
                P.dma("sp", lambda e, l_=l_, s_=s_: e.dma_start(out=h0s[:, l_, s_, :],
                                                                in_=srg_d[l_, s_].rearrange("(m p) -> p m", p=128)),
                      ("c4", s_), writes=["h0s"])
    P.op("dve", lambda e: e.memset(kcol[:, 0:1], EPS), writes=["kcol"])
    P.op("dve", lambda e: e.memset(kcol[:, 1:2], 1.0), writes=["kcol"])
    P.op("dve", lambda e: e.memset(kcol[:, 2:3], math.log(0.125)), writes=["kcol"])
    P.op("dve", lambda e: e.memset(kcol[:, 3:4], 0.0), writes=["kcol"])
    P.op("dve", lambda e: e.memset(lrT[:, :], 1.0), writes=["lrT"])
    P.op("dve", lambda e: e.memset(hst[:, :, :], 0.0), writes=[("hst", l_, m_) for l_ in range(NL) for m_ in range(4)])
    P.op("dve", lambda e: e.memset(S_state[:, :, :, :], 0.0), writes=[("S_state", l_) for l_ in range(NL)])
    P.op("dve", lambda e: e.memset(convst[:, :, :, :], 0.0), writes=[("convst", l_) for l_ in range(NL)])
    P.op("dve", lambda e: e.tensor_copy(out=ident_bf[:, :], in_=ident), reads=["consts"], writes=["ident_bf"])
    P.op("dve", lambda e: e.memset(ones_bf[:, :], 1.0), writes=["ones_bf"])
    t0, t0k = SC()
    P.op("dve", lambda e: e.tensor_scalar(out=dv[:, :, 0:8], in0=pvec[:, :, C_BA:C_BA + 8], scalar1=0.5, scalar2=None,
                                          op0=ALU.mult), reads=["pvec"], writes=["dv0"])
    P.op("act", lambda e: e.activation(out=t0[:, 0:NL * 4].rearrange("p (l m) -> p l m", l=NL),
                                       in_=pvec[:, :, C_LAM:C_LAM + 4], func=AF.Exp, scale=-1.0),
         reads=["pvec"], writes=[t0k])
    P.op("act", lambda e: e.activation(out=t0[:, 16:16 + NL * 4], in_=t0[:, 0:NL * 4], func=AF.Ln, bias=onec),
         reads=[t0k, "kcol"], writes=[t0k])
    P.op("dve", lambda e: e.tensor_scalar(out=dv[:, :, 8:12], in0=t0[:, 16:16 + NL * 4].rearrange("p (l m) -> p l m", l=NL),
                                          scalar1=-4.0, scalar2=None, op0=ALU.mult), reads=[t0k], writes=["dv1"])
    P.op("dve", lambda e: e.tensor_scalar(out=dv[:, :, 12:16], in0=t0[:, 16:16 + NL * 4].rearrange("p (l m) -> p l m", l=NL),
                                          scalar1=-8.0, scalar2=None, op0=ALU.mult), reads=[t0k], writes=["dv2"])
    P.op("dve", lambda e: e.tensor_scalar(out=dv[:, :, 16:17], in0=pvec[:, :, C_GN:C_GN + 1], scalar1=0.5, scalar2=None,
                                          op0=ALU.mult), reads=["pvec"], writes=["dv3"])
    DVK = ["dv0", "dv1", "dv2", "dv3", "pvec", "kcol"]

    first_w = [2]

    def wload(src_ap, shape3):
        i = cnt["w"] % WSLOTS
        cnt["w"] += 1
        a, b = shape3
        view = wring[i][:, 0:a * b].rearrange("p (a b) -> p a b", a=a)
        extra = []
        if first_w[0] > 0 and a * b >= 4096:
            first_w[0] -= 1
            extra = list(X0KEYS)
        P.dma("pool", lambda e: e.dma_start(out=view, in_=src_ap), ("w", i), reads=extra, writes=[("w", i)])
        return view, ("w", i)

    def w_cols(wd, l, c0, nc_):
        return wd[l, :, c0:c0 + nc_].rearrange("(k p) n -> p k n", p=128)

    def geom(g):
        p0, npr, ns = GROUPS[g]
        blocks = []
        c = 0
        while c < npr:
            n = min(512, npr - c)
            blocks.append((c, n, "P"))
            c += n
        if ns:
            blocks.append((npr, ns, "S"))
        ntile = (npr + ns) // 128
        return p0, npr, ns, blocks, ntile

    def tiles_of(c0, n):
        return list(range(c0 // 128, (c0 + n) // 128))

    rflat = resbuf[:, :, :].rearrange("p a g -> p (a g)")
    oflat = om[:, :, :].rearrange("p a g -> p (a g)").bitcast(F32)
    hflat = hT[:, :, :].rearrange("p a g -> p (a g)").bitcast(F32)
    OMKEYS = [("om", 0, t_) for t_ in range(6)] + [("om", 4 + q_, ("b", b_)) for q_ in range(4) for b_ in range(2)]
    HKEYS = [("h", b_, k_) for b_ in range(2) for k_ in range(8)]

    RKEYS = [("res", m_, b_) for m_ in range(8) for b_ in range(2)]
    SLOTKEYS = {"O": OMKEYS, "H": HKEYS, "R": RKEYS}

    def stg_guard(kind):
        P.op("sp", lambda e: e.nop(), writes=SLOTKEYS[kind])

    def stg_load(t):
        if t < 3:
            return oflat[:, t * 1024:(t + 1) * 1024], "O", t, ("stgO", t)
        return hflat[:, (t - 3) * 1024:(t - 2) * 1024], "H", t - 3, ("stgH", t - 3)

    def row0(g, t):
        p0, npr, ns, blocks, ntile = geom(g)
        return p0 + t * 128 if t * 128 < npr else SEQ + (t * 128 - npr)

    def prefetch_x(g, tiles):
        ntile = geom(g)[4]
        tl = [t for t in tiles if t < ntile]
        for kind in ("O", "H"):
            if any((t < 3) == (kind == "O") for t in tl):
                stg_guard(kind)
        for t in tl:
            buf, kind, slot, dk = stg_load(t)
            r0 = row0(g, t)
            o_ = P.dma("sp", lambda e, buf=buf, r0=r0: e.dma_start(out=buf, in_=x_d[r0:r0 + 128, :]), dk,
                       reads=SLOTKEYS[kind], writes=[("stg", kind, slot)])
            if g == 0:
                o_.prio = -1
                if t < 3:
                    wbd_dma.deps.append(o_)

    def load_tile(g, t):
        buf, kind, slot, dk = stg_load(t)
        for half in range(2):
            ps, pk = SM()
            for j in range(4):
                f = half * 4 + j
                P.op("pe", lambda e, ps=ps, j=j, f=f: e.transpose(out=ps[:, j * 128:(j + 1) * 128],
                                                              in_=buf[:, f * 128:(f + 1) * 128], identity=ident),
                     reads=[("stg", kind, slot)] + SLOTKEYS[kind] + ["consts"], writes=[pk])
            P.op("act", lambda e, ps=ps, half=half: e.activation(
                out=xT[:, half * 4:half * 4 + 4, t * 128:(t + 1) * 128],
                in_=ps[:, :].rearrange("p (a b) -> p a b", a=4), func=AF.Copy),
                reads=[pk], writes=[("x", t, half * 4 + j) for j in range(4)])

    def store_tile(g, t, si):
        buf = rflat[:, si * 1024:(si + 1) * 1024]
        r0 = row0(g, t)
        for half in range(2):
            ps, pk = SM()
            for j in range(4):
                f = half * 4 + j
                P.op("pe", lambda e, ps=ps, j=j, f=f: e.transpose(out=ps[:, j * 128:(j + 1) * 128],
                                                              in_=xT[:, f, t * 128:(t + 1) * 128], identity=ident),
                     reads=[("x", t, f), "consts"], writes=[pk])
            P.op("act", lambda e, ps=ps, half=half: e.activation(out=buf[:, half * 512:(half + 1) * 512],
                                                                in_=ps[:, :], func=AF.Copy),
                 reads=[pk] + RKEYS, writes=[("stgR", si, half)])
        P.dma("sp", lambda e: e.dma_start(out=y_d[r0:r0 + 128, :], in_=buf), ("stgR", si),
              reads=[("stgR", si, 0), ("stgR", si, 1)] + RKEYS)

    def load_store(gs, gl):
        nts = geom(gs)[4] if gs is not None else 0
        ntl = geom(gl)[4] if gl is not None else 0
        if gs is None:
            prefetch_x(gl, range(6))
        if nts:
            stg_guard("R")
        for t in range(max(nts, ntl)):
            if t < nts:
                store_tile(gs, t, t)
            if t < ntl:
                load_tile(gl, t)

    def rstd_inplace(ss, sk, n, inv_n):
        P.op("act", lambda e: e.activation(out=ss[:, 0:n], in_=ss[:, 0:n], func=AF.Ln, bias=epsc, scale=inv_n),
             reads=[sk, "kcol"], writes=[sk])
        P.op("act", lambda e: e.activation(out=ss[:, 0:n], in_=ss[:, 0:n], func=AF.Exp, scale=-0.5),
             reads=[sk], writes=[sk])

    def norm_to_h(g, l, gcol, only=None):
        p0, npr, ns, blocks, ntile = geom(g)
        for bi, (c0, n, kind) in enumerate(blocks):
            if only is not None and bi != only:
                continue
            xk = lambda k: [("x", t, k) for t in tiles_of(c0, n)]
            if bi in pre_ss:
                ss, sk = pre_ss.pop(bi)
            else:
                ss, sk = ssb[bi % 2], ("ssb", bi % 2)
                for k in range(8):
                    sq, sqk = SB()
                    P.op("act", lambda e, sq=sq, k=k: e.activation(out=sq[:, 0:n], in_=xT[:, k, c0:c0 + n], func=AF.Square),
                         reads=xk(k), writes=[sqk])
                    P.op("pe", lambda e, sq=sq, k=k, ss=ss: e.matmul(ss[:, 0:n], lhsT=ones_bf[:, :], rhs=sq[:, 0:n],
                                                                   start=(k == 0), stop=(k == 7)),
                         reads=[sqk, "ones_bf"], writes=[sk])
            rstd_inplace(ss, sk, n, 1.0 / D)
            for k in range(8):
                P.op("dve", lambda e, k=k, ss=ss: e.scalar_tensor_tensor(
                    out=hT[:, k, c0:c0 + n], in0=xT[:, k, c0:c0 + n], scalar=pvec[:, l, gcol + k:gcol + k + 1],
                    in1=ss[:, 0:n], op0=ALU.mult, op1=ALU.mult),
                    reads=xk(k) + [sk, "pvec"], writes=[("h", bi, k)])

    def gemm_fm(wv, wk, mcol, kparts, rhs_fn, rhs_keys, block, evac):
        ps, pk = MM()
        c0, n, kind = block
        for k in range(kparts):
            P.op("pe", lambda e, k=k, ps=ps: e.matmul(ps[:, 0:n], lhsT=wv[:, k, mcol:mcol + 128], rhs=rhs_fn(k),
                                                    start=(k == 0), stop=(k == kparts - 1)),
                 reads=[wk] + rhs_keys(k), writes=[pk])
        flush_pending()
        evac(ps, pk)
        bgstep(bgref[0], bgn[0])

    def phase_a(g, l, bg=None):
        p0, npr, ns, blocks, ntile = geom(g)
        nchp = npr // 64

        def hk(bi):
            return [("h", bi), ARENA]

        wv, wk = wload(w_cols(w_in_d, l, OFF_LR, 16), (8, 16))
        for bi, (c0, n, kind) in enumerate(blocks):
            ps, pk = MM()
            for k in range(8):
                P.op("pe", lambda e, k=k, ps=ps: e.matmul(ps[0:16, 0:n], lhsT=wv[:, k, 0:16], rhs=hT[:, k, c0:c0 + n],
                                                        start=(k == 0), stop=(k == 7)),
                     reads=[wk, ("h", bi, k)], writes=[pk])
            P.op("act", lambda e, ps=ps: e.activation(out=lrT[0:16, c0:c0 + n], in_=ps[0:16, 0:n], func=AF.Copy),
                 reads=[pk], writes=[("lrT", bi)])
        def la_gen():
          for bi, (c0, n, kind) in enumerate(blocks):
            tl = tiles_of(c0, n)
            pb = [SM(), SM()]
            tri = tri2 if kind == "P" else tri4
            for ti, t in enumerate(tl):
                ps, pk = MM()
                P.op("pe", lambda e, ps=ps, t=t: e.matmul(ps[:, 0:256], lhsT=lrT[0:17, t * 128:(t + 1) * 128],
                                                        rhs=wlr[0:17, l, :], start=True, stop=True),
                     reads=[("lrT", bi), "lrT", "wlr"], writes=[pk])
                e1, e1k = SC()
                P.op("act", lambda e, ps=ps, e1=e1: e.activation(out=e1[:, 0:256], in_=ps[:, 0:256], func=AF.Exp, scale=-1.0),
                     reads=[pk], writes=[e1k])
                e2, e2k = SC()
                P.op("act", lambda e, e1=e1, e2=e2: e.activation(out=e2[:, 0:256], in_=e1[:, 0:256], func=AF.Ln, bias=onec),
                     reads=[e1k, "kcol"], writes=[e2k])
                for pair in range(2):
                    P.op("pe", lambda e, e2=e2, pair=pair, ti=ti, tri=tri: e.matmul(
                        pb[pair][0][:, ti * 128:(ti + 1) * 128], lhsT=e2[:, pair * 128:(pair + 1) * 128], rhs=tri,
                        start=True, stop=True),
                        reads=[e2k, "consts"], writes=[pb[pair][1]])
                yield
            cs = 64 if kind == "P" else 32
            ch0 = c0 // 64 if kind == "P" else nchp
            nch = n // cs
            for pair in range(2):
                pbt, pbk = pb[pair]
                P.op("act", lambda e, pbt=pbt, pair=pair: e.activation(out=eb[:, pair, c0:c0 + n], in_=pbt[:, 0:n],
                                                                      func=AF.Exp, bias=lnq),
                     reads=[pbk, "kcol", ARENA], writes=[("eb", bi)])
                P.op("act", lambda e, pbt=pbt, pair=pair: e.activation(out=enb[:, pair, c0:c0 + n], in_=pbt[:, 0:n],
                                                                      func=AF.Exp, scale=-1.0),
                     reads=[pbk, ARENA], writes=[("enb", bi)])
                P.op("act", lambda e, pbt=pbt, pair=pair: e.activation(out=eb_last[:, pair, ch0:ch0 + nch],
                                                                      in_=pbt[:, cs - 1:n:cs], func=AF.Exp),
                     reads=[pbk], writes=[("ebl", bi)])
            yield

        lag = la_gen()
        bgref[0] = lag
        bgn[0] = 1
        wv, wk = wload(w_cols(w_in_d, l, OFF_XR, 512), (8, 512))
        for m in range(4):
            for bi, blk in enumerate(blocks):
                c0, n, kind = blk

                def ev(ps, pk, m=m, c0=c0, n=n, bi=bi, kind=kind):
                    if kind == "P":
                        P.op("dve", lambda e: e.tensor_copy(out=xr[:, m, 3 + c0:3 + c0 + n], in_=ps[:, 0:n]),
                             reads=[pk], writes=[("xr", m, bi)])
                    else:
                        P.op("act", lambda e: e.activation(out=xr_s[:, m, :, 3:35],
                                                           in_=ps[:, 0:128].rearrange("p (s t) -> p s t", s=4),
                                                           func=AF.Copy),
                             reads=[pk], writes=[("xr", m, bi)])
                gemm_fm(wv, wk, m * 128, 8, lambda k, c0=c0, n=n: hT[:, k, c0:c0 + n], lambda k, bi=bi: [("h", bi, k)], blk, ev)
        wv, wk = wload(w_cols(w_in_d, l, OFF_GR, 512), (8, 512))
        for m in range(4):
            for bi, blk in enumerate(blocks):
                c0, n, kind = blk

                def ev(ps, pk, m=m, c0=c0, n=n, bi=bi):
                    a1, a1k = SC()
                    P.op("act", lambda e: e.activation(out=a1[:, 0:n], in_=ps[:, 0:n], func=AF.Square),
                         reads=[pk], writes=[a1k])
                    a2, a2k = SC()
                    P.op("dve", lambda e: e.scalar_tensor_tensor(out=a2[:, 0:n], in0=a1[:, 0:n], scalar=1.0 / 0.044715,
                                                                 in1=ps[:, 0:n], op0=ALU.add, op1=ALU.mult),
                         reads=[pk, a1k], writes=[a2k])
                    P.op("act", lambda e: e.activation(out=a1[:, 0:n], in_=a2[:, 0:n], func=AF.Tanh,
                                                       scale=0.7978845608028654 * 0.044715),
                         reads=[a2k], writes=[a1k])
                    P.op("dve", lambda e: e.scalar_tensor_tensor(out=ggr[:, m, c0:c0 + n], in0=a1[:, 0:n], scalar=1.0,
                                                                 in1=ps[:, 0:n], op0=ALU.add, op1=ALU.mult),
                         reads=[pk, a1k, ARENA], writes=[("ggr", m, bi)])
                gemm_fm(wv, wk, m * 128, 8, lambda k, c0=c0, n=n: hT[:, k, c0:c0 + n], lambda k, bi=bi: [("h", bi, k)], blk, ev)
        for _ in lag:
            pass
        bgref[0] = bg
        bgn[0] = 4
        wv, wk = wload(w_cols(w_in_d, l, OFF_V, 512), (8, 512))
        for t in range(ntile):
            bi = [i for i, b in enumerate(blocks) if t in tiles_of(b[0], b[1])][0]
            ps, pk = MM()
            for k in range(8):
                P.op("pe", lambda e, k=k, ps=ps, t=t: e.matmul(ps[:, 0:512], lhsT=hT[:, k, t * 128:(t + 1) * 128],
                                                             rhs=wv[:, k, :], start=(k == 0), stop=(k == 7)),
                     reads=[wk, ("h", bi, k)], writes=[pk])
            P.op("act", lambda e, ps=ps, t=t: e.activation(out=v_tm[:, t, :], in_=ps[:, 0:512], func=AF.Copy),
                 reads=[pk, ARENA], writes=[("v", t)])
            bgstep(bgref[0], bgn[0])
        wv, wk = wload(w_cols(w_in_d, l, OFF_G, 512), (8, 512))
        for m in range(4):
            for bi, blk in enumerate(blocks):
                c0, n, kind = blk

                def ev(ps, pk, m=m, c0=c0, n=n, bi=bi):
                    th, thk = SC()
                    P.op("act", lambda e: e.activation(out=th[:, 0:n], in_=ps[:, 0:n], func=AF.Tanh, scale=0.5),
                         reads=[pk], writes=[thk])
                    P.op("dve", lambda e: e.scalar_tensor_tensor(out=sg[:, m, c0:c0 + n], in0=th[:, 0:n], scalar=1.0,
                                                                 in1=ps[:, 0:n], op0=ALU.add, op1=ALU.mult),
                         reads=[pk, thk, ARENA], writes=[("sg", m, bi)])
                gemm_fm(wv, wk, m * 128, 8, lambda k, c0=c0, n=n: hT[:, k, c0:c0 + n], lambda k, bi=bi: [("h", bi, k)], blk, ev)
        wv, wk = wload(w_cols(w_in_d, l, OFF_Q, 512), (8, 512))
        for m in range(4):
            for bi, blk in enumerate(blocks):
                c0, n, kind = blk

                def ev(ps, pk, m=m, c0=c0, n=n, bi=bi):
                    src = eb if m < 2 else enb
                    sk_ = ("eb", bi) if m < 2 else ("enb", bi)
                    P.op("dve", lambda e: e.tensor_tensor(out=qk[:, m, c0:c0 + n], in0=ps[:, 0:n],
                                                          in1=src[:, m % 2, c0:c0 + n], op=ALU.mult),
                         reads=[pk, sk_, ARENA], writes=[("qk", m, bi)])
                gemm_fm(wv, wk, m * 128, 8, lambda k, c0=c0, n=n: hT[:, k, c0:c0 + n], lambda k, bi=bi: [("h", bi, k)], blk, ev)

        bgref[0] = None

    def gla(g, l, bg=None):
        p0, npr, ns, blocks, ntile = geom(g)
        nchp = npr // 64
        last_prompt_group = (p0 + npr == SEQ)

        def bi_of(t):
            return [i for i, b in enumerate(blocks) if t in tiles_of(b[0], b[1])][0]

        qkk = lambda m, t: ("qk", m, bi_of(t))
        sslot = {}
        nsl = [0]

        def new_sslot(ci):
            s = nsl[0] % NSB
            nsl[0] += 1
            sslot[ci] = s
            return s

        def sz_write(s, src, srck):
            for h2 in range(2):
                P.op("pool", lambda e, s=s, h2=h2: e.tensor_copy(out=Sz[h2 * 64:(h2 + 1) * 64, s, h2::2, :],
                                                                in_=src[h2 * 64:(h2 + 1) * 64, :, :]),
                     reads=[srck, ARENA], writes=[("Sz", s, h2)])

        P.op("pool", lambda e: e.memset(Sz[:, :, :, :].rearrange("p t h e -> p (t h e)"), 0.0),
             reads=[ARENA], writes=[("Sz", s_, h2_) for s_ in range(NSB) for h2_ in range(2)])
        if ns:
            with nc.allow_non_contiguous_dma(reason="state load"):
                P.dma("sp", lambda e: e.dma_start(
                    out=S0[:, :, :, :].rearrange("p s a e -> p (s a) e"),
                    in_=sgla_d[l].rearrange("s (a h) d e -> (h d) (s a) e", h=2)), "S0", writes=["S0"])
        tinfo = {}

        tinfo_a = {}

        def pass1a(t):
            sample = (t * 128 >= npr)
            cs = 32 if sample else 64
            ncin = 128 // cs
            ci0 = nchp + 0 if sample else t * 2
            tc0 = t * 128
            bi = bi_of(t)
            ps, pk = SM()
            psb = ps[:, :].bitcast(BF16)
            for pair in range(2):
                P.op("pe", lambda e, psb=psb, pair=pair, tc0=tc0: e.transpose(
                    out=psb[:, pair * 128:(pair + 1) * 128], in_=qk[:, 2 + pair, tc0:tc0 + 128], identity=ident_bf[:, :]),
                    reads=[qkk(2 + pair, t), "ident_bf", ARENA], writes=[pk])
            if not sample:
                P.op("act", lambda e, psb=psb, t=t: e.activation(out=k_tm[:, t, :], in_=psb[:, 0:256], func=AF.Copy),
                     reads=[pk, ARENA], writes=[("k_tm", t)])
            else:
                for s in range(4):
                    P.op("dve", lambda e, psb=psb, s=s: e.tensor_scalar(out=k_tm_s[:, s, :], in0=psb[:, 0:256],
                                                                      scalar1=seqmask[:, s:s + 1], scalar2=None,
                                                                      op0=ALU.mult),
                         reads=[pk, "consts", ARENA], writes=[("k_tm_s", s)])
            tinfo_a[t] = True

        def pass1b(t):
            sample = (t * 128 >= npr)
            cs = 32 if sample else 64
            ncin = 128 // cs
            ci0 = nchp + 0 if sample else t * 2
            tc0 = t * 128
            bi = bi_of(t)
            asl = t % 3
            msk = mask4 if sample else mask2
            for h2 in range(2):
                ps, pk = SM()
                for pair in range(2):
                    P.op("pe", lambda e, ps=ps, pair=pair, h2=h2, tc0=tc0: e.matmul(
                        ps[:, pair * 128:(pair + 1) * 128], lhsT=qk[h2 * 64:(h2 + 1) * 64, 2 + pair, tc0:tc0 + 128],
                        rhs=qk[h2 * 64:(h2 + 1) * 64, pair, tc0:tc0 + 128], start=True, stop=True),
                        reads=[qkk(2 + pair, t), qkk(pair, t), ARENA], writes=[pk])
                P.op("dve", lambda e, ps=ps, asl=asl, msk=msk, h2=h2: e.tensor_tensor(
                    out=att[:, asl, :].rearrange("p (h i) -> p h i", h=4)[:, h2::2, :],
                    in0=ps[:, 0:256].rearrange("p (h i) -> p h i", h=2),
                    in1=msk.unsqueeze(1).to_broadcast([128, 2, 128]), op=ALU.mult),
                    reads=[pk, "consts", ARENA], writes=[("att", asl, h2)])
            for cc in range(ncin):
                ci = ci0 + cc
                ps, pk = SM()
                for pair in range(2):
                    if not sample:
                        r0 = cc * 64
                        P.op("pe", lambda e, ps=ps, pair=pair, r0=r0, t=t: e.matmul(
                            ps[:, pair * 256:(pair + 1) * 256], lhsT=k_tm[r0:r0 + 64, t, pair * 128:(pair + 1) * 128],
                            rhs=v_tm[r0:r0 + 64, t, pair * 256:(pair + 1) * 256], start=True, stop=True),
                            reads=[("k_tm", t), ("v", t), ARENA], writes=[pk])
                    else:
                        P.op("pe", lambda e, ps=ps, pair=pair, cc=cc, t=t: e.matmul(
                            ps[:, pair * 256:(pair + 1) * 256], lhsT=k_tm_s[:, cc, pair * 128:(pair + 1) * 128],
                            rhs=v_tm[:, t, pair * 256:(pair + 1) * 256], start=True, stop=True),
                            reads=[("k_tm_s", cc), ("v", t), ARENA], writes=[pk])
                dsl = ci % NDS
                for h2 in range(2):
                    P.op("dve", lambda e, ps=ps, h2=h2, dsl=dsl, ci=ci: e.tensor_tensor(
                        out=dSp[h2 * 64:(h2 + 1) * 64, dsl, :, :],
                        in0=ps[:, :].rearrange("p (a b e) -> p a b e", a=2, b=2)[h2 * 64:(h2 + 1) * 64, :, h2, :],
                        in1=eb_last[h2 * 64:(h2 + 1) * 64, :, ci:ci + 1].to_broadcast([64, 2, 128]), op=ALU.mult),
                        reads=[pk, ("ebl", bi)], writes=[("dSp", dsl, h2)])
                if not sample:
                    if ci == 0:
                        s = new_sslot(0)
                        sz_write(s, S_state[:, l, :, :], ("S_state", l))
                    src = S_state[:, l, :, :] if ci == 0 else Sf[:, (ci - 1) % 2, :, :]
                    srck = ("S_state", l) if ci == 0 else ("Sf", (ci - 1) % 2)
                    lastc = (ci == nchp - 1)
                    dst = S_state[:, l, :, :] if lastc else Sf[:, ci % 2, :, :]
                    dstk = ("S_state", l) if lastc else ("Sf", ci % 2)
                    if lastc and ci == 0:
                        raise AssertionError
                    for pair in range(2):
                        P.op("dve", lambda e, src=src, dst=dst, pair=pair, ci=ci, dsl=dsl: e.scalar_tensor_tensor(
                            out=dst[:, pair, :], in0=src[:, pair, :], scalar=eb_last[:, pair, ci:ci + 1],
                            in1=dSp[:, dsl, pair, :], op0=ALU.mult, op1=ALU.add),
                            reads=[srck, ("ebl", bi), ("dSp", dsl, 0), ("dSp", dsl, 1)], writes=[dstk])
                    if not lastc:
                        s = new_sslot(ci + 1)
                        sz_write(s, dst, dstk)
                    elif last_prompt_group:
                        with nc.allow_non_contiguous_dma(reason="state store"):
                            P.dma("sp", lambda e: e.dma_start(
                                out=glap_d[l].rearrange("(a h) d e -> (h d) a e", h=2), in_=S_state[:, l, :, :]),
                                ("glap", l), reads=[("S_state", l)])
                else:
                    s = new_sslot(ci)
                    sz_write(s, S0[:, cc, :, :], "S0")
                    for pair in range(2):
                        P.op("dve", lambda e, pair=pair, ci=ci, cc=cc, dsl=dsl: e.scalar_tensor_tensor(
                            out=Sf[:, cc % 2, pair, :], in0=S0[:, cc, pair, :], scalar=eb_last[:, pair, ci:ci + 1],
                            in1=dSp[:, dsl, pair, :], op0=ALU.mult, op1=ALU.add),
                            reads=["S0", ("ebl", bi), ("dSp", dsl, 0), ("dSp", dsl, 1)], writes=[("Sf", cc % 2)])
                    with nc.allow_non_contiguous_dma(reason="state store"):
                        P.dma("sp", lambda e, cc=cc: e.dma_start(
                            out=glas_d[l, cc].rearrange("(a h) d e -> (h d) a e", h=2), in_=Sf[:, cc % 2, :, :]),
                            ("glas", cc % 2), reads=[("Sf", cc % 2)])
            tinfo[t] = (sample, cs, ncin, ci0, tc0, bi, asl)

        def pass3(t):
            sample, cs, ncin, ci0, tc0, bi, asl = tinfo[t]
            po, pok = MM()
            for h in range(4):
                pair, h2 = h // 2, h % 2
                P.op("pe", lambda e, po=po, h=h, t=t, asl=asl: e.matmul(
                    po[:, h * 128:(h + 1) * 128], lhsT=v_tm[:, t, h * 128:(h + 1) * 128],
                    rhs=att[:, asl, h * 128:(h + 1) * 128], start=True, stop=False),
                    reads=[("v", t), ("att", asl, h2), ARENA], writes=[pok])
                for cc in range(ncin):
                    ci = ci0 + cc
                    s = sslot[ci]
                    P.op("pe", lambda e, po=po, h=h, pair=pair, h2=h2, cc=cc, s=s, tc0=tc0, cs=cs: e.matmul(
                        po[:, h * 128 + cc * cs:h * 128 + (cc + 1) * cs],
                        lhsT=Sz[:, s, h, :],
                        rhs=qk[:, pair, tc0 + cc * cs:tc0 + (cc + 1) * cs],
                        start=False, stop=(cc == ncin - 1)),
                        reads=[("Sz", s, 0), ("Sz", s, 1), qkk(pair, t), ARENA], writes=[pok])
            sq, sqk = SB()
            P.op("act", lambda e, po=po, sq=sq: e.activation(out=sq[:, :], in_=po[:, :], func=AF.Square),
                 reads=[pok], writes=[sqk])
            ss, sk = MM()
            P.op("pe", lambda e, ss=ss, sq=sq: e.matmul(ss[:, :], lhsT=ones_bf[:, :], rhs=sq[:, :], start=True, stop=True),
                 reads=[sqk, "ones_bf"], writes=[sk])
            rstd_inplace(ss, sk, 512, 1.0 / 128)
            t1, t1k = SC()
            P.op("dve", lambda e, po=po, t1=t1, tc0=tc0: e.tensor_tensor(
                out=t1[:, :].rearrange("p (h i) -> p h i", h=4), in0=po[:, :].rearrange("p (h i) -> p h i", h=4),
                in1=sg[:, :, tc0:tc0 + 128], op=ALU.mult),
                reads=[pok, sqk, ARENA] + [("sg", m, bi) for m in range(4)], writes=[t1k])
            P.op("dve", lambda e, t1=t1, ss=ss, tc0=tc0: e.scalar_tensor_tensor(
                out=om[:, 0:4, tc0:tc0 + 128], in0=t1[:, :].rearrange("p (h i) -> p h i", h=4),
                scalar=dv[:, l, 16:17], in1=ss[:, :].rearrange("p (h i) -> p h i", h=4), op0=ALU.mult, op1=ALU.mult),
                reads=[t1k, sk, "dv3"], writes=[("om", 0, t)])

        LAG = 2
        done3 = -1
        done1b = -1

        def flush3(upto):
            nonlocal done3
            while done3 < upto:
                done3 += 1
                pass3(done3)
                bgstep(bg, 6)

        for step in range(ntile + 1):
            if step < ntile:
                pass1a(step)
            t = step - 1
            if t >= 0:
                if t * 128 >= npr:
                    flush3(t - 1)
                pass1b(t)
                bgstep(bg, 6)
                flush3(t - LAG)
        flush3(ntile - 1)

    def bgstep(bg, n):
        if bg is None:
            return
        for _ in range(n):
            try:
                next(bg)
            except StopIteration:
                return

    def rglru(g, l):
        p0, npr, ns, blocks, ntile = geom(g)
        last_prompt_group = (p0 + npr == SEQ)
        P.op("dve", lambda e: e.tensor_copy(out=xr[:, :, 0:3], in_=convst[:, l, :, :]),
             reads=[("convst", l)], writes=["xrh"])
        if ns:
            with nc.allow_non_contiguous_dma(reason="conv state load"):
                for s in range(4):
                    for m_ in range(4):
                        P.dma("sp", lambda e, s=s, m_=m_: e.dma_start(
                            out=xr_s[:, m_, s, 0:3], in_=scv_d[l, s][:, m_ * 128:(m_ + 1) * 128].rearrange("j p -> p j")),
                            ("scv", m_), writes=[("xrsh", s)])
        def rg_iter(m, seg, slot):
            c0, n, kind, bis = seg
            cw = lambda j: pvec[:, l, C_CW + m * 4 + j:C_CW + m * 4 + j + 1]
            xk = [("xr", m, b_) for b_ in bis] + (["xrh"] if kind == "P" else [("xrsh", s_) for s_ in range(4)])
            ggk = [("ggr", m, b_) for b_ in bis]
            omk = [("om", 4 + m, ("b", b_)) for b_ in bis]

            def X(j):
                if kind == "P":
                    return xr[:, m, c0 + j:c0 + j + n]
                return xr_s[:, m, :, j:j + 32]

            def V(buf):
                if kind == "P":
                    return buf[:, 0:n]
                return buf[:, 0:128].rearrange("p (s t) -> p s t", s=4)
            bufs = [(resbuf[:, 4 * slot + q, :], [("res", 4 * slot + q, 0), ("res", 4 * slot + q, 1)]) for q in range(4)]
            (A, Ak), (B, Bk), (C, Ck), (Dd, Dk) = bufs
            xcb, xcbk = xcbw[slot], [("rg", "xcb", slot)]
            pg, pgk = ssb2, [("ssb", 0), ("ssb", 1)]
            pieces = [(0, min(n, 512))] + ([(512, n - 512)] if n > 512 else [])
            P.op("act", lambda e: e.activation(out=V(A), in_=X(3), func=AF.Identity,
                                               bias=pvec[:, l, C_CB + m:C_CB + m + 1], scale=cw(3)),
                 reads=xk + ["pvec"], writes=Ak)
            yield
            P.op("dve", lambda e: e.scalar_tensor_tensor(out=V(B), in0=X(2), scalar=cw(2), in1=V(A),
                                                         op0=ALU.mult, op1=ALU.add),
                 reads=xk + Ak + ["pvec"], writes=Bk)
            yield
            P.op("dve", lambda e: e.scalar_tensor_tensor(out=V(A), in0=X(1), scalar=cw(1), in1=V(B),
                                                         op0=ALU.mult, op1=ALU.add),
                 reads=xk + Bk + ["pvec"], writes=Ak)
            yield
            P.op("dve", lambda e: e.scalar_tensor_tensor(out=V(B), in0=X(0), scalar=cw(0), in1=V(A),
                                                         op0=ALU.mult, op1=ALU.add),
                 reads=xk + Ak + ["pvec"], writes=Bk)
            yield
            P.op("pool", lambda e: e.tensor_copy(out=xcb[:, 0:n], in_=B[:, 0:n]),
                 reads=Bk, writes=xcbk)
            yield
            for gi, (dst, dstk, bcol) in enumerate(((A, Ak, 0), (C, Ck, 4))):
                for (pc, pw) in pieces:
                    P.op("pe", lambda e, pc=pc, pw=pw, gi=gi: e.matmul(
                        pg[:, pc:pc + pw], lhsT=wbd[:, (l * 2 + gi) * 4 + m, :], rhs=xcb[:, pc:pc + pw],
                        start=True, stop=True), reads=xcbk + ["wbd"], writes=pgk)
                P.op("act", lambda e, dst=dst, bcol=bcol: e.activation(out=dst[:, 0:n], in_=pg[:, 0:n], func=AF.Tanh,
                                                                      scale=0.5, bias=dv[:, l, bcol + m:bcol + m + 1]),
                     reads=pgk + ["dv0"], writes=dstk)
                yield
            P.op("act", lambda e: e.activation(out=Dd[:, 0:n], in_=A[:, 0:n], func=AF.Exp,
                                               scale=dv[:, l, 8 + m:9 + m], bias=dv[:, l, 8 + m:9 + m]),
                 reads=Ak + ["dv1"], writes=Dk)
            yield
            P.op("act", lambda e: e.activation(out=A[:, 0:n], in_=A[:, 0:n], func=AF.Exp,
                                               scale=dv[:, l, 12 + m:13 + m], bias=dv[:, l, 12 + m:13 + m]),
                 reads=Ak + ["dv2"], writes=Ak)
            yield
            P.op("dve", lambda e: e.tensor_scalar(out=A[:, 0:n], in0=A[:, 0:n], scalar1=0.9999999, scalar2=None,
                                                  op0=ALU.min), reads=Ak, writes=Ak)
            yield
            P.op("act", lambda e: e.activation(out=A[:, 0:n], in_=A[:, 0:n], func=AF.Ln, scale=-1.0, bias=onec),
                 reads=Ak + ["kcol"], writes=Ak)
            yield
            P.op("act", lambda e: e.activation(out=A[:, 0:n], in_=A[:, 0:n], func=AF.Exp, scale=0.5),
                 reads=Ak, writes=Ak)
            yield
            P.op("dve", lambda e: e.scalar_tensor_tensor(out=C[:, 0:n], in0=C[:, 0:n], scalar=1.0,
                                                         in1=B[:, 0:n], op0=ALU.add, op1=ALU.mult),
                 reads=Ck + Bk, writes=Ck)
            yield
            P.op("dve", lambda e: e.scalar_tensor_tensor(out=C[:, 0:n], in0=C[:, 0:n], scalar=0.5,
                                                         in1=A[:, 0:n], op0=ALU.mult, op1=ALU.mult),
                 reads=Ck + Ak, writes=Ck)
            yield
            if kind == "P":
                P.op("dve", lambda e: e.tensor_tensor_scan(
                    out=B[:, 0:n], data0=Dd[:, 0:n], data1=C[:, 0:n], initial=hst[:, l, m:m + 1],
                    op0=ALU.mult, op1=ALU.add),
                    reads=Dk + Ck + [("hst", l, m)], writes=Bk)
                yield
                P.op("dve", lambda e: e.tensor_copy(out=hst[:, l, m:m + 1], in_=B[:, n - 1:n]),
                     reads=Bk, writes=[("hst", l, m)])
                yield
            else:
                for s in range(4):
                    P.op("dve", lambda e, s=s: e.tensor_tensor_scan(
                        out=B[:, s * 32:(s + 1) * 32], data0=Dd[:, s * 32:(s + 1) * 32],
                        data1=C[:, s * 32:(s + 1) * 32], initial=h0s[:, l, s, m:m + 1],
                        op0=ALU.mult, op1=ALU.add),
                        reads=Dk + Ck + ["h0s"], writes=Bk)
                    yield
                P.op("act", lambda e: e.activation(out=hso[:, :, m:m + 1],
                                                   in_=B[:, 31:128:32].unsqueeze(2), func=AF.Copy),
                     reads=Bk, writes=[("hso", m)])
                yield
            P.op("dve", lambda e: e.scalar_tensor_tensor(out=om[:, 4 + m, c0:c0 + n], in0=B[:, 0:n], scalar=0.5,
                                                         in1=ggr[:, m, c0:c0 + n], op0=ALU.mult, op1=ALU.mult),
                 reads=Bk + ggk + [ARENA], writes=omk)
            yield

        segs = [(0, npr, "P", [b_ for b_, bl in enumerate(blocks) if bl[2] == "P"])]
        if ns:
            segs.append((npr, ns, "S", [len(blocks) - 1]))
        for seg in segs:
            for m0 in (0, 2):
                gens = [rg_iter(m0, seg, 0), rg_iter(m0 + 1, seg, 1)]
                live = [True, True]
                while any(live):
                    for qi in range(2):
                        if live[qi]:
                            try:
                                next(gens[qi])
                            except StopIteration:
                                live[qi] = False
                    yield
        P.op("dve", lambda e: e.tensor_copy(out=convst[:, l, :, :], in_=xr[:, :, npr:npr + 3]),
             reads=[("xr", m, bi) for m in range(4) for bi in range(len(blocks))] + ["xrh"], writes=[("convst", l)])
        with nc.allow_non_contiguous_dma(reason="small state stores"):
            if last_prompt_group:
                for m_ in range(4):
                    P.dma("sp", lambda e, m_=m_: e.dma_start(
                        out=cvp_d[l][:, m_ * 128:(m_ + 1) * 128].rearrange("j p -> p j"), in_=convst[:, l, m_, :]),
                        ("cvp", m_), reads=[("convst", l)])
                P.dma("sp", lambda e: e.dma_start(out=rgp_d[l].rearrange("(m p) -> p m", p=128), in_=hst[:, l, :]),
                      ("rgp", l), reads=[("hst", l, m) for m in range(4)])
            if ns:
                bi_s = len(blocks) - 1
                for s in range(4):
                    for m_ in range(4):
                        P.dma("sp", lambda e, s=s, m_=m_: e.dma_start(
                            out=cvs_d[l, s][:, m_ * 128:(m_ + 1) * 128].rearrange("j p -> p j"),
                            in_=xr_s[:, m_, s, 32:35]), ("cvs", m_), reads=[("xr", m_, bi_s)])
                    P.dma("sp", lambda e, s=s: e.dma_start(out=rgs_d[l, s].rearrange("(m p) -> p m", p=128),
                                                           in_=hso[:, s, :]),
                          ("rgs", s), reads=[("hso", m) for m in range(4)])

    def res_evac(ps, pk, m, bi, c0, n):
        sq, sqk = SB()
        P.op("act", lambda e: e.activation(out=sq[:, 0:n], in_=ps[:, 0:n], func=AF.Square), reads=[pk], writes=[sqk])
        P.op("dve", lambda e: e.tensor_copy(out=resbuf[:, m, c0:c0 + n], in_=ps[:, 0:n]),
             reads=[pk, sqk], writes=[("res", m, bi)])
        ss, sk = ssb[bi % 2], ("ssb", bi % 2)
        pending.append(lambda: P.op(
            "pe", lambda e: e.matmul(ss[:, 0:n], lhsT=ones_bf[:, :], rhs=sq[:, 0:n], start=(m == 0), stop=(m == 7)),
            reads=[sqk, "ones_bf"], writes=[sk]))

    def flush_pending():
        while pending:
            pending.pop(0)()

    def res_apply(g, l, gcol, want_next=True, only=None):
        flush_pending()
        p0, npr, ns, blocks, ntile = geom(g)
        for bi, (c0, n, kind) in enumerate(blocks):
            if only is not None and bi != only:
                continue
            ss, sk = ssb[bi % 2], ("ssb", bi % 2)
            xk = lambda m: [("x", t, m) for t in tiles_of(c0, n)]
            rstd_inplace(ss, sk, n, 1.0 / D)
            if want_next:
                ns_, nsk = SM()
                pre_ss[bi] = (ns_, nsk)
            for m in range(8):
                tt, ttk = SC()
                P.op("dve", lambda e, m=m, tt=tt, ss=ss: e.scalar_tensor_tensor(
                    out=tt[:, 0:n], in0=resbuf[:, m, c0:c0 + n], scalar=pvec[:, l, gcol + m:gcol + m + 1],
                    in1=ss[:, 0:n], op0=ALU.mult, op1=ALU.mult),
                    reads=[("res", m, bi), sk, "pvec"], writes=[ttk])
                P.op("pool", lambda e, m=m, tt=tt: e.tensor_tensor(out=xT[:, m, c0:c0 + n], in0=xT[:, m, c0:c0 + n],
                                                                  in1=tt[:, 0:n], op=ALU.add),
                     reads=[ttk] + xk(m), writes=xk(m))
                if want_next:
                    sq, sqk = SB()
                    P.op("act", lambda e, m=m, sq=sq: e.activation(out=sq[:, 0:n], in_=xT[:, m, c0:c0 + n], func=AF.Square),
                         reads=xk(m), writes=[sqk])
                    P.op("pe", lambda e, m=m, sq=sq, ns_=ns_: e.matmul(ns_[:, 0:n], lhsT=ones_bf[:, :], rhs=sq[:, 0:n],
                                                                    start=(m == 0), stop=(m == 7)),
                         reads=[sqk, "ones_bf"], writes=[nsk])

    def phase_wout(g, l):
        p0, npr, ns, blocks, ntile = geom(g)
        wvs = [wload(w_cols(w_out_d, l, ch * 512, 512), (8, 512)) for ch in range(2)]
        for bi, blk in enumerate(blocks):
            c0, n, kind = blk
            omk = [("om", 0, t) for t in tiles_of(c0, n)] + [("om", 4 + q, ("b", bi)) for q in range(4)]
            for m in range(8):
                wv, wk = wvs[m // 4]
                gemm_fm(wv, wk, (m % 4) * 128, 8, lambda k, c0=c0, n=n: om[:, k, c0:c0 + n], lambda k, omk=omk: omk, blk,
                        lambda ps, pk, m=m, bi=bi, c0=c0, n=n: res_evac(ps, pk, m, bi, c0, n))
        if l == layers - 1 and g + 1 < ngroups and nph[0] < stop:
            prefetch_x(g + 1, range(0, 3))
        flush_pending()

    def ff1_group(g, l, wv, wk, mm, m, bi, blk):
        c0, n, kind = blk

        def ev(ps, pk):
            rb, rbk = SB()
            P.op("act", lambda e: e.activation(out=rb[:, 0:n], in_=ps[:, 0:n], func=AF.Relu),
                 reads=[pk], writes=[rbk])
            P.op("dve", lambda e: e.tensor_tensor(out=hid[:, m, c0:c0 + n], in0=ps[:, 0:n], in1=rb[:, 0:n],
                                                  op=ALU.mult),
                 reads=[pk, rbk, ARENA], writes=[("hid", m, bi)])
        gemm_fm(wv, wk, mm * 128, 8, lambda k: hT[:, k, c0:c0 + n], lambda k: [("h", bi, k)], blk, ev)

    def phase_ffn(g, l):
        p0, npr, ns, blocks, ntile = geom(g)
        barrier()
        wv, wk = wload(w_cols(w_ff1_d, l, 0, 512), (8, 512))
        for bi, blk in enumerate(blocks):
            res_apply(g, l, C_GPOSTM, only=bi)
            norm_to_h(g, l, C_GPREF, only=bi)
            for mm in range(4):
                ff1_group(g, l, wv, wk, mm, mm, bi, blk)
        for ch in range(1, 8):
            wv, wk = wload(w_cols(w_ff1_d, l, ch * 512, 512), (8, 512))
            for mm in range(4):
                for bi, blk in enumerate(blocks):
                    ff1_group(g, l, wv, wk, mm, ch * 4 + mm, bi, blk)
        if l == layers - 1 and g + 1 < ngroups and nph[0] < stop:
            prefetch_x(g + 1, range(3, 6))
        for m in range(8):
            src = w_ff2_d[l, :, m * 128:(m + 1) * 128].rearrange("(k p) n -> p k n", p=128)
            wv, wk = wload(src, (32, 128))
            for bi, blk in enumerate(blocks):
                c0, n, kind = blk
                gemm_fm(wv, wk, 0, 32, lambda k, c0=c0, n=n: hid[:, k, c0:c0 + n],
                        lambda k, bi=bi: [("hid", k, bi), ARENA], blk,
                        lambda ps, pk, m=m, bi=bi, c0=c0, n=n: res_evac(ps, pk, m, bi, c0, n))
        for bi in range(len(blocks)):
            res_apply(g, l, C_GPOSTF, want_next=(l + 1 < layers), only=bi)
            if l + 1 < layers:
                norm_to_h(g, l + 1, C_GPRE, only=bi)
        barrier()

    def mixer(g, l):
        bg = rglru(g, l)
        phase_a(g, l, bg)
        gla(g, l, bg)
        for _ in bg:
            pass

    def go(fn, *a):
        if nph[0] < stop:
            fn(*a)
        nph[0] += 1

    go(load_store, None, 0)
    for g in range(ngroups):
        for l in range(layers):
            if l == 0:
                go(norm_to_h, g, l, C_GPRE)
            go(mixer, g, l)
            go(phase_wout, g, l)
            go(phase_ffn, g, l)
        go(load_store, g, g + 1 if g + 1 < ngroups else None)
    P.finish()
    P.emit_all()
    return nc, P


def _consts():
    c = np.zeros((128, CW), np.float32)
    j = np.arange(128)[:, None]
    i = np.arange(128)[None, :]
    c[:, K_ID:K_ID + 128] = (j == i)
    m2 = ((j // 64) == (i // 64)) & (j <= i)
    m4 = ((j // 32) == (i // 32)) & (j <= i)
    c[:, K_TRI2:K_TRI2 + 128] = m2 * (-1.0 / 16.0)
    c[:, K_TRI4:K_TRI4 + 128] = m4 * (-1.0 / 16.0)
    c[:, K_M2:K_M2 + 128] = m2
    c[:, K_M4:K_M4 + 128] = m4
    for s in range(4):
        c[s * 32:(s + 1) * 32, K_SEQ + s] = 1.0
    return c


_CACHE = {}


def kernel(x_prompt, x_sample, state_gla, state_rglru, state_conv, g_pre_mix, w_in, w_lr2, b_lr, gla_norm,
           conv_w, conv_b, rg_wa, rg_ba, rg_wx, rg_bx, rg_lambda, w_out, g_post_mix, g_pre_ff, w_ff1, w_ff2,
           g_post_ff):
    f = lambda a: np.ascontiguousarray(np.asarray(a, dtype=np.float32))
    x_prompt, x_sample = f(x_prompt), f(x_sample)
    state_gla, state_rglru, state_conv = f(state_gla), f(state_rglru), f(state_conv)
    pvec = np.zeros((128, NL, PC), np.float32)
    for l in range(NL):
        for col, v in ((C_GPRE, g_pre_mix), (C_GPOSTM, g_post_mix), (C_GPREF, g_pre_ff), (C_GPOSTF, g_post_ff)):
            pvec[:, l, col:col + 8] = f(v)[l].reshape(8, 128).T
        pvec[:, l, C_GN] = f(gla_norm)[l]
        pvec[:, l, C_CW:C_CW + 16] = f(conv_w)[l].reshape(4, 4, 128).transpose(2, 1, 0).reshape(128, 16)
        pvec[:, l, C_CB:C_CB + 4] = f(conv_b)[l].reshape(4, 128).T
        pvec[:, l, C_BA:C_BA + 4] = f(rg_ba)[l].reshape(4, 128).T
        pvec[:, l, C_BX:C_BX + 4] = f(rg_bx)[l].reshape(4, 128).T
        pvec[:, l, C_LAM:C_LAM + 4] = f(rg_lambda)[l].reshape(4, 128).T
    wlr = np.zeros((17, NL, 256), np.float32)
    wlr[:16] = f(w_lr2).transpose(1, 0, 2)
    wlr[16] = f(b_lr)
    wbd = np.zeros((128, NL * 2 * 4, 128), np.float32)
    for l in range(NL):
        for gi, w in enumerate((f(rg_wa), f(rg_wx))):
            for m in range(4):
                for hb in range(2):
                    blk = w[l, m * 2 + hb]
                    wbd[hb * 64:(hb + 1) * 64, (l * 2 + gi) * 4 + m, hb * 64:(hb + 1) * 64] = blk
    consts = _consts()
    w_in, w_out, w_ff1, w_ff2 = f(w_in), f(w_out), f(w_ff1), f(w_ff2)

    if "nc" not in _CACHE:
        _CACHE["nc"] = build_program()[0]
    nc = _CACHE["nc"]
    in_maps = []
    for c in range(8):
        xs = x_sample[4 * c:4 * c + 4].reshape(128, D)
        in_maps.append({
            "x": np.ascontiguousarray(np.concatenate([x_prompt[c], xs], axis=0)),
            "sgla": np.ascontiguousarray(state_gla[:, 4 * c:4 * c + 4]),
            "srg": np.ascontiguousarray(state_rglru[:, 4 * c:4 * c + 4]),
            "scv": np.ascontiguousarray(state_conv[:, 4 * c:4 * c + 4]),
            "w_in": w_in, "w_out": w_out, "w_ff1": w_ff1, "w_ff2": w_ff2,
            "pvec": pvec, "wlr": wlr, "wbd": wbd, "consts": consts,
        })
    res = run_bass_kernel_spmd(nc, in_maps, core_ids=list(range(8)))
    R = res.results
    y_prompt = np.stack([R[c]["y"][:SEQ] for c in range(8)]).astype(np.float32)
    y_sample = np.concatenate([R[c]["y"][SEQ:].reshape(4, 32, D) for c in range(8)], axis=0).astype(np.float32)
    gla_prompt = np.stack([R[c]["glap"] for c in range(8)], axis=1).astype(np.float32)
    rglru_prompt = np.stack([R[c]["rgp"] for c in range(8)], axis=1).astype(np.float32)
    conv_prompt = np.stack([R[c]["cvp"] for c in range(8)], axis=1).astype(np.float32)
    gla_sample = np.concatenate([R[c]["glas"] for c in range(8)], axis=1).astype(np.float32)
    rglru_sample = np.concatenate([R[c]["rgs"] for c in range(8)], axis=1).astype(np.float32)
    conv_sample = np.concatenate([R[c]["cvs"] for c in range(8)], axis=1).astype(np.float32)
    return (y_prompt, y_sample, gla_prompt, rglru_prompt, conv_prompt, gla_sample, rglru_sample, conv_sample)
```

```python
import contextlib
import math
import numpy as np
import concourse.bass as bass
import concourse.mybir as mybir
from concourse.bass_utils import run_bass_kernel_spmd

F32 = mybir.dt.float32
BF16 = mybir.dt.bfloat16
AF = mybir.ActivationFunctionType
ALU = mybir.AluOpType

ENGS = ("pe", "act", "dve", "pool", "sp")
SEM_L = 4000
_ACT_COMPAT = {"T": "EGS", "X": "EL", "L": "L", "G": "G", "S": "S"}
_ACT_HOME = {"T": "E", "X": "E", "L": "L", "G": "G", "S": "S"}


class _Op:
    __slots__ = ("eng", "emit", "deps", "idx", "sig", "signo", "dma_key", "dma_seq",
                 "clock", "waits", "is_dma", "name", "gseq", "dur", "nbytes", "aset", "cls", "prio")


class _Rec:
    def __init__(self):
        self.call = None

    def __getattr__(self, name):
        def f(*a, **k):
            self.call = (name, a, k)
            return None
        return f


class Prog:
    def __init__(self, nc):
        self.nc = nc
        self.streams = {e: [] for e in ENGS}
        self.ops = []
        self.last_write = {}
        self.readers = {}
        self.dma_count = {}
        self.dma_last = {}
        self.dma_ops = {}
        self.stack = contextlib.ExitStack()

    def sbuf(self, name, shape, dtype):
        return self.stack.enter_context(self.nc.sbuf_tensor("sb_" + name, list(shape), dtype))

    def psum(self, name, shape, dtype):
        return self.stack.enter_context(self.nc.psum_tensor("ps_" + name, list(shape), dtype))

    def _mk(self, eng, emit, reads, writes, name=None):
        o = _Op()
        o.eng = eng
        if emit is not None:
            rec = _Rec()
            emit(rec)
            mname, a, k = rec.call
            emit = (lambda engh, mname=mname, a=a, k=k: getattr(engh, mname)(*a, **k))
            o.dur, o.nbytes = self._est(eng, mname, a, k)
            o.aset = None
            if mname == "activation":
                fn = k.get("func")
                if fn == AF.Tanh:
                    o.aset = "T"
                elif fn == AF.Ln:
                    o.aset = "L"
                elif fn == AF.Exp:
                    o.aset = "X"
                elif fn == AF.Gelu_apprx_tanh:
                    o.aset = "G"
                elif fn == AF.Silu:
                    o.aset = "S"
        else:
            o.dur, o.nbytes = 0.0, 0
            o.aset = None
        o.emit = emit
        o.name = name
        o.is_dma = False
        o.dma_key = None
        o.dma_seq = 0
        o.sig = False
        o.signo = 0
        o.prio = None
        deps = []
        for r in reads:
            w = self.last_write.get(r)
            if w is not None:
                deps.append(w)
        for w_ in writes:
            w = self.last_write.get(w_)
            if w is not None:
                deps.append(w)
            deps.extend(self.readers.get(w_, ()))
        for r in reads:
            self.readers.setdefault(r, []).append(o)
        for w_ in writes:
            self.last_write[w_] = o
            self.readers[w_] = []
        o.deps = deps
        o.idx = len(self.streams[eng])
        o.gseq = len(self.ops)
        self.streams[eng].append(o)
        self.ops.append(o)
        return o

    @staticmethod
    def _est(eng, mname, a, k):
        out = k.get("out", a[0] if a else None)
        n = 1
        try:
            for d in out.shape[1:]:
                n *= d
        except Exception:
            n = 512
        if mname == "dma_start":
            return 0.0, n * 4 * 128
        if eng == "pe":
            lh = k.get("lhsT", k.get("in_"))
            f = 4.0 if (mname == "matmul" and lh is not None and lh.dtype == F32) else 1.0
            return max(max(64.0, n) / 2.4 * f + 10.0, 105.0), 0
        wide = max(0, n - 512)
        if eng == "act":
            return 1.2 * (120.0 + 0.8 * min(n, 512) + 1.5 * wide), 0
        if eng == "dve":
            f = 2.0 if mname in ("tensor_tensor_scan",) else 1.0
            return 1.24 * (110.0 + f * (0.85 * min(n, 512) + 1.5 * wide)), 0
        if eng == "pool":
            return 1.23 * (150.0 + 1.6 * n), 0
        return 60.0, 0

    def schedule(self):
        ops = self.ops
        n = len(ops)
        succ = [[] for _ in range(n)]
        indeg = [0] * n
        for o in ops:
            ds = set(p.gseq for p in o.deps if p is not o)
            indeg[o.gseq] = len(ds)
            for d in ds:
                succ[d].append(o.gseq)
        for o in ops:
            o.cls = o.prio if o.prio is not None else (0 if any((p.eng == "pe" and o.eng != "pe") for p in o.deps) else 1)
        cur_set = [None]
        ready_time = [0.0] * n
        eng_free = {e: 0.0 for e in ENGS}
        ready = {e: [] for e in ENGS}
        for o in ops:
            if indeg[o.gseq] == 0:
                ready[o.eng].append(o.gseq)
        order = []
        SYNC = 150.0
        while len(order) < n:
            best = None
            for e in ENGS:
                rl = ready[e]
                if not rl:
                    continue
                tfree = eng_free[e]
                cg = None
                ckey = None
                for g_ in rl:
                    rt = ready_time[g_]
                    og = ops[g_]
                    sw = 0
                    if e == "act" and og.aset is not None and cur_set[0] is not None \
                            and cur_set[0] not in _ACT_COMPAT[og.aset]:
                        sw = 1
                    key = (0.0, sw, og.cls, g_) if rt <= tfree else (rt - tfree, sw, og.cls, g_)
                    if ckey is None or key < ckey:
                        ckey, cg = key, g_
                start = max(tfree, ready_time[cg])
                if best is None or (start, cg) < (best[0], best[2]):
                    best = (start, e, cg)
            start, e, g_ = best
            o = ops[g_]
            ready[e].remove(g_)
            if o.is_dma:
                busy = 1400.0 if e == "pool" else 120.0
                fin = start + busy + 2200.0 + o.nbytes / 180.0
            else:
                busy = o.dur
                if e == "act" and o.aset is not None:
                    if cur_set[0] is None or cur_set[0] not in _ACT_COMPAT[o.aset]:
                        if cur_set[0] is not None:
                            busy += 1283.0
                        cur_set[0] = _ACT_HOME[o.aset]
                fin = start + busy
            eng_free[e] = start + busy
            order.append(o)
            for s_ in succ[g_]:
                rt = fin if (e == "pe" and ops[s_].eng == "pe") else fin + SYNC
                if ready_time[s_] < rt:
                    ready_time[s_] = rt
                indeg[s_] -= 1
                if indeg[s_] == 0:
                    ready[ops[s_].eng].append(s_)
        self.ops = order
        self.streams = {e: [] for e in ENGS}
        for i, o in enumerate(order):
            o.gseq = i
            o.idx = len(self.streams[o.eng])
            self.streams[o.eng].append(o)
        self.est_span = max(eng_free.values())

    def op(self, eng, emit, reads=(), writes=(), name=None):
        return self._mk(eng, emit, tuple(reads), tuple(writes), name)

    def dma(self, eng, emit, key, reads=(), writes=(), name=None):
        o = self._mk(eng, emit, tuple(reads), tuple(writes), name)
        o.is_dma = True
        o.dma_key = key
        prev = self.dma_last.get(key)
        if prev is not None:
            o.deps.append(prev)
        self.dma_count[key] = self.dma_count.get(key, 0) + 1
        o.dma_seq = self.dma_count[key]
        self.dma_last[key] = o
        self.dma_ops[(key, o.dma_seq)] = o
        return o

    def finish(self):
        o = self._mk("sp", None, (), (), "finish")
        o.deps = list(self.dma_last.values())
        return o

    def emit_all(self, sched=True):
        nc = self.nc
        if sched:
            self.schedule()
        last_clock = {e: {} for e in ENGS}
        for o in self.ops:
            clock = dict(last_clock[o.eng])
            waits = {}
            for p in o.deps:
                if p is o:
                    continue
                if p.is_dma:
                    chan = ("dma", p.dma_key)
                    need = p.dma_seq
                else:
                    if p.eng == "pe" and o.eng == "pe" and not o.is_dma:
                        continue
                    chan = p.eng
                    need = p.idx
                if clock.get(chan, -1) >= need:
                    continue
                if waits.get(chan, -1) < need:
                    waits[chan] = need
            for chan, need in waits.items():
                if isinstance(chan, tuple):
                    p = self.dma_ops[(chan[1], need)]
                else:
                    p = self.streams[chan][need]
                    p.sig = True
                if clock.get(chan, -1) < need:
                    clock[chan] = need
                for c, v in p.clock.items():
                    if clock.get(c, -1) < v:
                        clock[c] = v
            o.clock = clock
            o.waits = waits
            last_clock[o.eng] = clock

        eng_sems = {}
        for e in ENGS:
            n = 0
            for o in self.streams[e]:
                if o.sig:
                    n += 1
                    o.signo = n
            nsem = (n + SEM_L - 1) // SEM_L
            eng_sems[e] = [self.stack.enter_context(nc.semaphore(f"s_{e}_{i}")) for i in range(nsem)]
        dma_sems = {}
        for i, k in enumerate(self.dma_count):
            dma_sems[k] = self.stack.enter_context(nc.semaphore(f"d_{i}"))
        self.n_sems = sum(len(v) for v in eng_sems.values()) + len(dma_sems)
        streams = self.streams

        def run(engh, e):
            for o in streams[e]:
                for chan, need in o.waits.items():
                    if isinstance(chan, tuple):
                        engh.wait_ge(dma_sems[chan[1]], 16 * need)
                    else:
                        p = streams[chan][need]
                        s = p.signo - 1
                        engh.wait_ge(eng_sems[chan][s // SEM_L], s % SEM_L + 1)
                if o.emit is None:
                    continue
                ins = o.emit(engh)
                if o.is_dma:
                    ins.then_inc(dma_sems[o.dma_key], 16)
                elif o.sig:
                    s = o.signo - 1
                    ins.then_inc(eng_sems[e][s // SEM_L], 1)

        with nc.allow_non_contiguous_dma(reason="small strided state/param transfers"), nc.Block() as block:
            @block.tensor
            def _(eng):
                run(eng, "pe")

            @block.scalar
            def _(eng):
                run(eng, "act")

            @block.vector
            def _(eng):
                run(eng, "dve")

            @block.gpsimd
            def _(eng):
                run(eng, "pool")

            @block.sync
            def _(eng):
                run(eng, "sp")
        self.stack.close()


D = 1024
DIN = 2576
DFF = 4096
NL = 2
SEQ = 2048
NTOK = 2176
GROUPS = [(0, 768, 0), (768, 768, 0), (1536, 512, 128)]
GMAX = 768
EPS = 1e-6
OFF_Q, OFF_K, OFF_V, OFF_G, OFF_LR, OFF_XR, OFF_GR = 0, 256, 512, 1024, 1536, 1552, 2064
PC = 65
C_GPRE, C_GPOSTM, C_GPREF, C_GPOSTF, C_GN, C_CW, C_CB, C_BA, C_BX, C_LAM = 0, 8, 16, 24, 32, 33, 49, 53, 57, 61
K_ID, K_TRI2, K_TRI4, K_M2, K_M4, K_SEQ = 0, 128, 256, 384, 512, 640
CW = 644
NSCR = 6
NSCB = 4
WSLOTS = 3


def build_program(layers=NL, ngroups=len(GROUPS), stop=10 ** 9):
    nc = bass.Bass("TRN2", target_bir_lowering=False)
    P = Prog(nc)

    def din(name, shape):
        return nc.dram_tensor(name, list(shape), F32, kind="ExternalInput").ap()

    def dout(name, shape):
        return nc.dram_tensor(name, list(shape), F32, kind="ExternalOutput").ap()

    x_d = din("x", [NTOK, D])
    sgla_d = din("sgla", [NL, 4, 4, 64, 128])
    srg_d = din("srg", [NL, 4, 512])
    scv_d = din("scv", [NL, 4, 3, 512])
    w_in_d = din("w_in", [NL, D, DIN])
    w_out_d = din("w_out", [NL, D, D])
    w_ff1_d = din("w_ff1", [NL, D, DFF])
    w_ff2_d = din("w_ff2", [NL, DFF, D])
    pvec_d = din("pvec", [128, NL, PC])
    wlr_d = din("wlr", [17, NL, 256])
    wbd_d = din("wbd", [128, NL * 2 * 4, 128])
    consts_d = din("consts", [128, CW])

    y_d = dout("y", [NTOK, D])
    glap_d = dout("glap", [NL, 4, 64, 128])
    rgp_d = dout("rgp", [NL, 512])
    cvp_d = dout("cvp", [NL, 3, 512])
    glas_d = dout("glas", [NL, 4, 4, 64, 128])
    rgs_d = dout("rgs", [NL, 4, 512])
    cvs_d = dout("cvs", [NL, 4, 3, 512])

    xT = P.sbuf("xT", [128, 8, GMAX], F32)
    hT = P.sbuf("hT", [128, 8, GMAX], BF16)
    om = P.sbuf("om", [128, 8, GMAX], BF16)
    resbuf = P.sbuf("resbuf", [128, 8, GMAX], F32)
    wring = [P.sbuf(f"wring{i}", [128, 4096], BF16) for i in range(WSLOTS)]
    arena = P.sbuf("arena", [128, 24576], BF16)
    hid = arena[:, :].rearrange("p (k g) -> p k g", k=32)
    _ao = [0]

    def carve(nbf16):
        a = _ao[0]
        _ao[0] += nbf16
        assert _ao[0] <= 24576
        return arena[:, a:a + nbf16]

    eb = carve(2 * GMAX * 2).bitcast(F32).rearrange("p (a g) -> p a g", a=2)
    enb = carve(2 * GMAX * 2).bitcast(F32).rearrange("p (a g) -> p a g", a=2)
    qk = carve(4 * GMAX).rearrange("p (a g) -> p a g", a=4)
    v_tm = carve(6 * 512).rearrange("p (t e) -> p t e", t=6)
    sg = carve(4 * GMAX).rearrange("p (a g) -> p a g", a=4)
    ggr = carve(4 * GMAX).rearrange("p (a g) -> p a g", a=4)
    k_tm = carve(6 * 256).rearrange("p (t e) -> p t e", t=6)
    k_tm_s = carve(4 * 256).rearrange("p (t e) -> p t e", t=4)
    NSB = 7
    Sz = carve(NSB * 512).rearrange("p (t h e) -> p t h e", t=NSB, h=4)

    att = P.sbuf("att", [128, 3, 512], BF16)
    xr = P.sbuf("xr", [128, 4, GMAX + 3], F32)
    xr_s = P.sbuf("xr_s", [128, 4, 4, 35], F32)
    NDS = 3
    dSp = P.sbuf("dSp", [128, NDS, 2, 128], F32)
    scr = [P.sbuf(f"scr{i}", [128, 512], F32) for i in range(NSCR)]
    scb = [P.sbuf(f"scb{i}", [128, 512], BF16) for i in range(NSCB)]
    xcbw = [P.sbuf(f"xcbw{i}", [128, GMAX], BF16) for i in range(2)]
    lrT = P.sbuf("lrT", [32, GMAX], F32)
    consts = P.sbuf("consts", [128, CW], F32)
    ident_bf = P.sbuf("ident_bf", [128, 128], BF16)
    ones_bf = P.sbuf("ones_bf", [128, 128], BF16)
    pvec = P.sbuf("pvec", [128, NL, PC], F32)
    dv = P.sbuf("dv", [128, NL, 24], F32)
    kcol = P.sbuf("kcol", [128, 4], F32)
    wlr = P.sbuf("wlr", [17, NL, 256], F32)
    wbd = P.sbuf("wbd", [128, NL * 2 * 4, 128], BF16)
    eb_last = P.sbuf("eb_last", [128, 2, 16], F32)
    Sf = P.sbuf("Sf", [128, 2, 2, 128], F32)
    S_state = P.sbuf("S_state", [128, NL, 2, 128], F32)
    S0 = P.sbuf("S0", [128, 4, 2, 128], F32)
    hst = P.sbuf("hst", [128, NL, 4], F32)
    h0s = P.sbuf("h0s", [128, NL, 4, 4], F32)
    hso = P.sbuf("hso", [128, 4, 4], F32)
    convst = P.sbuf("convst", [128, NL, 4, 3], F32)

    mmb = [P.psum(f"mmb{i}", [128, 512], F32) for i in range(3)]
    ssb2 = P.psum("ssb2", [128, 1024], F32)
    ssb = [ssb2[:, 0:512], ssb2[:, 512:1024]]
    smb = [P.psum(f"smb{i}", [128, 512], F32) for i in range(3)]
    cnt = {"mm": 0, "sm": 0, "scr": 0, "scb": 0, "w": 0}

    def MM():
        i = cnt["mm"] % 3
        cnt["mm"] += 1
        return mmb[i], ("mmb", i)

    def SM():
        i = cnt["sm"] % 3
        cnt["sm"] += 1
        return smb[i], ("smb", i)

    def SC():
        i = cnt["scr"] % NSCR
        cnt["scr"] += 1
        return scr[i], ("scr", i)

    def SB():
        i = cnt["scb"] % NSCB
        cnt["scb"] += 1
        return scb[i], ("scb", i)

    ARENA = "arena"
    nph = [0]
    bgref = [None]
    bgn = [4]
    pending = []
    pre_ss = {}

    def barrier():
        P.op("sp", lambda e: e.nop(), reads=(), writes=[ARENA])

    ident = consts[:, K_ID:K_ID + 128]
    tri2 = consts[:, K_TRI2:K_TRI2 + 128]
    tri4 = consts[:, K_TRI4:K_TRI4 + 128]
    mask2 = consts[:, K_M2:K_M2 + 128]
    mask4 = consts[:, K_M4:K_M4 + 128]
    seqmask = consts[:, K_SEQ:K_SEQ + 4]
    epsc = kcol[:, 0:1]
    onec = kcol[:, 1:2]
    lnq = kcol[:, 2:3]

    X0KEYS = [("stg", "O", i_) for i_ in range(3)]

    P.dma("sp", lambda e: e.dma_start(out=consts[:, :], in_=consts_d), "c0", writes=["consts"])
    P.dma("sp", lambda e: e.dma_start(out=pvec[:, :, :], in_=pvec_d), "c1", writes=["pvec"])
    P.dma("sp", lambda e: e.dma_start(out=wlr[:, :, :], in_=wlr_d), "c2", writes=["wlr"])
    wbd_dma = P.dma("pool", lambda e: e.dma_start(out=wbd[:, :, :], in_=wbd_d), "c3", writes=["wbd"])
    with nc.allow_non_contiguous_dma(reason="tiny state loads"):
        for l_ in range(NL):
            for s_ in range(4):
                P.dma("sp", lambda e, l_=l_, s_=s_: e.dma_start(out=h0s[:, l_, s_, :],
                                                                in_=srg_d[l_, s_].rearrange("(m p) -> p m", p=128)),
                      ("c4", s_), writes=["h0s"])
    P.op("dve", lambda e: e.memset(kcol[:, 0:1], EPS), writes=["kcol"])
    P.op("dve", lambda e: e.memset(kcol[:, 1:2], 1.0), writes=["kcol"])
    P.op("dve", lambda e: e.memset(kcol[:, 2:3], math.log(0.125)), writes=["kcol"])
    P.op("dve", lambda e: e.memset(kcol[:, 3:4], 0.0), writes=["kcol"])
    P.op("dve", lambda e: e.memset(lrT[:, :], 1.0), writes=["lrT"])
    P.op("dve", lambda e: e.memset(hst[:, :, :], 0.0), writes=[("hst", l_, m_) for l_ in range(NL) for m_ in range(4)])
    P.op("dve", lambda e: e.memset(S_state[:, :, :, :], 0.0), writes=[("S_state", l_) for l_ in range(NL)])
    P.op("dve", lambda e: e.memset(convst[:, :, :, :], 0.0), writes=[("convst", l_) for l_ in range(NL)])
    P.op("dve", lambda e: e.tensor_copy(out=ident_bf[:, :], in_=ident), reads=["consts"], writes=["ident_bf"])
    P.op("dve", lambda e: e.memset(ones_bf[:, :], 1.0), writes=["ones_bf"])
    t0, t0k = SC()
    P.op("dve", lambda e: e.tensor_scalar(out=dv[:, :, 0:8], in0=pvec[:, :, C_BA:C_BA + 8], scalar1=0.5, scalar2=None,
                                          op0=ALU.mult), reads=["pvec"], writes=["dv0"])
    P.op("act", lambda e: e.activation(out=t0[:, 0:NL * 4].rearrange("p (l m) -> p l m", l=NL),
                                       in_=pvec[:, :, C_LAM:C_LAM + 4], func=AF.Exp, scale=-1.0),
         reads=["pvec"], writes=[t0k])
    P.op("act", lambda e: e.activation(out=t0[:, 16:16 + NL * 4], in_=t0[:, 0:NL * 4], func=AF.Ln, bias=onec),
         reads=[t0k, "kcol"], writes=[t0k])
    P.op("dve", lambda e: e.tensor_scalar(out=dv[:, :, 8:12], in0=t0[:, 16:16 + NL * 4].rearrange("p (l m) -> p l m", l=NL),
                                          scalar1=-4.0, scalar2=None, op0=ALU.mult), reads=[t0k], writes=["dv1"])
    P.op("dve", lambda e: e.tensor_scalar(out=dv[:, :, 12:16], in0=t0[:, 16:16 + NL * 4].rearrange("p (l m) -> p l m", l=NL),
                                          scalar1=-8.0, scalar2=None, op0=ALU.mult), reads=[t0k], writes=["dv2"])
    P.op("dve", lambda e: e.tensor_scalar(out=dv[:, :, 16:17], in0=pvec[:, :, C_GN:C_GN + 1], scalar1=0.5, scalar2=None,
                                          op0=ALU.mult), reads=["pvec"], writes=["dv3"])
    DVK = ["dv0", "dv1", "dv2", "dv3", "pvec", "kcol"]

    first_w = [2]

    def wload(src_ap, shape3):
        i = cnt["w"] % WSLOTS
        cnt["w"] += 1
        a, b = shape3
        view = wring[i][:, 0:a * b].rearrange("p (a b) -> p a b", a=a)
        extra = []
        if first_w[0] > 0 and a * b >= 4096:
            first_w[0] -= 1
            extra = list(X0KEYS)
        P.dma("pool", lambda e: e.dma_start(out=view, in_=src_ap), ("w", i), reads=extra, writes=[("w", i)])
        return view, ("w", i)

    def w_cols(wd, l, c0, nc_):
        return wd[l, :, c0:c0 + nc_].rearrange("(k p) n -> p k n", p=128)

    def geom(g):
        p0, npr, ns = GROUPS[g]
        blocks = []
        c = 0
        while c < npr:
            n = min(512, npr - c)
            blocks.append((c, n, "P"))
            c += n
        if ns:
            blocks.append((npr, ns, "S"))
        ntile = (npr + ns) // 128
        return p0, npr, ns, blocks, ntile

    def tiles_of(c0, n):
        return list(range(c0 // 128, (c0 + n) // 128))

    rflat = resbuf[:, :, :].rearrange("p a g -> p (a g)")
    oflat = om[:, :, :].rearrange("p a g -> p (a g)").bitcast(F32)
    hflat = hT[:, :, :].rearrange("p a g -> p (a g)").bitcast(F32)
    OMKEYS = [("om", 0, t_) for t_ in range(6)] + [("om", 4 + q_, ("b", b_)) for q_ in range(4) for b_ in range(2)]
    HKEYS = [("h", b_, k_) for b_ in range(2) for k_ in range(8)]

    RKEYS = [("res", m_, b_) for m_ in range(8) for b_ in range(2)]
    SLOTKEYS = {"O": OMKEYS, "H": HKEYS, "R": RKEYS}

    def stg_guard(kind):
        P.op("sp", lambda e: e.nop(), writes=SLOTKEYS[kind])

    def stg_load(t):
        if t < 3:
            return oflat[:, t * 1024:(t + 1) * 1024], "O", t, ("stgO", t)
        return hflat[:, (t - 3) * 1024:(t - 2) * 1024], "H", t - 3, ("stgH", t - 3)

    def row0(g, t):
        p0, npr, ns, blocks, ntile = geom(g)
        return p0 + t * 128 if t * 128 < npr else SEQ + (t * 128 - npr)

    def prefetch_x(g, tiles):
        ntile = geom(g)[4]
        tl = [t for t in tiles if t < ntile]
        for kind in ("O", "H"):
            if any((t < 3) == (kind == "O") for t in tl):
                stg_guard(kind)
        for t in tl:
            buf, kind, slot, dk = stg_load(t)
            r0 = row0(g, t)
            o_ = P.dma("sp", lambda e, buf=buf, r0=r0: e.dma_start(out=buf, in_=x_d[r0:r0 + 128, :]), dk,
                       reads=SLOTKEYS[kind], writes=[("stg", kind, slot)])
            if g == 0:
                o_.prio = -1
                if t < 3:
                    wbd_dma.deps.append(o_)

    def load_tile(g, t):
        buf, kind, slot, dk = stg_load(t)
        for half in range(2):
            ps, pk = SM()
            for j in range(4):
                f = half * 4 + j
                P.op("pe", lambda e, ps=ps, j=j, f=f: e.transpose(out=ps[:, j * 128:(j + 1) * 128],
                                                              in_=buf[:, f * 128:(f + 1) * 128], identity=ident),
                     reads=[("stg", kind, slot)] + SLOTKEYS[kind] + ["consts"], writes=[pk])
            P.op("act", lambda e, ps=ps, half=half: e.activation(
                out=xT[:, half * 4:half * 4 + 4, t * 128:(t + 1) * 128],
                in_=ps[:, :].rearrange("p (a b) -> p a b", a=4), func=AF.Copy),
                reads=[pk], writes=[("x", t, half * 4 + j) for j in range(4)])

    def store_tile(g, t, si):
        buf = rflat[:, si * 1024:(si + 1) * 1024]
        r0 = row0(g, t)
        for half in range(2):
            ps, pk = SM()
            for j in range(4):
                f = half * 4 + j
                P.op("pe", lambda e, ps=ps, j=j, f=f: e.transpose(out=ps[:, j * 128:(j + 1) * 128],
                                                              in_=xT[:, f, t * 128:(t + 1) * 128], identity=ident),
                     reads=[("x", t, f), "consts"], writes=[pk])
            P.op("act", lambda e, ps=ps, half=half: e.activation(out=buf[:, half * 512:(half + 1) * 512],
                                                                in_=ps[:, :], func=AF.Copy),
                 reads=[pk] + RKEYS, writes=[("stgR", si, half)])
        P.dma("sp", lambda e: e.dma_start(out=y_d[r0:r0 + 128, :], in_=buf), ("stgR", si),
              reads=[("stgR", si, 0), ("stgR", si, 1)] + RKEYS)

    def load_store(gs, gl):
        nts = geom(gs)[4] if gs is not None else 0
        ntl = geom(gl)[4] if gl is not None else 0
        if gs is None:
            prefetch_x(gl, range(6))
        if nts:
            stg_guard("R")
        for t in range(max(nts, ntl)):
            if t < nts:
                store_tile(gs, t, t)
            if t < ntl:
                load_tile(gl, t)

    def rstd_inplace(ss, sk, n, inv_n):
        P.op("act", lambda e: e.activation(out=ss[:, 0:n], in_=ss[:, 0:n], func=AF.Ln, bias=epsc, scale=inv_n),
             reads=[sk, "kcol"], writes=[sk])
        P.op("act", lambda e: e.activation(out=ss[:, 0:n], in_=ss[:, 0:n], func=AF.Exp, scale=-0.5),
             reads=[sk], writes=[sk])

    def norm_to_h(g, l, gcol, only=None):
        p0, npr, ns, blocks, ntile = geom(g)
        for bi, (c0, n, kind) in enumerate(blocks):
            if only is not None and bi != only:
                continue
            xk = lambda k: [("x", t, k) for t in tiles_of(c0, n)]
            if bi in pre_ss:
                ss, sk = pre_ss.pop(bi)
            else:
                ss, sk = ssb[bi % 2], ("ssb", bi % 2)
                for k in range(8):
                    sq, sqk = SB()
                    P.op("act", lambda e, sq=sq, k=k: e.activation(out=sq[:, 0:n], in_=xT[:, k, c0:c0 + n], func=AF.Square),
                         reads=xk(k), writes=[sqk])
                    P.op("pe", lambda e, sq=sq, k=k, ss=ss: e.matmul(ss[:, 0:n], lhsT=ones_bf[:, :], rhs=sq[:, 0:n],
                                                                   start=(k == 0), stop=(k == 7)),
                         reads=[sqk, "ones_bf"], writes=[sk])
            rstd_inplace(ss, sk, n, 1.0 / D)
            for k in range(8):
                P.op("dve", lambda e, k=k, ss=ss: e.scalar_tensor_tensor(
                    out=hT[:, k, c0:c0 + n], in0=xT[:, k, c0:c0 + n], scalar=pvec[:, l, gcol + k:gcol + k + 1],
                    in1=ss[:, 0:n], op0=ALU.mult, op1=ALU.mult),
                    reads=xk(k) + [sk, "pvec"], writes=[("h", bi, k)])

    def gemm_fm(wv, wk, mcol, kparts, rhs_fn, rhs_keys, block, evac):
        ps, pk = MM()
        c0, n, kind = block
        for k in range(kparts):
            P.op("pe", lambda e, k=k, ps=ps: e.matmul(ps[:, 0:n], lhsT=wv[:, k, mcol:mcol + 128], rhs=rhs_fn(k),
                                                    start=(k == 0), stop=(k == kparts - 1)),
                 reads=[wk] + rhs_keys(k), writes=[pk])
        flush_pending()
        evac(ps, pk)
        bgstep(bgref[0], bgn[0])

    def phase_a(g, l, bg=None):
        p0, npr, ns, blocks, ntile = geom(g)
        nchp = npr // 64

        def hk(bi):
            return [("h", bi), ARENA]

        wv, wk = wload(w_cols(w_in_d, l, OFF_LR, 16), (8, 16))
        for bi, (c0, n, kind) in enumerate(blocks):
            ps, pk = MM()
            for k in range(8):
                P.op("pe", lambda e, k=k, ps=ps: e.matmul(ps[0:16, 0:n], lhsT=wv[:, k, 0:16], rhs=hT[:, k, c0:c0 + n],
                                                        start=(k == 0), stop=(k == 7)),
                     reads=[wk, ("h", bi, k)], writes=[pk])
            P.op("act", lambda e, ps=ps: e.activation(out=lrT[0:16, c0:c0 + n], in_=ps[0:16, 0:n], func=AF.Copy),
                 reads=[pk], writes=[("lrT", bi)])
        def la_gen():
          for bi, (c0, n, kind) in enumerate(blocks):
            tl = tiles_of(c0, n)
            pb = [SM(), SM()]
            tri = tri2 if kind == "P" else tri4
            for ti, t in enumerate(tl):
                ps, pk = MM()
                P.op("pe", lambda e, ps=ps, t=t: e.matmul(ps[:, 0:256], lhsT=lrT[0:17, t * 128:(t + 1) * 128],
                                                        rhs=wlr[0:17, l, :], start=True, stop=True),
                     reads=[("lrT", bi), "lrT", "wlr"], writes=[pk])
                e1, e1k = SC()
                P.op("act", lambda e, ps=ps, e1=e1: e.activation(out=e1[:, 0:256], in_=ps[:, 0:256], func=AF.Exp, scale=-1.0),
                     reads=[pk], writes=[e1k])
                e2, e2k = SC()
                P.op("act", lambda e, e1=e1, e2=e2: e.activation(out=e2[:, 0:256], in_=e1[:, 0:256], func=AF.Ln, bias=onec),
                     reads=[e1k, "kcol"], writes=[e2k])
                for pair in range(2):
                    P.op("pe", lambda e, e2=e2, pair=pair, ti=ti, tri=tri: e.matmul(
                        pb[pair][0][:, ti * 128:(ti + 1) * 128], lhsT=e2[:, pair * 128:(pair + 1) * 128], rhs=tri,
                        start=True, stop=True),
                        reads=[e2k, "consts"], writes=[pb[pair][1]])
                yield
            cs = 64 if kind == "P" else 32
            ch0 = c0 // 64 if kind == "P" else nchp
            nch = n // cs
            for pair in range(2):
                pbt, pbk = pb[pair]
                P.op("act", lambda e, pbt=pbt, pair=pair: e.activation(out=eb[:, pair, c0:c0 + n], in_=pbt[:, 0:n],
                                                                      func=AF.Exp, bias=lnq),
                     reads=[pbk, "kcol", ARENA], writes=[("eb", bi)])
                P.op("act", lambda e, pbt=pbt, pair=pair: e.activation(out=enb[:, pair, c0:c0 + n], in_=pbt[:, 0:n],
                                                                      func=AF.Exp, scale=-1.0),
                     reads=[pbk, ARENA], writes=[("enb", bi)])
                P.op("act", lambda e, pbt=pbt, pair=pair: e.activation(out=eb_last[:, pair, ch0:ch0 + nch],
                                                                      in_=pbt[:, cs - 1:n:cs], func=AF.Exp),
                     reads=[pbk], writes=[("ebl", bi)])
            yield

        lag = la_gen()
        bgref[0] = lag
        bgn[0] = 1
        wv, wk = wload(w_cols(w_in_d, l, OFF_XR, 512), (8, 512))
        for m in range(4):
            for bi, blk in enumerate(blocks):
                c0, n, kind = blk

                def ev(ps, pk, m=m, c0=c0, n=n, bi=bi, kind=kind):
                    if kind == "P":
                        P.op("dve", lambda e: e.tensor_copy(out=xr[:, m, 3 + c0:3 + c0 + n], in_=ps[:, 0:n]),
                             reads=[pk], writes=[("xr", m, bi)])
                    else:
                        P.op("act", lambda e: e.activation(out=xr_s[:, m, :, 3:35],
                                                           in_=ps[:, 0:128].rearrange("p (s t) -> p s t", s=4),
                                                           func=AF.Copy),
                             reads=[pk], writes=[("xr", m, bi)])
                gemm_fm(wv, wk, m * 128, 8, lambda k, c0=c0, n=n: hT[:, k, c0:c0 + n], lambda k, bi=bi: [("h", bi, k)], blk, ev)
        wv, wk = wload(w_cols(w_in_d, l, OFF_GR, 512), (8, 512))
        for m in range(4):
            for bi, blk in enumerate(blocks):
                c0, n, kind = blk

                def ev(ps, pk, m=m, c0=c0, n=n, bi=bi):
                    P.op("act", lambda e: e.activation(out=ggr[:, m, c0:c0 + n], in_=ps[:, 0:n], func=AF.Gelu_apprx_tanh),
                         reads=[pk, ARENA], writes=[("ggr", m, bi)])
                gemm_fm(wv, wk, m * 128, 8, lambda k, c0=c0, n=n: hT[:, k, c0:c0 + n], lambda k, bi=bi: [("h", bi, k)], blk, ev)
        for _ in lag:
            pass
        bgref[0] = bg
        bgn[0] = 4
        wv, wk = wload(w_cols(w_in_d, l, OFF_V, 512), (8, 512))
        for t in range(ntile):
            bi = [i for i, b in enumerate(blocks) if t in tiles_of(b[0], b[1])][0]
            ps, pk = MM()
            for k in range(8):
                P.op("pe", lambda e, k=k, ps=ps, t=t: e.matmul(ps[:, 0:512], lhsT=hT[:, k, t * 128:(t + 1) * 128],
                                                             rhs=wv[:, k, :], start=(k == 0), stop=(k == 7)),
                     reads=[wk, ("h", bi, k)], writes=[pk])
            P.op("act", lambda e, ps=ps, t=t: e.activation(out=v_tm[:, t, :], in_=ps[:, 0:512], func=AF.Copy),
                 reads=[pk, ARENA], writes=[("v", t)])
            bgstep(bgref[0], bgn[0])
        wv, wk = wload(w_cols(w_in_d, l, OFF_G, 512), (8, 512))
        for m in range(4):
            for bi, blk in enumerate(blocks):
                c0, n, kind = blk

                def ev(ps, pk, m=m, c0=c0, n=n, bi=bi):
                    P.op("act", lambda e: e.activation(out=sg[:, m, c0:c0 + n], in_=ps[:, 0:n], func=AF.Silu),
                         reads=[pk, ARENA], writes=[("sg", m, bi)])
                gemm_fm(wv, wk, m * 128, 8, lambda k, c0=c0, n=n: hT[:, k, c0:c0 + n], lambda k, bi=bi: [("h", bi, k)], blk, ev)
        wv, wk = wload(w_cols(w_in_d, l, OFF_Q, 512), (8, 512))
        for m in range(4):
            for bi, blk in enumerate(blocks):
                c0, n, kind = blk

                def ev(ps, pk, m=m, c0=c0, n=n, bi=bi):
                    src = eb if m < 2 else enb
                    sk_ = ("eb", bi) if m < 2 else ("enb", bi)
                    P.op("dve", lambda e: e.tensor_tensor(out=qk[:, m, c0:c0 + n], in0=ps[:, 0:n],
                                                          in1=src[:, m % 2, c0:c0 + n], op=ALU.mult),
                         reads=[pk, sk_, ARENA], writes=[("qk", m, bi)])
                gemm_fm(wv, wk, m * 128, 8, lambda k, c0=c0, n=n: hT[:, k, c0:c0 + n], lambda k, bi=bi: [("h", bi, k)], blk, ev)

        bgref[0] = None

    def gla(g, l, bg=None):
        p0, npr, ns, blocks, ntile = geom(g)
        nchp = npr // 64
        last_prompt_group = (p0 + npr == SEQ)

        def bi_of(t):
            return [i for i, b in enumerate(blocks) if t in tiles_of(b[0], b[1])][0]

        qkk = lambda m, t: ("qk", m, bi_of(t))
        sslot = {}
        nsl = [0]

        def new_sslot(ci):
            s = nsl[0] % NSB
            nsl[0] += 1
            sslot[ci] = s
            return s

        def sz_write(s, src, srck):
            for h2 in range(2):
                P.op("pool", lambda e, s=s, h2=h2: e.tensor_copy(out=Sz[h2 * 64:(h2 + 1) * 64, s, h2::2, :],
                                                                in_=src[h2 * 64:(h2 + 1) * 64, :, :]),
                     reads=[srck, ARENA], writes=[("Sz", s, h2)])

        P.op("pool", lambda e: e.memset(Sz[:, :, :, :].rearrange("p t h e -> p (t h e)"), 0.0),
             reads=[ARENA], writes=[("Sz", s_, h2_) for s_ in range(NSB) for h2_ in range(2)])
        if ns:
            with nc.allow_non_contiguous_dma(reason="state load"):
                P.dma("sp", lambda e: e.dma_start(
                    out=S0[:, :, :, :].rearrange("p s a e -> p (s a) e"),
                    in_=sgla_d[l].rearrange("s (a h) d e -> (h d) (s a) e", h=2)), "S0", writes=["S0"])
        tinfo = {}

        tinfo_a = {}

        def pass1a(t):
            sample = (t * 128 >= npr)
            cs = 32 if sample else 64
            ncin = 128 // cs
            ci0 = nchp + 0 if sample else t * 2
            tc0 = t * 128
            bi = bi_of(t)
            ps, pk = SM()
            psb = ps[:, :].bitcast(BF16)
            for pair in range(2):
                P.op("pe", lambda e, psb=psb, pair=pair, tc0=tc0: e.transpose(
                    out=psb[:, pair * 128:(pair + 1) * 128], in_=qk[:, 2 + pair, tc0:tc0 + 128], identity=ident_bf[:, :]),
                    reads=[qkk(2 + pair, t), "ident_bf", ARENA], writes=[pk])
            if not sample:
                P.op("act", lambda e, psb=psb, t=t: e.activation(out=k_tm[:, t, :], in_=psb[:, 0:256], func=AF.Copy),
                     reads=[pk, ARENA], writes=[("k_tm", t)])
            else:
                for s in range(4):
                    P.op("dve", lambda e, psb=psb, s=s: e.tensor_scalar(out=k_tm_s[:, s, :], in0=psb[:, 0:256],
                                                                      scalar1=seqmask[:, s:s + 1], scalar2=None,
                                                                      op0=ALU.mult),
                         reads=[pk, "consts", ARENA], writes=[("k_tm_s", s)])
            tinfo_a[t] = True

        def pass1b(t):
            sample = (t * 128 >= npr)
            cs = 32 if sample else 64
            ncin = 128 // cs
            ci0 = nchp + 0 if sample else t * 2
            tc0 = t * 128
            bi = bi_of(t)
            asl = t % 3
            msk = mask4 if sample else mask2
            for h2 in range(2):
                ps, pk = SM()
                for pair in range(2):
                    P.op("pe", lambda e, ps=ps, pair=pair, h2=h2, tc0=tc0: e.matmul(
                        ps[:, pair * 128:(pair + 1) * 128], lhsT=qk[h2 * 64:(h2 + 1) * 64, 2 + pair, tc0:tc0 + 128],
                        rhs=qk[h2 * 64:(h2 + 1) * 64, pair, tc0:tc0 + 128], start=True, stop=True),
                        reads=[qkk(2 + pair, t), qkk(pair, t), ARENA], writes=[pk])
                P.op("dve", lambda e, ps=ps, asl=asl, msk=msk, h2=h2: e.tensor_tensor(
                    out=att[:, asl, :].rearrange("p (h i) -> p h i", h=4)[:, h2::2, :],
                    in0=ps[:, 0:256].rearrange("p (h i) -> p h i", h=2),
                    in1=msk.unsqueeze(1).to_broadcast([128, 2, 128]), op=ALU.mult),
                    reads=[pk, "consts", ARENA], writes=[("att", asl, h2)])
            for cc in range(ncin):
                ci = ci0 + cc
                ps, pk = SM()
                for pair in range(2):
                    if not sample:
                        r0 = cc * 64
                        P.op("pe", lambda e, ps=ps, pair=pair, r0=r0, t=t: e.matmul(
                            ps[:, pair * 256:(pair + 1) * 256], lhsT=k_tm[r0:r0 + 64, t, pair * 128:(pair + 1) * 128],
                            rhs=v_tm[r0:r0 + 64, t, pair * 256:(pair + 1) * 256], start=True, stop=True),
                            reads=[("k_tm", t), ("v", t), ARENA], writes=[pk])
                    else:
                        P.op("pe", lambda e, ps=ps, pair=pair, cc=cc, t=t: e.matmul(
                            ps[:, pair * 256:(pair + 1) * 256], lhsT=k_tm_s[:, cc, pair * 128:(pair + 1) * 128],
                            rhs=v_tm[:, t, pair * 256:(pair + 1) * 256], start=True, stop=True),
                            reads=[("k_tm_s", cc), ("v", t), ARENA], writes=[pk])
                dsl = ci % NDS
                for h2 in range(2):
                    P.op("dve", lambda e, ps=ps, h2=h2, dsl=dsl, ci=ci: e.tensor_tensor(
                        out=dSp[h2 * 64:(h2 + 1) * 64, dsl, :, :],
                        in0=ps[:, :].rearrange("p (a b e) -> p a b e", a=2, b=2)[h2 * 64:(h2 + 1) * 64, :, h2, :],
                        in1=eb_last[h2 * 64:(h2 + 1) * 64, :, ci:ci + 1].to_broadcast([64, 2, 128]), op=ALU.mult),
                        reads=[pk, ("ebl", bi)], writes=[("dSp", dsl, h2)])
                if not sample:
                    if ci == 0:
                        s = new_sslot(0)
                        sz_write(s, S_state[:, l, :, :], ("S_state", l))
                    src = S_state[:, l, :, :] if ci == 0 else Sf[:, (ci - 1) % 2, :, :]
                    srck = ("S_state", l) if ci == 0 else ("Sf", (ci - 1) % 2)
                    lastc = (ci == nchp - 1)
                    dst = S_state[:, l, :, :] if lastc else Sf[:, ci % 2, :, :]
                    dstk = ("S_state", l) if lastc else ("Sf", ci % 2)
                    if lastc and ci == 0:
                        raise AssertionError
                    for pair in range(2):
                        P.op("dve", lambda e, src=src, dst=dst, pair=pair, ci=ci, dsl=dsl: e.scalar_tensor_tensor(
                            out=dst[:, pair, :], in0=src[:, pair, :], scalar=eb_last[:, pair, ci:ci + 1],
                            in1=dSp[:, dsl, pair, :], op0=ALU.mult, op1=ALU.add),
                            reads=[srck, ("ebl", bi), ("dSp", dsl, 0), ("dSp", dsl, 1)], writes=[dstk])
                    if not lastc:
                        s = new_sslot(ci + 1)
                        sz_write(s, dst, dstk)
                    elif last_prompt_group:
                        with nc.allow_non_contiguous_dma(reason="state store"):
                            P.dma("sp", lambda e: e.dma_start(
                                out=glap_d[l].rearrange("(a h) d e -> (h d) a e", h=2), in_=S_state[:, l, :, :]),
                                ("glap", l), reads=[("S_state", l)])
                else:
                    s = new_sslot(ci)
                    sz_write(s, S0[:, cc, :, :], "S0")
                    for pair in range(2):
                        P.op("dve", lambda e, pair=pair, ci=ci, cc=cc, dsl=dsl: e.scalar_tensor_tensor(
                            out=Sf[:, cc % 2, pair, :], in0=S0[:, cc, pair, :], scalar=eb_last[:, pair, ci:ci + 1],
                            in1=dSp[:, dsl, pair, :], op0=ALU.mult, op1=ALU.add),
                            reads=["S0", ("ebl", bi), ("dSp", dsl, 0), ("dSp", dsl, 1)], writes=[("Sf", cc % 2)])
                    with nc.allow_non_contiguous_dma(reason="state store"):
                        P.dma("sp", lambda e, cc=cc: e.dma_start(
                            out=glas_d[l, cc].rearrange("(a h) d e -> (h d) a e", h=2), in_=Sf[:, cc % 2, :, :]),
                            ("glas", cc % 2), reads=[("Sf", cc % 2)])
            tinfo[t] = (sample, cs, ncin, ci0, tc0, bi, asl)

        def pass3(t):
            sample, cs, ncin, ci0, tc0, bi, asl = tinfo[t]
            po, pok = MM()
            for h in range(4):
                pair, h2 = h // 2, h % 2
                P.op("pe", lambda e, po=po, h=h, t=t, asl=asl: e.matmul(
                    po[:, h * 128:(h + 1) * 128], lhsT=v_tm[:, t, h * 128:(h + 1) * 128],
                    rhs=att[:, asl, h * 128:(h + 1) * 128], start=True, stop=False),
                    reads=[("v", t), ("att", asl, h2), ARENA], writes=[pok])
                for cc in range(ncin):
                    ci = ci0 + cc
                    s = sslot[ci]
                    P.op("pe", lambda e, po=po, h=h, pair=pair, h2=h2, cc=cc, s=s, tc0=tc0, cs=cs: e.matmul(
                        po[:, h * 128 + cc * cs:h * 128 + (cc + 1) * cs],
                        lhsT=Sz[:, s, h, :],
                        rhs=qk[:, pair, tc0 + cc * cs:tc0 + (cc + 1) * cs],
                        start=False, stop=(cc == ncin - 1)),
                        reads=[("Sz", s, 0), ("Sz", s, 1), qkk(pair, t), ARENA], writes=[pok])
            sq, sqk = SB()
            P.op("act", lambda e, po=po, sq=sq: e.activation(out=sq[:, :], in_=po[:, :], func=AF.Square),
                 reads=[pok], writes=[sqk])
            ss, sk = MM()
            P.op("pe", lambda e, ss=ss, sq=sq: e.matmul(ss[:, :], lhsT=ones_bf[:, :], rhs=sq[:, :], start=True, stop=True),
                 reads=[sqk, "ones_bf"], writes=[sk])
            rstd_inplace(ss, sk, 512, 1.0 / 128)
            t1, t1k = SC()
            P.op("dve", lambda e, po=po, t1=t1, tc0=tc0: e.tensor_tensor(
                out=t1[:, :].rearrange("p (h i) -> p h i", h=4), in0=po[:, :].rearrange("p (h i) -> p h i", h=4),
                in1=sg[:, :, tc0:tc0 + 128], op=ALU.mult),
                reads=[pok, sqk, ARENA] + [("sg", m, bi) for m in range(4)], writes=[t1k])
            P.op("dve", lambda e, t1=t1, ss=ss, tc0=tc0: e.scalar_tensor_tensor(
                out=om[:, 0:4, tc0:tc0 + 128], in0=t1[:, :].rearrange("p (h i) -> p h i", h=4),
                scalar=pvec[:, l, C_GN:C_GN + 1], in1=ss[:, :].rearrange("p (h i) -> p h i", h=4),
                op0=ALU.mult, op1=ALU.mult),
                reads=[t1k, sk, "pvec"], writes=[("om", 0, t)])

        LAG = 2
        done3 = -1
        done1b = -1

        def flush3(upto):
            nonlocal done3
            while done3 < upto:
                done3 += 1
                pass3(done3)
                bgstep(bg, 6)

        for step in range(ntile + 1):
            if step < ntile:
                pass1a(step)
            t = step - 1
            if t >= 0:
                if t * 128 >= npr:
                    flush3(t - 1)
                pass1b(t)
                bgstep(bg, 6)
                flush3(t - LAG)
        flush3(ntile - 1)

    def bgstep(bg, n):
        if bg is None:
            return
        for _ in range(n):
            try:
                next(bg)
            except StopIteration:
                return

    def rglru(g, l):
        p0, npr, ns, blocks, ntile = geom(g)
        last_prompt_group = (p0 + npr == SEQ)
        P.op("dve", lambda e: e.tensor_copy(out=xr[:, :, 0:3], in_=convst[:, l, :, :]),
             reads=[("convst", l)], writes=["xrh"])
        if ns:
            with nc.allow_non_contiguous_dma(reason="conv state load"):
                for s in range(4):
                    for m_ in range(4):
                        P.dma("sp", lambda e, s=s, m_=m_: e.dma_start(
                            out=xr_s[:, m_, s, 0:3], in_=scv_d[l, s][:, m_ * 128:(m_ + 1) * 128].rearrange("j p -> p j")),
                            ("scv", m_), writes=[("xrsh", s)])
        def rg_iter(m, seg, slot):
            c0, n, kind, bis = seg
            cw = lambda j: pvec[:, l, C_CW + m * 4 + j:C_CW + m * 4 + j + 1]
            xk = [("xr", m, b_) for b_ in bis] + (["xrh"] if kind == "P" else [("xrsh", s_) for s_ in range(4)])
            ggk = [("ggr", m, b_) for b_ in bis]
            omk = [("om", 4 + m, ("b", b_)) for b_ in bis]

            def X(j):
                if kind == "P":
                    return xr[:, m, c0 + j:c0 + j + n]
                return xr_s[:, m, :, j:j + 32]

            def V(buf):
                if kind == "P":
                    return buf[:, 0:n]
                return buf[:, 0:128].rearrange("p (s t) -> p s t", s=4)
            bufs = [(resbuf[:, 4 * slot + q, :], [("res", 4 * slot + q, 0), ("res", 4 * slot + q, 1)]) for q in range(4)]
            (A, Ak), (B, Bk), (C, Ck), (Dd, Dk) = bufs
            xcb, xcbk = xcbw[slot], [("rg", "xcb", slot)]
            pg, pgk = ssb2, [("ssb", 0), ("ssb", 1)]
            pieces = [(0, min(n, 512))] + ([(512, n - 512)] if n > 512 else [])
            P.op("act", lambda e: e.activation(out=V(A), in_=X(3), func=AF.Identity,
                                               bias=pvec[:, l, C_CB + m:C_CB + m + 1], scale=cw(3)),
                 reads=xk + ["pvec"], writes=Ak)
            yield
            P.op("dve", lambda e: e.scalar_tensor_tensor(out=V(B), in0=X(2), scalar=cw(2), in1=V(A),
                                                         op0=ALU.mult, op1=ALU.add),
                 reads=xk + Ak + ["pvec"], writes=Bk)
            yield
            P.op("dve", lambda e: e.scalar_tensor_tensor(out=V(A), in0=X(1), scalar=cw(1), in1=V(B),
                                                         op0=ALU.mult, op1=ALU.add),
                 reads=xk + Bk + ["pvec"], writes=Ak)
            yield
            P.op("dve", lambda e: e.scalar_tensor_tensor(out=V(B), in0=X(0), scalar=cw(0), in1=V(A),
                                                         op0=ALU.mult, op1=ALU.add),
                 reads=xk + Ak + ["pvec"], writes=Bk)
            yield
            P.op("pool", lambda e: e.tensor_copy(out=xcb[:, 0:n], in_=B[:, 0:n]),
                 reads=Bk, writes=xcbk)
            yield
            for gi, (dst, dstk, bcol) in enumerate(((A, Ak, 0), (C, Ck, 4))):
                for (pc, pw) in pieces:
                    P.op("pe", lambda e, pc=pc, pw=pw, gi=gi: e.matmul(
                        pg[:, pc:pc + pw], lhsT=wbd[:, (l * 2 + gi) * 4 + m, :], rhs=xcb[:, pc:pc + pw],
                        start=True, stop=True), reads=xcbk + ["wbd"], writes=pgk)
                P.op("act", lambda e, dst=dst, bcol=bcol: e.activation(out=dst[:, 0:n], in_=pg[:, 0:n], func=AF.Tanh,
                                                                      scale=0.5, bias=dv[:, l, bcol + m:bcol + m + 1]),
                     reads=pgk + ["dv0"], writes=dstk)
                yield
            P.op("act", lambda e: e.activation(out=Dd[:, 0:n], in_=A[:, 0:n], func=AF.Exp,
                                               scale=dv[:, l, 8 + m:9 + m], bias=dv[:, l, 8 + m:9 + m]),
                 reads=Ak + ["dv1"], writes=Dk)
            yield
            P.op("act", lambda e: e.activation(out=A[:, 0:n], in_=A[:, 0:n], func=AF.Exp,
                                               scale=dv[:, l, 12 + m:13 + m], bias=dv[:, l, 12 + m:13 + m]),
                 reads=Ak + ["dv2"], writes=Ak)
            yield
            P.op("dve", lambda e: e.tensor_scalar(out=A[:, 0:n], in0=A[:, 0:n], scalar1=0.9999999, scalar2=None,
                                                  op0=ALU.min), reads=Ak, writes=Ak)
            yield
            P.op("act", lambda e: e.activation(out=A[:, 0:n], in_=A[:, 0:n], func=AF.Ln, scale=-1.0, bias=onec),
                 reads=Ak + ["kcol"], writes=Ak)
            yield
            P.op("act", lambda e: e.activation(out=A[:, 0:n], in_=A[:, 0:n], func=AF.Exp, scale=0.5),
                 reads=Ak, writes=Ak)
            yield
            P.op("dve", lambda e: e.scalar_tensor_tensor(out=C[:, 0:n], in0=C[:, 0:n], scalar=1.0,
                                                         in1=B[:, 0:n], op0=ALU.add, op1=ALU.mult),
                 reads=Ck + Bk, writes=Ck)
            yield
            P.op("dve", lambda e: e.scalar_tensor_tensor(out=C[:, 0:n], in0=C[:, 0:n], scalar=0.5,
                                                         in1=A[:, 0:n], op0=ALU.mult, op1=ALU.mult),
                 reads=Ck + Ak, writes=Ck)
            yield
            if kind == "P":
                P.op("dve", lambda e: e.tensor_tensor_scan(
                    out=B[:, 0:n], data0=Dd[:, 0:n], data1=C[:, 0:n], initial=hst[:, l, m:m + 1],
                    op0=ALU.mult, op1=ALU.add),
                    reads=Dk + Ck + [("hst", l, m)], writes=Bk)
                yield
                P.op("dve", lambda e: e.tensor_copy(out=hst[:, l, m:m + 1], in_=B[:, n - 1:n]),
                     reads=Bk, writes=[("hst", l, m)])
                yield
            else:
                for s in range(4):
                    P.op("dve", lambda e, s=s: e.tensor_tensor_scan(
                        out=B[:, s * 32:(s + 1) * 32], data0=Dd[:, s * 32:(s + 1) * 32],
                        data1=C[:, s * 32:(s + 1) * 32], initial=h0s[:, l, s, m:m + 1],
                        op0=ALU.mult, op1=ALU.add),
                        reads=Dk + Ck + ["h0s"], writes=Bk)
                    yield
                P.op("act", lambda e: e.activation(out=hso[:, :, m:m + 1],
                                                   in_=B[:, 31:128:32].unsqueeze(2), func=AF.Copy),
                     reads=Bk, writes=[("hso", m)])
                yield
            P.op("dve", lambda e: e.scalar_tensor_tensor(out=om[:, 4 + m, c0:c0 + n], in0=B[:, 0:n], scalar=1.0,
                                                         in1=ggr[:, m, c0:c0 + n], op0=ALU.mult, op1=ALU.mult),
                 reads=Bk + ggk + [ARENA], writes=omk)
            yield

        segs = [(0, npr, "P", [b_ for b_, bl in enumerate(blocks) if bl[2] == "P"])]
        if ns:
            segs.append((npr, ns, "S", [len(blocks) - 1]))
        for seg in segs:
            for m0 in (0, 2):
                gens = [rg_iter(m0, seg, 0), rg_iter(m0 + 1, seg, 1)]
                live = [True, True]
                while any(live):
                    for qi in range(2):
                        if live[qi]:
                            try:
                                next(gens[qi])
                            except StopIteration:
                                live[qi] = False
                    yield
        P.op("dve", lambda e: e.tensor_copy(out=convst[:, l, :, :], in_=xr[:, :, npr:npr + 3]),
             reads=[("xr", m, bi) for m in range(4) for bi in range(len(blocks))] + ["xrh"], writes=[("convst", l)])
        with nc.allow_non_contiguous_dma(reason="small state stores"):
            if last_prompt_group:
                for m_ in range(4):
                    P.dma("sp", lambda e, m_=m_: e.dma_start(
                        out=cvp_d[l][:, m_ * 128:(m_ + 1) * 128].rearrange("j p -> p j"), in_=convst[:, l, m_, :]),
                        ("cvp", m_), reads=[("convst", l)])
                P.dma("sp", lambda e: e.dma_start(out=rgp_d[l].rearrange("(m p) -> p m", p=128), in_=hst[:, l, :]),
                      ("rgp", l), reads=[("hst", l, m) for m in range(4)])
            if ns:
                bi_s = len(blocks) - 1
                for s in range(4):
                    for m_ in range(4):
                        P.dma("sp", lambda e, s=s, m_=m_: e.dma_start(
                            out=cvs_d[l, s][:, m_ * 128:(m_ + 1) * 128].rearrange("j p -> p j"),
                            in_=xr_s[:, m_, s, 32:35]), ("cvs", m_), reads=[("xr", m_, bi_s)])
                    P.dma("sp", lambda e, s=s: e.dma_start(out=rgs_d[l, s].rearrange("(m p) -> p m", p=128),
                                                           in_=hso[:, s, :]),
                          ("rgs", s), reads=[("hso", m) for m in range(4)])

    def res_evac(ps, pk, m, bi, c0, n):
        sq, sqk = SB()
        P.op("act", lambda e: e.activation(out=sq[:, 0:n], in_=ps[:, 0:n], func=AF.Square), reads=[pk], writes=[sqk])
        P.op("dve", lambda e: e.tensor_copy(out=resbuf[:, m, c0:c0 + n], in_=ps[:, 0:n]),
             reads=[pk, sqk], writes=[("res", m, bi)])
        ss, sk = ssb[bi % 2], ("ssb", bi % 2)
        pending.append(lambda: P.op(
            "pe", lambda e: e.matmul(ss[:, 0:n], lhsT=ones_bf[:, :], rhs=sq[:, 0:n], start=(m == 0), stop=(m == 7)),
            reads=[sqk, "ones_bf"], writes=[sk]))

    def flush_pending():
        while pending:
            pending.pop(0)()

    def res_apply(g, l, gcol, want_next=True, only=None):
        flush_pending()
        p0, npr, ns, blocks, ntile = geom(g)
        for bi, (c0, n, kind) in enumerate(blocks):
            if only is not None and bi != only:
                continue
            ss, sk = ssb[bi % 2], ("ssb", bi % 2)
            xk = lambda m: [("x", t, m) for t in tiles_of(c0, n)]
            rstd_inplace(ss, sk, n, 1.0 / D)
            if want_next:
                ns_, nsk = SM()
                pre_ss[bi] = (ns_, nsk)
            for m in range(8):
                tt, ttk = SC()
                P.op("dve", lambda e, m=m, tt=tt, ss=ss: e.scalar_tensor_tensor(
                    out=tt[:, 0:n], in0=resbuf[:, m, c0:c0 + n], scalar=pvec[:, l, gcol + m:gcol + m + 1],
                    in1=ss[:, 0:n], op0=ALU.mult, op1=ALU.mult),
                    reads=[("res", m, bi), sk, "pvec"], writes=[ttk])
                P.op("pool", lambda e, m=m, tt=tt: e.tensor_tensor(out=xT[:, m, c0:c0 + n], in0=xT[:, m, c0:c0 + n],
                                                                  in1=tt[:, 0:n], op=ALU.add),
                     reads=[ttk] + xk(m), writes=xk(m))
                if want_next:
                    sq, sqk = SB()
                    P.op("act", lambda e, m=m, sq=sq: e.activation(out=sq[:, 0:n], in_=xT[:, m, c0:c0 + n], func=AF.Square),
                         reads=xk(m), writes=[sqk])
                    P.op("pe", lambda e, m=m, sq=sq, ns_=ns_: e.matmul(ns_[:, 0:n], lhsT=ones_bf[:, :], rhs=sq[:, 0:n],
                                                                    start=(m == 0), stop=(m == 7)),
                         reads=[sqk, "ones_bf"], writes=[nsk])

    def phase_wout(g, l):
        p0, npr, ns, blocks, ntile = geom(g)
        wvs = [wload(w_cols(w_out_d, l, ch * 512, 512), (8, 512)) for ch in range(2)]
        for bi, blk in enumerate(blocks):
            c0, n, kind = blk
            omk = [("om", 0, t) for t in tiles_of(c0, n)] + [("om", 4 + q, ("b", bi)) for q in range(4)]
            for m in range(8):
                wv, wk = wvs[m // 4]
                gemm_fm(wv, wk, (m % 4) * 128, 8, lambda k, c0=c0, n=n: om[:, k, c0:c0 + n], lambda k, omk=omk: omk, blk,
                        lambda ps, pk, m=m, bi=bi, c0=c0, n=n: res_evac(ps, pk, m, bi, c0, n))
        if l == layers - 1 and g + 1 < ngroups and nph[0] < stop:
            prefetch_x(g + 1, range(0, 3))
        flush_pending()

    def ff1_group(g, l, wv, wk, mm, m, bi, blk):
        c0, n, kind = blk

        def ev(ps, pk):
            rb, rbk = SB()
            P.op("act", lambda e: e.activation(out=rb[:, 0:n], in_=ps[:, 0:n], func=AF.Relu),
                 reads=[pk], writes=[rbk])
            P.op("dve", lambda e: e.tensor_tensor(out=hid[:, m, c0:c0 + n], in0=ps[:, 0:n], in1=rb[:, 0:n],
                                                  op=ALU.mult),
                 reads=[pk, rbk, ARENA], writes=[("hid", m, bi)])
        gemm_fm(wv, wk, mm * 128, 8, lambda k: hT[:, k, c0:c0 + n], lambda k: [("h", bi, k)], blk, ev)

    def phase_ffn(g, l):
        p0, npr, ns, blocks, ntile = geom(g)
        barrier()
        wv, wk = wload(w_cols(w_ff1_d, l, 0, 512), (8, 512))
        for bi, blk in enumerate(blocks):
            res_apply(g, l, C_GPOSTM, only=bi)
            norm_to_h(g, l, C_GPREF, only=bi)
            for mm in range(4):
                ff1_group(g, l, wv, wk, mm, mm, bi, blk)
        for ch in range(1, 8):
            wv, wk = wload(w_cols(w_ff1_d, l, ch * 512, 512), (8, 512))
            for mm in range(4):
                for bi, blk in enumerate(blocks):
                    ff1_group(g, l, wv, wk, mm, ch * 4 + mm, bi, blk)
        if l == layers - 1 and g + 1 < ngroups and nph[0] < stop:
            prefetch_x(g + 1, range(3, 6))
        for m in range(8):
            src = w_ff2_d[l, :, m * 128:(m + 1) * 128].rearrange("(k p) n -> p k n", p=128)
            wv, wk = wload(src, (32, 128))
            for bi, blk in enumerate(blocks):
                c0, n, kind = blk
                gemm_fm(wv, wk, 0, 32, lambda k, c0=c0, n=n: hid[:, k, c0:c0 + n],
                        lambda k, bi=bi: [("hid", k, bi), ARENA], blk,
                        lambda ps, pk, m=m, bi=bi, c0=c0, n=n: res_evac(ps, pk, m, bi, c0, n))
        for bi in range(len(blocks)):
            res_apply(g, l, C_GPOSTF, want_next=(l + 1 < layers), only=bi)
            if l + 1 < layers:
                norm_to_h(g, l + 1, C_GPRE, only=bi)
        barrier()

    def mixer(g, l):
        bg = rglru(g, l)
        phase_a(g, l, bg)
        gla(g, l, bg)
        for _ in bg:
            pass

    def go(fn, *a):
        if nph[0] < stop:
            fn(*a)
        nph[0] += 1

    go(load_store, None, 0)
    for g in range(ngroups):
        for l in range(layers):
            if l == 0:
                go(norm_to_h, g, l, C_GPRE)
            go(mixer, g, l)
            go(phase_wout, g, l)
            go(phase_ffn, g, l)
        go(load_store, g, g + 1 if g + 1 < ngroups else None)
    P.finish()
    P.emit_all()
    return nc, P


def _consts():
    c = np.zeros((128, CW), np.float32)
    j = np.arange(128)[:, None]
    i = np.arange(128)[None, :]
    c[:, K_ID:K_ID + 128] = (j == i)
    m2 = ((j // 64) == (i // 64)) & (j <= i)
    m4 = ((j // 32) == (i // 32)) & (j <= i)
    c[:, K_TRI2:K_TRI2 + 128] = m2 * (-1.0 / 16.0)
    c[:, K_TRI4:K_TRI4 + 128] = m4 * (-1.0 / 16.0)
    c[:, K_M2:K_M2 + 128] = m2
    c[:, K_M4:K_M4 + 128] = m4
    for s in range(4):
        c[s * 32:(s + 1) * 32, K_SEQ + s] = 1.0
    return c


_CACHE = {}


def kernel(x_prompt, x_sample, state_gla, state_rglru, state_conv, g_pre_mix, w_in, w_lr2, b_lr, gla_norm,
           conv_w, conv_b, rg_wa, rg_ba, rg_wx, rg_bx, rg_lambda, w_out, g_post_mix, g_pre_ff, w_ff1, w_ff2,
           g_post_ff):
    f = lambda a: np.ascontiguousarray(np.asarray(a, dtype=np.float32))
    x_prompt, x_sample = f(x_prompt), f(x_sample)
    state_gla, state_rglru, state_conv = f(state_gla), f(state_rglru), f(state_conv)
    pvec = np.zeros((128, NL, PC), np.float32)
    for l in range(NL):
        for col, v in ((C_GPRE, g_pre_mix), (C_GPOSTM, g_post_mix), (C_GPREF, g_pre_ff), (C_GPOSTF, g_post_ff)):
            pvec[:, l, col:col + 8] = f(v)[l].reshape(8, 128).T
        pvec[:, l, C_GN] = f(gla_norm)[l]
        pvec[:, l, C_CW:C_CW + 16] = f(conv_w)[l].reshape(4, 4, 128).transpose(2, 1, 0).reshape(128, 16)
        pvec[:, l, C_CB:C_CB + 4] = f(conv_b)[l].reshape(4, 128).T
        pvec[:, l, C_BA:C_BA + 4] = f(rg_ba)[l].reshape(4, 128).T
        pvec[:, l, C_BX:C_BX + 4] = f(rg_bx)[l].reshape(4, 128).T
        pvec[:, l, C_LAM:C_LAM + 4] = f(rg_lambda)[l].reshape(4, 128).T
    wlr = np.zeros((17, NL, 256), np.float32)
    wlr[:16] = f(w_lr2).transpose(1, 0, 2)
    wlr[16] = f(b_lr)
    wbd = np.zeros((128, NL * 2 * 4, 128), np.float32)
    for l in range(NL):
        for gi, w in enumerate((f(rg_wa), f(rg_wx))):
            for m in range(4):
                for hb in range(2):
                    blk = w[l, m * 2 + hb]
                    wbd[hb * 64:(hb + 1) * 64, (l * 2 + gi) * 4 + m, hb * 64:(hb + 1) * 64] = blk
    consts = _consts()
    w_in, w_out, w_ff1, w_ff2 = f(w_in), f(w_out), f(w_ff1), f(w_ff2)

    if "nc" not in _CACHE:
        _CACHE["nc"] = build_program()[0]
    nc = _CACHE["nc"]
    in_maps = []
    for c in range(8):
        xs = x_sample[4 * c:4 * c + 4].reshape(128, D)
        in_maps.append({
            "x": np.ascontiguousarray(np.concatenate([x_prompt[c], xs], axis=0)),
            "sgla": np.ascontiguousarray(state_gla[:, 4 * c:4 * c + 4]),
            "srg": np.ascontiguousarray(state_rglru[:, 4 * c:4 * c + 4]),
            "scv": np.ascontiguousarray(state_conv[:, 4 * c:4 * c + 4]),
            "w_in": w_in, "w_out": w_out, "w_ff1": w_ff1, "w_ff2": w_ff2,
            "pvec": pvec, "wlr": wlr, "wbd": wbd, "consts": consts,
        })
    res = run_bass_kernel_spmd(nc, in_maps, core_ids=list(range(8)))
    R = res.results
    y_prompt = np.stack([R[c]["y"][:SEQ] for c in range(8)]).astype(np.float32)
    y_sample = np.concatenate([R[c]["y"][SEQ:].reshape(4, 32, D) for c in range(8)], axis=0).astype(np.float32)
    gla_prompt = np.stack([R[c]["glap"] for c in range(8)], axis=1).astype(np.float32)
    rglru_prompt = np.stack([R[c]["rgp"] for c in range(8)], axis=1).astype(np.float32)
    conv_prompt = np.stack([R[c]["cvp"] for c in range(8)], axis=1).astype(np.float32)
    gla_sample = np.concatenate([R[c]["glas"] for c in range(8)], axis=1).astype(np.float32)
    rglru_sample = np.concatenate([R[c]["rgs"] for c in range(8)], axis=1).astype(np.float32)
    conv_sample = np.concatenate([R[c]["cvs"] for c in range(8)], axis=1).astype(np.float32)
    return (y_prompt, y_sample, gla_prompt, rglru_prompt, conv_prompt, gla_sample, rglru_sample, conv_sample)
```
